# Optimizing a Trainium2 kernel written in Bass

```python
import jax, jax.numpy as jnp
from jax import lax
import numpy as np

D_MODEL = 2048
BATCH = 2
SEQ = 8192
DEPTH = 1

GRID_W = 64
D_MIX = D_MODEL
NH_A = 4
DH_A = D_MIX // 2 // NH_A
W_A = NH_A * DH_A
NH_B = 8
DH_B = (D_MIX - W_A) // NH_B
W_B = NH_B * DH_B
CHUNK = 64
WIN_ROWS_MAX = 8
WIN_COLS = 16
D_FF = ((8 * D_MODEL // 3 + 255) // 256) * 256
CONV_W = 3
N_MOD = 6
N_GATES = 4 * NH_A
IN_SIZES = (W_A, W_A, W_A, W_A, N_GATES, W_B, W_B, W_B)
IN_COLS = sum(IN_SIZES)
IN_SPLITS = tuple(int(s) for s in np.cumsum(IN_SIZES)[:-1])
EPS = 1e-6

kernel_name = 'hybrid_mlstm_natten_convglu_adaln_block'


def rms_norm(x, g):
    xf = x.astype(jnp.float32)
    y = xf * lax.rsqrt(jnp.mean(xf * xf, axis=-1, keepdims=True) + EPS)
    return (y * g.astype(jnp.float32)).astype(x.dtype)


def modulate(h, shift, scale):
    return h * (1 + scale[:, None, :]) + shift[:, None, :]


def to_heads(t, n_heads):
    b, s, _ = t.shape
    return t.reshape(b, s, n_heads, -1).transpose(0, 2, 1, 3)


def from_heads(t):
    b, h, s, d = t.shape
    return t.transpose(0, 2, 1, 3).reshape(b, s, h * d)


def mlstm_scan(q, k, v, log_i, log_f):
    b_, h_, s_, d_ = q.shape
    nc = s_ // CHUNK

    def chunks(t):
        return jnp.moveaxis(t.reshape(t.shape[:2] + (nc, CHUNK) + t.shape[3:]), 2, 0)

    xs = (chunks(q), chunks(k), chunks(v), chunks(log_i), chunks(log_f))
    lower = jnp.tril(jnp.ones((CHUNK, CHUNK), dtype=bool))

    def step(carry, inp):
        C, n, m = carry
        qc, kc, vc, li, lf = inp
        bcum = jnp.cumsum(lf, axis=-1)
        gsum = bcum[..., -1]
        log_d = jnp.where(lower, bcum[..., :, None] - bcum[..., None, :] + li[..., None, :], -jnp.inf)
        m_inter = bcum + m[..., None]
        m_t = jnp.maximum(m_inter, jnp.max(log_d, axis=-1))
        s = jnp.einsum('bhtd,bhsd->bhts', qc, kc) * jnp.exp(log_d - m_t[..., None])
        a = jnp.exp(m_inter - m_t)
        num = jnp.einsum('bhts,bhse->bhte', s, vc) + a[..., None] * jnp.einsum('bhtd,bhde->bhte', qc, C)
        den = jnp.sum(s, axis=-1) + a * jnp.einsum('bhtd,bhd->bht', qc, n)
        h = num / jnp.maximum(jnp.abs(den), jnp.exp(-m_t))[..., None]
        log_w = gsum[..., None] - bcum + li
        m_new = jnp.maximum(gsum + m, jnp.max(log_w, axis=-1))
        decay = jnp.exp(gsum + m - m_new)
        w = jnp.exp(log_w - m_new[..., None])
        C = decay[..., None, None] * C + jnp.einsum('bhs,bhsd,bhse->bhde', w, kc, vc)
        n = decay[..., None] * n + jnp.einsum('bhs,bhsd->bhd', w, kc)
        return (C, n, m_new), h

    init = (jnp.zeros((b_, h_, d_, d_), jnp.float32),
            jnp.zeros((b_, h_, d_), jnp.float32),
            jnp.full((b_, h_), -1e30, jnp.float32))
    _, hs = lax.scan(step, init, xs)
    return jnp.moveaxis(hs, 0, 2).reshape(b_, h_, s_, d_)


def mlstm_bidirectional(q, k, v, gates):
    i_f, f_f, i_b, f_b = jnp.split(gates, 4, axis=1)
    h_fwd = mlstm_scan(q, k, v, i_f, jax.nn.log_sigmoid(f_f))
    flip = lambda t: jnp.flip(t, axis=2)
    h_bwd = flip(mlstm_scan(flip(q), flip(k), flip(v), flip(i_b), flip(jax.nn.log_sigmoid(f_b))))
    return h_fwd + h_bwd


def neighborhood_attention(q, k, v, rpb):
    b_, h_, s_, d_ = q.shape
    rows = s_ // GRID_W
    kr = min(WIN_ROWS_MAX, rows)
    kc = WIN_COLS
    grid = lambda t: t.reshape(b_, h_, rows, GRID_W, d_)
    q, k, v = grid(q), grid(k), grid(v)
    row_start = jnp.clip(jnp.arange(rows) - kr // 2, 0, rows - kr)
    cols = jnp.arange(GRID_W)
    col_idx = jnp.clip(cols - kc // 2, 0, GRID_W - kc)[:, None] + jnp.arange(kc)
    dc = col_idx - cols[:, None] + (WIN_COLS - 1)
    scale = d_ ** -0.5

    def row_block(r):
        rs = row_start[r]
        qr = lax.dynamic_index_in_dim(q, r, axis=2, keepdims=False)
        kb = lax.dynamic_slice_in_dim(k, rs, kr, axis=2)[:, :, :, col_idx]
        vb = lax.dynamic_slice_in_dim(v, rs, kr, axis=2)[:, :, :, col_idx]
        dr = rs + jnp.arange(kr) - r + (WIN_ROWS_MAX - 1)
        bias = rpb[:, dr[None, :, None], dc[:, None, :]]
        s = jnp.einsum('bhcd,bhrcjd->bhcrj', qr, kb).astype(jnp.float32) * scale + bias.astype(jnp.float32)
        p = jax.nn.softmax(s.reshape(b_, h_, GRID_W, kr * kc), axis=-1).reshape(s.shape)
        return jnp.einsum('bhcrj,bhrcjd->bhcd', p.astype(v.dtype), vb)

    out = lax.map(row_block, jnp.arange(rows))
    return jnp.moveaxis(out, 0, 2).reshape(b_, h_, s_, d_)


def conv_glu(h, w_up, conv_w, conv_b, w_down):
    u, g = jnp.split(h @ w_up, 2, axis=-1)
    s_ = g.shape[1]
    pad = CONV_W // 2
    gp = jnp.pad(g, ((0, 0), (pad, pad), (0, 0)))
    g = sum(gp[:, j:j + s_] * conv_w[j] for j in range(CONV_W)) + conv_b
    return (jax.nn.gelu(g, approximate=False) * u) @ w_down


def setup_inputs(seed: int = 0) -> dict:
    key = jax.random.key(seed)
    ks = jax.random.split(key, 20)
    nrm = jax.random.normal
    f32 = jnp.float32
    x = nrm(ks[0], (BATCH, SEQ, D_MODEL), f32)
    c = nrm(ks[1], (BATCH, D_MODEL), f32)
    w_ada = nrm(ks[2], (DEPTH, D_MODEL, N_MOD * D_MODEL), f32) * (0.5 * D_MODEL ** -0.5)
    b_ada = nrm(ks[3], (DEPTH, N_MOD * D_MODEL), f32) * 0.02
    g_norm1 = 1.0 + 0.02 * nrm(ks[4], (DEPTH, D_MODEL), f32)
    w_in = nrm(ks[5], (DEPTH, D_MODEL, IN_COLS), f32) * D_MODEL ** -0.5
    gk = jax.random.split(ks[6], 4)
    b_gates = jnp.concatenate([
        0.1 * nrm(gk[0], (DEPTH, NH_A), f32),
        3.0 + 0.5 * nrm(gk[1], (DEPTH, NH_A), f32),
        0.1 * nrm(gk[2], (DEPTH, NH_A), f32),
        3.0 + 0.5 * nrm(gk[3], (DEPTH, NH_A), f32),
    ], axis=-1)
    g_head_a = 1.0 + 0.02 * nrm(ks[7], (DEPTH, W_A), f32)
    rpb = 0.1 * nrm(ks[8], (DEPTH, NH_B, 2 * WIN_ROWS_MAX - 1, 2 * WIN_COLS - 1), f32)
    w_out = nrm(ks[9], (DEPTH, D_MIX, D_MODEL), f32) * D_MIX ** -0.5
    g_norm2 = 1.0 + 0.02 * nrm(ks[10], (DEPTH, D_MODEL), f32)
    w_up = nrm(ks[11], (DEPTH, D_MODEL, 2 * D_FF), f32) * D_MODEL ** -0.5
    conv_w = nrm(ks[12], (DEPTH, CONV_W, D_FF), f32) * CONV_W ** -0.5
    conv_b = 0.02 * nrm(ks[13], (DEPTH, D_FF), f32)
    w_down = nrm(ks[14], (DEPTH, D_FF, D_MODEL), f32) * D_FF ** -0.5
    g_final = 1.0 + 0.02 * nrm(ks[15], (D_MODEL,), f32)
    return {'x': x, 'c': c, 'w_ada': w_ada, 'b_ada': b_ada, 'g_norm1': g_norm1, 'w_in': w_in,
            'b_gates': b_gates, 'g_head_a': g_head_a, 'rpb': rpb, 'w_out': w_out, 'g_norm2': g_norm2,
            'w_up': w_up, 'conv_w': conv_w, 'conv_b': conv_b, 'w_down': w_down, 'g_final': g_final}


def reference(x, c, w_ada, b_ada, g_norm1, w_in, b_gates, g_head_a, rpb, w_out, g_norm2,
              w_up, conv_w, conv_b, w_down, g_final):
    f32 = jnp.float32
    for l in range(DEPTH):
        mod = jax.nn.silu(c) @ w_ada[l] + b_ada[l]
        sh1, sc1, ga1, sh2, sc2, ga2 = jnp.split(mod, N_MOD, axis=-1)

        h = modulate(rms_norm(x, g_norm1[l]), sh1, sc1)
        qa, ka, va, oa, gates, qb, kb, vb = jnp.split(h @ w_in[l], IN_SPLITS, axis=-1)

        q_a = to_heads(qa, NH_A).astype(f32)
        k_a = to_heads(ka, NH_A).astype(f32) * (DH_A ** -0.5)
        v_a = to_heads(va, NH_A).astype(f32)
        g_a = (gates.astype(f32) + b_gates[l].astype(f32)).transpose(0, 2, 1)
        h_a = mlstm_bidirectional(q_a, k_a, v_a, g_a)
        h_a = h_a * lax.rsqrt(jnp.mean(h_a * h_a, axis=-1, keepdims=True) + EPS)
        h_a = (from_heads(h_a) * g_head_a[l] * jax.nn.sigmoid(oa.astype(f32))).astype(x.dtype)

        h_b = from_heads(neighborhood_attention(to_heads(qb, NH_B), to_heads(kb, NH_B),
                                                to_heads(vb, NH_B), rpb[l])).astype(x.dtype)

        mixed = jnp.concatenate([h_a, h_b], axis=-1) @ w_out[l]
        x = x + ga1[:, None, :] * mixed

        h = modulate(rms_norm(x, g_norm2[l]), sh2, sc2)
        x = x + ga2[:, None, :] * conv_glu(h, w_up[l], conv_w[l], conv_b[l], w_down[l])
    return rms_norm(x, g_final)
```

```python
import contextlib
import numpy as np
import ml_dtypes
import concourse.bass as bass
import concourse.mybir as mybir
from concourse.bass_utils import run_bass_kernel_spmd

F32 = mybir.dt.float32
BF16 = mybir.dt.bfloat16
AF = mybir.ActivationFunctionType
ALU = mybir.AluOpType
AX = mybir.AxisListType
EPS = 1e-6
ENGS = ("pe", "act", "dve", "pool", "sp")


class _Op:
    __slots__ = ("eng", "fn", "deps", "ms", "val", "dsem", "idx")

    def __init__(self, eng, fn, dsem, idx):
        self.eng, self.fn, self.dsem, self.idx = eng, fn, dsem, idx
        self.deps, self.ms, self.val = [], False, None


class Prog:
    stopped = False
    pool = {}

    def __init__(self, nc, tag):
        self.nc, self.tag = nc, tag
        self.ops, self.lastw, self.readers, self.dsems = [], {}, {}, {}

    def op(self, eng, fn, reads=(), writes=(), dsem=None):
        if Prog.stopped:
            return None
        o = _Op(eng, fn, dsem, len(self.ops))
        deps = {}
        for r in reads:
            w = self.lastw.get(r)
            if w is not None:
                deps[w.idx] = w
        for r in writes:
            w = self.lastw.get(r)
            if w is not None:
                deps[w.idx] = w
            for rd in self.readers.get(r, ()):
                deps[rd.idx] = rd
        for d in deps.values():
            if d.eng == "pe" and eng == "pe" and d.dsem is None and dsem is None:
                continue
            o.deps.append(d)
            d.ms = True
        for r in writes:
            self.lastw[r] = o
            self.readers[r] = []
        for r in reads:
            if r not in writes:
                self.readers.setdefault(r, []).append(o)
        if dsem is not None:
            self.dsems.setdefault(dsem, 0)
        self.ops.append(o)
        return o

    def emit(self):
        if Prog.stopped:
            return
        nc = self.nc
        G = Prog.pool.setdefault(id(nc), {"es": {}, "ec": {e: 0 for e in ENGS}, "slots": [], "sc": []})
        for e in ENGS:
            if e not in G["es"]:
                G["es"][e] = nc.alloc_semaphore(f"s_{e}")
        slot = {}
        for i, kname in enumerate(self.dsems):
            if i >= len(G["slots"]):
                G["slots"].append(nc.alloc_semaphore(f"d_{i}"))
                G["sc"].append(0)
            slot[kname] = i
        per = {e: [o for o in self.ops if o.eng == e] for e in ENGS}
        for e in ENGS:
            if per[e]:
                per[e][-1].ms = True
        cnt = dict(G["ec"])
        dcnt = {kname: G["sc"][i] for kname, i in slot.items()}
        for o in self.ops:
            if o.dsem is not None:
                dcnt[o.dsem] += 16
                o.val = dcnt[o.dsem]
            elif o.ms:
                cnt[o.eng] += 1
                o.val = cnt[o.eng]
        esem = G["es"]
        dsem = {kname: G["slots"][i] for kname, i in slot.items()}

        def run(e, engobj):
            waited = {}
            for o in per[e]:
                for d in o.deps:
                    if d.dsem is not None:
                        key, sem = ("d", d.dsem), dsem[d.dsem]
                    else:
                        key, sem = ("e", d.eng), esem[d.eng]
                    if waited.get(key, 0) < d.val:
                        engobj.wait_ge(sem, d.val)
                        waited[key] = d.val
                ins = o.fn()
                if o.dsem is not None:
                    ins.then_inc(dsem[o.dsem], 16)
                elif o.ms:
                    ins.then_inc(esem[e], 1)
            for kname, v in dcnt.items():
                if v > G["sc"][slot[kname]] and waited.get(("d", kname), 0) < v:
                    engobj.wait_ge(dsem[kname], v)
            for e2 in ENGS:
                if cnt[e2] > G["ec"][e2] and waited.get(("e", e2), 0) < cnt[e2]:
                    engobj.wait_ge(esem[e2], cnt[e2])

        with nc.Block() as block:
            block.tensor(lambda eng: run("pe", eng))
            block.scalar(lambda eng: run("act", eng))
            block.vector(lambda eng: run("dve", eng))
            block.gpsimd(lambda eng: run("pool", eng))
            block.sync(lambda eng: run("sp", eng))
        for kname, i in slot.items():
            G["sc"][i] = dcnt[kname]
        G["ec"] = cnt
        nc.all_engine_barrier()


class K:
    def __init__(self, nc):
        self.nc = nc
        self.v = {"dve": nc.vector, "pool": nc.gpsimd}
        self.q = {"sp": nc.sync, "pool": nc.gpsimd, "act": nc.scalar}

    def mm(self, out, lhsT, rhs, start=True, stop=True):
        return lambda: self.nc.tensor.matmul(out, lhsT, rhs, start=start, stop=stop)

    def tr(self, out, in_, ident):
        return lambda: self.nc.tensor.transpose(out, in_, ident)

    def act(self, out, in_, func, bias=None, scale=None, accum=None):
        kw = {}
        if bias is not None:
            kw["bias"] = bias
        if scale is not None:
            kw["scale"] = scale
        if accum is not None:
            kw["accum_out"] = accum
        return lambda: self.nc.scalar.activation(out, in_, func, **kw)

    def dma(self, q, out, in_, slow=False):
        if slow:
            return lambda: self.q[q].dma_start(out=out, in_=in_, allow_slow_non_contiguous=True)
        return lambda: self.q[q].dma_start(out=out, in_=in_)

    def ts(self, e, out, in0, s1, s2, op0, op1=None):
        if op1 is None:
            return lambda: self.v[e].tensor_scalar(out, in0, s1, None, op0)
        return lambda: self.v[e].tensor_scalar(out, in0, s1, s2, op0, op1)

    def tt(self, e, out, in0, in1, op):
        return lambda: self.v[e].tensor_tensor(out, in0, in1, op)

    def stt(self, e, out, in0, scalar, in1, op0, op1):
        return lambda: self.v[e].scalar_tensor_tensor(out, in0, scalar, in1, op0, op1)

    def cp(self, e, out, in_):
        return lambda: self.v[e].tensor_copy(out, in_)

    def ms(self, e, ap, c):
        return lambda: self.v[e].memset(ap, c)

    def rcp(self, out, in_):
        return lambda: self.nc.vector.reciprocal(out, in_)

    def rmax(self, out, in_):
        return lambda: self.nc.vector.reduce_max(out, in_, AX.X)


NTOK = 8192
TRAILER = "none"
STOP_AFTER = ""


class _Stop(Exception):
    pass


def _stop(P, tag):
    if STOP_AFTER == tag:
        P.emit()
        Prog.stopped = True
NT = 64
NWIN = 2052
SCALE_B = 128 ** -0.5


def build(mode):
    Prog.stopped = False
    nc = bass.Bass("TRN2", target_bir_lowering=False)
    k = K(nc)
    dt = lambda name, shape, dty, kind: nc.dram_tensor(name, shape, dty, kind=kind).ap()
    IN, OUT, INT = "ExternalInput", "ExternalOutput", "Internal"
    do1 = mode in ("h1", "fused")
    do2 = mode in ("h2", "fused")
    D = {}
    D["ccol"] = dt("ccol", [128, 16], F32, IN)
    D["ident"] = dt("ident", [128, 128], F32, IN)
    if do1:
        D["xb"] = dt("xb", [NTOK, 2048], F32, IN)
        D["wada1"] = dt("wada1", [2048, 4096], F32, IN)
        D["bada1"] = dt("bada1", [1, 4096], F32, IN)
        D["g1col"] = dt("g1col", [128, 16], F32, IN)
        D["win"] = dt("win", [2048, NWIN], F32, IN)
        D["bg"] = dt("bg", [128, 4], F32, IN)
        D["gha"] = dt("gha", [128, 256], F32, IN)
        D["nab"] = dt("nab", [2, 128, 3200], F32, IN)
        D["trif"] = dt("trif", [128, 128], F32, IN)
        D["trib"] = dt("trib", [128, 128], F32, IN)
        D["qAT"] = dt("qAT_d", [NT, 128, 2, 128], BF16, INT)
        D["kAT"] = dt("kAT_d", [NT, 128, 2, 128], BF16, INT)
        D["qBT"] = dt("qBT_d", [2, NT, 128, 128], BF16, INT)
        D["kBT"] = dt("kBT_d", [2, NT, 128, 128], BF16, INT)
        D["vA"] = dt("vA_d", [NT, 128, 257], BF16, INT)
        D["og"] = dt("og_d", [NT, 128, 256], F32, INT)
        D["ktok"] = dt("ktok_d", [NT, 128, 256], BF16, INT)
        D["vB"] = dt("vB_d", [NT, 128, 256], BF16, INT)
        D["hf"] = dt("hf_d", [NT, 128, 256], F32, INT)
        D["mixs"] = dt("mixs", [4, 4, 512, 2050], BF16, OUT if mode == "h1" else INT)
        D["rmask"] = dt("rmask", [128, 4], F32, IN)
    if do2:
        D["mixr"] = dt("mixr", [2048, 2050], BF16, IN if mode == "h2" else INT)
        D["xtok"] = dt("xtok", [2050, 2048], F32, IN)
        D["flags"] = dt("flags", [128, 2], F32, IN)
        D["wada2"] = dt("wada2", [2048, 8192], F32, IN)
        D["bada2"] = dt("bada2", [1, 8192], F32, IN)
        D["wout"] = dt("wout", [2048, 2048], F32, IN)
        D["g2col"] = dt("g2col", [128, 16], F32, IN)
        D["wup"] = dt("wup", [44, 128, 16, 256], F32, IN)
        D["convw"] = dt("convw", [128, 44, 3], F32, IN)
        D["convb"] = dt("convb", [128, 44], F32, IN)
        D["wdn"] = dt("wdn", [4, 2, 128, 22, 512], F32, IN)
        D["gfin"] = dt("gfin", [128, 2048], F32, IN)
        D["h2T"] = dt("h2T_d", [128, 16, 2050], BF16, INT)
        D["x1"] = dt("x1_d", [2048, 2048], F32, INT)
        D["z"] = dt("z_d", [2048, 2048], F32, INT)
        D["out"] = dt("out", [2048, 2048], F32, OUT)

    with contextlib.ExitStack() as top:
        gsb = lambda name, shape, dty: top.enter_context(nc.sbuf_tensor(name, shape, dty))
        ident_f = gsb("ident_f", [128, 128], F32)
        ident_b = gsb("ident_b", [128, 128], BF16)
        ones_f = gsb("ones_f", [128, 128], F32)
        ones_b = gsb("ones_b", [128, 128], BF16)
        gates_sb = gsb("gates_sb", [128, NT, 4], F32)
        ga2_bc = gsb("ga2_bc", [128, 2048], F32)
        ga1_bcP = gsb("ga1_bcP", [128, 2048], F32)
        Gc2P = gsb("Gc2P", [128, 16], F32)
        shc2P = gsb("shc2P", [128, 16], F32)

        def consts(P):
            P.op("sp", k.dma("sp", ident_f[:], D["ident"]), writes=["ident_f"], dsem="c_idf")
            P.op("pool", k.dma("pool", ident_b[:], D["ident"]), writes=["ident_b"], dsem="c_idb")
            P.op("dve", k.ms("dve", ones_f[:], 1.0), writes=["ones_f"])
            P.op("dve", k.ms("dve", ones_b[:], 1.0), writes=["ones_b"])

        def mk_modrows(P, sb, psb, tag, nmax):
            cc = sb("cc" + tag, [128, 16], F32)
            scb = sb("scb" + tag, [128, 16], BF16)
            badar = sb("badar" + tag, [1, nmax], F32)
            wst = [sb(f"wst{tag}{i}", [128, 16, 256], BF16) for i in range(2)]
            P.op("sp", k.dma("sp", cc[:], D["ccol"]), writes=["cc"], dsem="cc")
            P.op("act", k.act(scb[:], cc[:], AF.Silu), reads=["cc"], writes=["scb"])
            state = {"i": 0}

            def run(wada, bada, c0, ncols, modrow):
                P.op("sp", k.dma("sp", badar[0:1, 0:ncols], bada[:, c0:c0 + ncols]), writes=["badar"], dsem="badar")
                for cb in range(ncols // 256):
                    b = state["i"] % 2
                    state["i"] += 1
                    src = wada[:, c0 + cb * 256:c0 + (cb + 1) * 256].rearrange("(k p) n -> p k n", p=128)
                    P.op("pool", k.dma("pool", wst[b][:], src), writes=[f"wst{b}"], dsem=f"wst{b}")
                    for kk in range(16):
                        P.op("pe", k.mm(psb[0:1, 0:256], scb[:, kk:kk + 1], wst[b][:, kk, :], kk == 0, kk == 15),
                             reads=["scb", f"wst{b}"], writes=["pmisc"])
                    P.op("dve", k.tt("dve", modrow[0:1, cb * 256:(cb + 1) * 256], psb[0:1, 0:256],
                                     badar[0:1, cb * 256:(cb + 1) * 256], ALU.add),
                         reads=["pmisc", "badar"], writes=["modrow"])
            return run

        def row2col(P, psb, modrow, off, col0, res):
            for kk in range(16):
                P.op("pe", k.mm(psb[:, col0 + kk:col0 + kk + 1], modrow[0:1, off + kk * 128:off + (kk + 1) * 128],
                                ones_f[0:1, 0:1]), reads=["modrow", "modrow2", "ones_f"], writes=[res])

        def norm_T(P, xt, xs, M, ss, sd, rstd, junk, tpb, Gc, shc, dst_fn, rx, rdst, ev_i):
            P.op("act", k.act(junk[0:M, :], xt[0:M, :], AF.Square, accum=ss[0:M, 0:1]), reads=[rx], writes=["junk", "ss"])
            P.op("act", k.act(sd[0:M, 0:1], ss[0:M, 0:1], AF.Sqrt, bias=EPS, scale=1.0 / 2048), reads=["ss"], writes=["sd"])
            P.op("dve", k.rcp(rstd[0:M, 0:1], sd[0:M, 0:1]), reads=["sd"], writes=["rstd"])
            rxs = rx if xs is xt else "xs_shared"
            P.op("dve", k.ts("dve", xs[0:M, :], xt[0:M, :], rstd[0:M, 0:1], None, ALU.mult), reads=[rx, "rstd"], writes=[rxs])
            for g in range(4):
                bank = tpb[g % 2]
                for j in range(4):
                    kk = g * 4 + j
                    P.op("pe", k.tr(bank[:, j * 128:j * 128 + M], xs[0:M, kk * 128:(kk + 1) * 128], ident_f[0:M, 0:M]),
                         reads=[rxs, "ident_f"], writes=[f"tp{g % 2}"])
                for j in range(4):
                    kk = g * 4 + j
                    src = bank[:, j * 128:j * 128 + M]
                    if g % 2 == 0:
                        P.op("act", k.act(dst_fn(kk), src, AF.Identity, bias=shc[:, kk:kk + 1], scale=Gc[:, kk:kk + 1]),
                             reads=[f"tp{g % 2}", "Gc"], writes=[rdst(kk)])
                    else:
                        P.op("dve", k.ts("dve", dst_fn(kk), src, Gc[:, kk:kk + 1], shc[:, kk:kk + 1], ALU.mult, ALU.add),
                             reads=[f"tp{g % 2}", "Gc"], writes=[rdst(kk)])

        if do1:
            with contextlib.ExitStack() as st:
                sb = lambda name, shape, dty: st.enter_context(nc.sbuf_tensor(name, shape, dty))
                psum = lambda name, dty=F32, n=512: st.enter_context(nc.psum_tensor(name, [128, n], dty))
                P = Prog(nc, "a")
                consts(P)
                tpb = [psum("tp0"), psum("tp1")]
                fmb = [psum("fm0"), psum("fm1")]
                tmb = [psum("tm0"), psum("tm1")]
                pmisc = psum("pmisc")
                Wb = sb("Wb", [128, 16, NWIN], BF16)
                P.op("pool", k.dma("pool", Wb[:], D["win"].rearrange("(k p) n -> p k n", p=128)), writes=["Wb"], dsem="Wb")
                modrow = sb("modrow", [1, 4096], F32)
                mk_modrows(P, sb, pmisc, "1", 4096)(D["wada1"], D["bada1"], 0, 4096, modrow)
                g1c = sb("g1c", [128, 16], F32)
                P.op("sp", k.dma("sp", g1c[:], D["g1col"]), writes=["g1c"], dsem="g1c")
                row2col(P, pmisc, modrow, 2048, 256, "pmisc")
                row2col(P, pmisc, modrow, 0, 272, "pmisc")
                Gc = sb("Gc", [128, 16], F32)
                shc = sb("shc", [128, 16], F32)
                P.op("dve", k.ts("dve", Gc[:], pmisc[:, 256:272], 1.0, None, ALU.add), reads=["pmisc"], writes=["Gc0"])
                P.op("dve", k.tt("dve", Gc[:], Gc[:], g1c[:], ALU.mult), reads=["Gc0", "g1c"], writes=["Gc"])
                P.op("dve", k.cp("dve", shc[:], pmisc[:, 272:288]), reads=["pmisc"], writes=["Gc"])
                bgs = sb("bgs", [128, 4], F32)
                ghas = sb("ghas", [128, 256], F32)
                P.op("sp", k.dma("sp", ghas[:], D["gha"]), writes=["ghas"], dsem="ghas")

                xts = [sb(f"xt{i}", [128, 2048], F32) for i in range(2)]
                hT = [sb(f"hT{i}", [128, 16, 512], BF16) for i in range(2)]
                junk = sb("junk", [128, 2048], BF16)
                ss = sb("ss", [128, 1], F32)
                sd = sb("sd", [128, 1], F32)
                rstd = sb("rstd", [128, 1], F32)
                fmst = [sb(f"fmst{i}", [128, 512], BF16) for i in range(4)]
                vst = [sb(f"vst{i}", [128, 257], BF16) for i in range(2)]
                ogs = [sb(f"ogs{i}", [128, 256], F32) for i in range(2)]
                kts = [sb(f"kts{i}", [128, 256], BF16) for i in range(2)]
                vbs = [sb(f"vbs{i}", [128, 256], BF16) for i in range(2)]
                for i in range(2):
                    P.op("dve", k.ms("dve", vst[i][:, 256:257], 1.0), writes=[f"vst{i}"])

                def load_x(t):
                    P.op("sp", k.dma("sp", xts[t % 2][:], D["xb"][t * 128:(t + 1) * 128, :]), writes=[f"xt{t % 2}"], dsem=f"xt{t % 2}")

                load_x(0)
                fmi = 0
                for tg in range(16):
                    g = tg % 2
                    for ti in range(4):
                        t = tg * 4 + ti
                        if t + 1 < NT:
                            load_x(t + 1)
                        xt = xts[t % 2]
                        norm_T(P, xt, xt, 128, ss, sd, rstd, junk, tpb, Gc, shc,
                               lambda kk, g=g, ti=ti: hT[g][:, kk, ti * 128:(ti + 1) * 128], f"xt{t % 2}", (lambda kk, g=g, ti=ti: f"hT{g}_{ti}_{kk}"), t)
                    for fc in range(8):
                        bank = fmb[fc % 2]
                        for kk in range(16):
                            P.op("pe", k.mm(bank[:, 0:512], Wb[:, kk, fc * 128:(fc + 1) * 128], hT[g][:, kk, :], kk == 0, kk == 15),
                                 reads=["Wb"] + [f"hT{g}_{ti_}_{kk}" for ti_ in range(4)], writes=[f"fm{fc % 2}"])
                        fb = fmi % 4
                        fmi += 1
                        scl = 0.0625 if fc in (2, 3) else 1.0
                        if fc % 2 == 0:
                            P.op("act", k.act(fmst[fb][:], bank[:, 0:512], AF.Copy, scale=scl), reads=[f"fm{fc % 2}"], writes=[f"fmst{fb}"])
                        else:
                            P.op("dve", k.ts("dve", fmst[fb][:], bank[:, 0:512], scl, None, ALU.mult), reads=[f"fm{fc % 2}"], writes=[f"fmst{fb}"])
                        src = fmst[fb][:].rearrange("p (c t) -> p c t", c=4)
                        c0 = tg * 4
                        if fc < 4:
                            arr = D["qAT"] if fc < 2 else D["kAT"]
                            dst = arr[c0:c0 + 4, :, fc % 2, :].rearrange("c p t -> p c t")
                        else:
                            arr = D["qBT"] if fc % 2 == 0 else D["kBT"]
                            dst = arr[(fc - 4) // 2, c0:c0 + 4, :, :].rearrange("c p t -> p c t")
                        P.op("sp", k.dma("sp", dst, src), reads=[f"fmst{fb}"], writes=[f"d_fm{fc}"], dsem=f"fmst{fb}")
                    for ti in range(4):
                        t = tg * 4 + ti
                        b = t % 2
                        lhs = lambda kk: hT[g][:, kk, ti * 128:(ti + 1) * 128]
                        for kk in range(16):
                            P.op("pe", k.mm(tmb[0][:, 0:512], lhs(kk), Wb[:, kk, 1024:1536], kk == 0, kk == 15),
                                 reads=["Wb", f"hT{g}_{ti}_{kk}"], writes=["tm0"])
                        for kk in range(16):
                            P.op("pe", k.mm(tmb[1][:, 0:512], lhs(kk), Wb[:, kk, 1536:2048], kk == 0, kk == 15),
                                 reads=["Wb", f"hT{g}_{ti}_{kk}"], writes=["tm1"])
                        for kk in range(16):
                            P.op("pe", k.mm(pmisc[:, 0:4], lhs(kk), Wb[:, kk, 2048:2052], kk == 0, kk == 15),
                                 reads=["Wb", f"hT{g}_{ti}_{kk}"], writes=["pmisc"])
                        P.op("act", k.act(vst[b][:, 0:256], tmb[0][:, 0:256], AF.Copy), reads=["tm0"], writes=[f"vst{b}"])
                        P.op("act", k.act(ogs[b][:], tmb[0][:, 256:512], AF.Sigmoid), reads=["tm0"], writes=[f"ogs{b}"])
                        P.op("pool", k.tt("pool", ogs[b][:], ogs[b][:], ghas[:], ALU.mult), reads=["ghas"], writes=[f"ogs{b}"])
                        P.op("dve", k.ts("dve", kts[b][:], tmb[1][:, 0:256], 0.0625, None, ALU.mult), reads=["tm1"], writes=[f"kts{b}"])
                        P.op("dve", k.cp("dve", vbs[b][:], tmb[1][:, 256:512]), reads=["tm1"], writes=[f"vbs{b}"])
                        P.op("dve", k.cp("dve", gates_sb[:, t, :], pmisc[:, 0:4]), reads=["pmisc"], writes=["gates"])
                        P.op("sp", k.dma("sp", D["vA"][t], vst[b][:]), reads=[f"vst{b}"], writes=["d_vA"], dsem=f"vst{b}")
                        P.op("sp", k.dma("sp", D["og"][t], ogs[b][:]), reads=[f"ogs{b}"], writes=["d_og"], dsem=f"ogs{b}")
                        P.op("sp", k.dma("sp", D["ktok"][t], kts[b][:]), reads=[f"kts{b}"], writes=["d_ktok"], dsem=f"kts{b}")
                        P.op("sp", k.dma("sp", D["vB"][t], vbs[b][:]), reads=[f"vbs{b}"], writes=["d_vB"], dsem=f"vbs{b}")
                P.emit()

            with contextlib.ExitStack() as st:
              if STOP_AFTER != "a":
                  sb = lambda name, shape, dty: st.enter_context(nc.sbuf_tensor(name, shape, dty))
                  psum = lambda name, dty=F32, n=512: st.enter_context(nc.psum_tensor(name, [128, n], dty))
                  P = Prog(nc, "b")
                  pA = psum("pA")
                  pS = [psum("pS0"), psum("pS1")]
                  pN = [psum("pN0"), psum("pN1")]
                  pK = [psum("pK0"), psum("pK1")]
                  pT = psum("pT", BF16, 1024)
                  if STOP_AFTER == "bY":
                      tmpo = sb("tmpo", [128, 64], F32)
                      P.op("pe", k.mm(pA[:, 0:64], ones_f[:], ident_f[:, 0:64]), reads=["ones_f"], writes=["pA"])
                      P.op("dve", k.cp("dve", tmpo[:], pA[:, 0:64]), reads=["pA"], writes=["tmpo"])
                  _stop(P, "bY")
                  _stop(P, "bX")
                  bgs = sb("bgs2", [128, 4], F32)
                  P.op("sp", k.dma("sp", bgs[:], D["bg"]), writes=["bgs"], dsem="bgs")
                  masks = [sb("maskf", [128, 128], F32), sb("maskb", [128, 128], F32)]
                  P.op("sp", k.dma("sp", masks[0][:], D["trif"]), writes=["mask0"], dsem="mask0")
                  P.op("sp", k.dma("sp", masks[1][:], D["trib"]), writes=["mask1"], dsem="mask1")
                  zeros_b = sb("zeros_b", [128, 16, 1], BF16)
                  rmask = sb("rmask_s", [128, 4], F32)
                  P.op("sp", k.dma("sp", rmask[:], D["rmask"]), writes=["rmask"], dsem="rmask")
                  P.op("dve", k.ms("dve", zeros_b[:], 0.0), writes=["zeros_b"])
                  P.op("sp", k.dma("sp", D["mixs"][0, :, :, 0:1].rearrange("r (k p) o -> p (r k) o", p=128), zeros_b[:], slow=True), reads=["zeros_b"], dsem="zp0")
                  P.op("sp", k.dma("sp", D["mixs"][3, :, :, 2049:2050].rearrange("r (k p) o -> p (r k) o", p=128), zeros_b[:], slow=True), reads=["zeros_b"], dsem="zp1")
                  _stop(P, "b0")
                  wv, flv, decv = [], [], []
                  sm = lambda name: sb(name, [128, 64], F32)
                  for di in range(2):
                      gi, gf = 2 * di, 2 * di + 1
                      e = "dve"
                      zf, li, az, ez, lz, mz, lf, u, bc = [sm(f"{n}{di}") for n in ("zf", "li", "az", "ez", "lz", "mz", "lf", "u", "bc")]
                      w_, fl_, dec_ = sm(f"w{di}"), sm(f"fl{di}"), sm(f"dec{di}")
                      t1, t2 = sm(f"t1{di}"), sm(f"t2{di}")
                      rows = sb(f"rows{di}", [1, 6, 64], F32)
                      umc = sb(f"umc{di}", [64, 1], F32)
                      R = lambda n: f"{n}{di}"
                      P.op(e, k.ts(e, zf[:], gates_sb[:, :, gf], bgs[:, gf:gf + 1], None, ALU.add), reads=["gates", "bgs"], writes=[R("zf")])
                      P.op(e, k.ts(e, li[:], gates_sb[:, :, gi], bgs[:, gi:gi + 1], None, ALU.add), reads=["gates", "bgs"], writes=[R("li")])
                      P.op("act", k.act(az[:], zf[:], AF.Abs), reads=[R("zf")], writes=[R("az")])
                      P.op("act", k.act(ez[:], az[:], AF.Exp, scale=-1.0), reads=[R("az")], writes=[R("ez")])
                      P.op("act", k.act(lz[:], ez[:], AF.Ln, bias=1.0), reads=[R("ez")], writes=[R("lz")])
                      P.op(e, k.ts(e, mz[:], zf[:], 0.0, None, ALU.min), reads=[R("zf")], writes=[R("mz")])
                      P.op(e, k.tt(e, lf[:], mz[:], lz[:], ALU.subtract), reads=[R("mz"), R("lz")], writes=[R("lf")])
                      if di == 0:
                          _stop(P, "b1_1")
                      P.op("pe", k.mm(pA[:, 0:64], masks[di][:], lf[:]), reads=[R("lf"), f"mask{di}"], writes=["pA"])
                      P.op("pe", k.mm(pA[:, 64:128], ones_f[:], lf[:]), reads=[R("lf"), "ones_f"], writes=["pA"])
                      P.op(e, k.cp(e, bc[:], pA[:, 0:64]), reads=["pA"], writes=[R("bc")])
                      P.op(e, k.tt(e, u[:], li[:], bc[:], ALU.subtract), reads=[R("li"), R("bc")], writes=[R("u")])
                      P.op(e, k.cp(e, rows[0:1, 1, :], pA[0:1, 64:128]), reads=["pA"], writes=[R("gsum")])
                      if di == 0:
                          _stop(P, "b1_1a")
                      P.op("pe", k.tr(pA[0:64, 128:256], u[:], ident_f[:]), reads=[R("u"), "ident_f"], writes=["pA"])
                      P.op(e, k.rmax(umc[:], pA[0:64, 128:256]), reads=["pA"], writes=[R("umc")])
                      if di == 0:
                          _stop(P, "b1_1b")
                      P.op("pe", k.mm(pA[0:1, 256:320], umc[:], ident_f[0:64, 0:64]), reads=[R("umc"), "ident_f"], writes=["pA"])
                      P.op(e, k.cp(e, rows[0:1, 0, :], pA[0:1, 256:320]), reads=["pA"], writes=[R("umax")])
                      se = "dve"
                      if di == 0:
                          _stop(P, "b1_2")
                      sc = sb(f"scan{di}", [1, 4, 64], F32)
                      tmp = sb(f"scant{di}", [1, 64], F32)
                      P.op(se, k.cp(se, sc[0:1, 0, :], rows[0:1, 1, :]), reads=[R("gsum")], writes=[R("scan")])
                      P.op(se, k.tt(se, sc[0:1, 1, :], rows[0:1, 0, :], rows[0:1, 1, :], ALU.add), reads=[R("umax"), R("gsum")], writes=[R("scan")])
                      cur, nxt = 0, 2
                      for d_ in (1, 2, 4, 8, 16, 32):
                          n_ = 64 - d_
                          if di == 0:
                              lo, hi = slice(0, n_), slice(d_, 64)
                          else:
                              lo, hi = slice(d_, 64), slice(0, n_)
                          Gc_, Hc_, Gn_, Hn_ = sc[0:1, cur, :], sc[0:1, cur + 1, :], sc[0:1, nxt, :], sc[0:1, nxt + 1, :]
                          P.op(se, k.cp(se, sc[0:1, nxt:nxt + 2, :], sc[0:1, cur:cur + 2, :]), writes=[R("scan")])
                          P.op(se, k.tt(se, tmp[0:1, 0:n_], sc[0:1, cur + 1, lo], sc[0:1, cur, hi], ALU.add), writes=[R("scan")])
                          P.op(se, k.tt(se, sc[0:1, nxt + 1, hi], tmp[0:1, 0:n_], sc[0:1, cur + 1, hi], ALU.max), writes=[R("scan")])
                          P.op(se, k.tt(se, sc[0:1, nxt, hi], sc[0:1, cur, lo], sc[0:1, cur, hi], ALU.add), writes=[R("scan")])
                          cur, nxt = nxt, cur
                      P.op(se, k.ms(se, rows[0:1, 2, :], -1e30), writes=[R("scan")])
                      if di == 0:
                          P.op(se, k.cp(se, rows[0:1, 2, 1:64], sc[0:1, cur + 1, 0:63]), writes=[R("scan")])
                      else:
                          P.op(se, k.cp(se, rows[0:1, 2, 0:63], sc[0:1, cur + 1, 1:64]), writes=[R("scan")])
                      P.op(se, k.tt(se, rows[0:1, 3, :], rows[0:1, 2, :], rows[0:1, 0, :], ALU.max), writes=[R("scan")])
                      if di == 0:
                          _stop(P, "b1_3")
                      P.op(e, k.tt(e, rows[0:1, 4, :], rows[0:1, 2, :], rows[0:1, 3, :], ALU.subtract), reads=[R("scan")], writes=[R("dd")])
                      P.op("act", k.act(rows[0:1, 5, :], rows[0:1, 4, :], AF.Exp), reads=[R("dd")], writes=[R("decr")])
                      P.op("pe", k.mm(pA[:, 320:384], ones_f[0:1, :], rows[0:1, 3, :]), reads=[R("scan"), "ones_f"], writes=["pA"])
                      P.op("pe", k.mm(pA[:, 384:448], ones_f[0:1, :], rows[0:1, 5, :]), reads=[R("decr"), "ones_f"], writes=["pA"])
                      P.op(e, k.cp(e, dec_[:], pA[:, 384:448]), reads=["pA"], writes=[R("dec")])
                      P.op(e, k.cp(e, t2[:], pA[:, 320:384]), reads=["pA"], writes=[R("t2")])
                      P.op(e, k.tt(e, t1[:], u[:], t2[:], ALU.subtract), reads=[R("u"), R("t2")], writes=[R("t1")])
                      P.op("act", k.act(w_[:], t1[:], AF.Exp), reads=[R("t1")], writes=[R("w")])
                      P.op(e, k.tt(e, t2[:], bc[:], t2[:], ALU.add), reads=[R("bc")], writes=[R("t2")])
                      P.op("act", k.act(fl_[:], t2[:], AF.Exp, scale=-1.0), reads=[R("t2")], writes=[R("fl")])
                      wv.append(w_)
                      flv.append(fl_)
                      decv.append(dec_)

                  _stop(P, "b1")
                  mod2_steps = []
                  if mode == "fused":
                      cc2 = sb("cc2b", [128, 16], F32)
                      scb2 = sb("scb2b", [128, 16], BF16)
                      badar2 = sb("badar2b", [1, 256], F32)
                      modrow2 = sb("modrow2b", [1, 2048], F32)
                      g2cb = sb("g2cb", [128, 16], F32)
                      wst2 = [sb(f"wst2b{i}", [128, 16, 256], BF16) for i in range(2)]
                      P.op("sp", k.dma("sp", cc2[:], D["ccol"]), writes=["cc2"], dsem="cc2")
                      P.op("sp", k.dma("sp", g2cb[:], D["g2col"]), writes=["g2cb"], dsem="g2cb")
                      P.op("act", k.act(scb2[:], cc2[:], AF.Silu), reads=["cc2"], writes=["scb2"])

                      def mod2_block(part, cb):
                          def f():
                              c0 = part * 2048
                              P.op("sp", k.dma("sp", badar2[0:1, :], D["bada2"][:, c0 + cb * 256:c0 + (cb + 1) * 256]), writes=["badar2"], dsem="badar2")
                              bi = (part * 8 + cb) % 2
                              src = D["wada2"][:, c0 + cb * 256:c0 + (cb + 1) * 256].rearrange("(k p) n -> p k n", p=128)
                              P.op("pool", k.dma("pool", wst2[bi][:], src), writes=[f"wst2_{bi}"], dsem=f"wst2_{bi}")
                              for kk in range(16):
                                  P.op("pe", k.mm(pA[0:1, 0:256], scb2[:, kk:kk + 1], wst2[bi][:, kk, :], kk == 0, kk == 15),
                                       reads=["scb2", f"wst2_{bi}"], writes=["pA"])
                              P.op("dve", k.tt("dve", modrow2[0:1, cb * 256:(cb + 1) * 256], pA[0:1, 0:256], badar2[0:1, 0:256], ALU.add),
                                   reads=["pA", "badar2"], writes=["modrow2"])
                              if cb == 7:
                                  if part in (0, 3):
                                      dst = ga1_bcP if part == 0 else ga2_bc
                                      for nb in range(4):
                                          P.op("pe", k.mm(pA[:, 0:512], ones_f[0:1, :], modrow2[0:1, nb * 512:(nb + 1) * 512]), reads=["modrow2"], writes=["pA"])
                                          P.op("dve", k.cp("dve", dst[:, nb * 512:(nb + 1) * 512], pA[:, 0:512]), reads=["pA"], writes=[f"modout{part}"])
                                  elif part == 1:
                                      row2col(P, pA, modrow2, 0, 272, "pA")
                                      P.op("dve", k.cp("dve", shc2P[:], pA[:, 272:288]), reads=["pA"], writes=["shc2P"])
                                  else:
                                      row2col(P, pA, modrow2, 0, 256, "pA")
                                      P.op("dve", k.ts("dve", Gc2P[:], pA[:, 256:272], 1.0, None, ALU.add), reads=["pA"], writes=["Gc2P0"])
                                      P.op("dve", k.tt("dve", Gc2P[:], Gc2P[:], g2cb[:], ALU.mult), reads=["Gc2P0", "g2cb"], writes=["Gc2P"])
                          return f
                      mod2_steps = [mod2_block(p_, c_) for p_ in range(4) for c_ in range(8)]

                  NG = 2
                  qT4 = [sb(f"qT4_{i}", [128, 4, 2, 128], BF16) for i in range(NG)]
                  kT4 = [sb(f"kT4_{i}", [128, 4, 2, 128], BF16) for i in range(NG)]
                  ktk4 = [sb(f"ktk4_{i}", [128, 4, 256], BF16) for i in range(NG)]
                  vas4 = [sb(f"vas4_{i}", [128, 4, 257], BF16) for i in range(NG)]
                  hfc4 = [sb(f"hfc4_{i}", [128, 4, 256], F32) for i in range(NG)]
                  ogc4 = [sb(f"ogc4_{i}", [128, 4, 256], F32) for i in range(NG)]
                  num4 = [sb(f"num4_{i}", [128, 4, 257], F32) for i in range(2)]
                  ho4 = sb("ho4", [128, 4, 256], BF16)
                  mst4 = sb("mst4", [128, 4, 2, 128], BF16)
                  rrg = sb("rrg", [128, 4], F32)
                  rr2g = sb("rr2g", [128, 4], F32)
                  ss4 = sb("ss4g", [128, 4], F32)
                  sd4 = sb("sd4g", [128, 4], F32)
                  rms4 = sb("rms4g", [128, 4], F32)
                  junkg = sb("junkg", [128, 256], BF16)
                  SPb = [sb(f"SPb{i}", [128, 128], BF16) for i in range(2)]
                  kw = [sb(f"kw{i}", [128, 256], BF16) for i in range(2)]
                  C = sb("C", [128, 2, 257], F32)
                  Cd = sb("Cd", [128, 2, 257], BF16)
                  MS = [sb(f"MS{i}", [128, 2, 4, 512], BF16) for i in range(2)]
                  NS = [sb(f"NS{i}", [128, 1, 4, 512], BF16) for i in range(2)]
                  evc = [0]

                  def stage_put(bufs, name, nk, f0, cnt, src_fn, tt, rsrc):
                      g, idx = divmod(tt, 4)
                      buf, res = bufs[g % 2], f"{name}{g % 2}"
                      allres = [f"{res}_{kk}_{r_}_{i_}" for kk in range(nk) for r_ in range(4) for i_ in range(4)]
                      for kk in range(nk):
                          for r_ in range(4):
                              dst = buf[:, kk, r_, idx * 128:(idx + 1) * 128]
                              evc[0] += 1
                              wres = [f"{res}_{kk}_{r_}_{idx}"]
                              if evc[0] % 2 == 0:
                                  P.op("act", k.act(dst, src_fn(kk), AF.Copy, scale=rmask[:, r_:r_ + 1]), reads=[rsrc, "rmask"], writes=wres)
                              else:
                                  P.op("dve", k.ts("dve", dst, src_fn(kk), rmask[:, r_:r_ + 1], None, ALU.mult), reads=[rsrc, "rmask"], writes=wres)
                      cnt[g] = cnt.get(g, 0) + 1
                      if cnt[g] == 4:
                          j, g4 = divmod(g, 4)
                          for kk in range(nk):
                              rd = [f"{res}_{kk}_{r_}_{i_}" for r_ in range(4) for i_ in range(4)]
                              dstv = lambda jj, c0, c1, kk=kk: D["mixs"][jj, :, f0 + kk * 128:f0 + (kk + 1) * 128, c0:c1].rearrange("r f t -> f r t")
                              P.op("sp", k.dma("sp", dstv(j, 1 + g4 * 512, 1 + (g4 + 1) * 512), buf[:, kk, :, :]), reads=rd, dsem=res)
                              if g4 == 3 and j < 3:
                                  P.op("sp", k.dma("sp", dstv(j + 1, 0, 1), buf[:, kk, :, 511:512], slow=True), reads=rd, dsem=res)
                              if g4 == 0 and j > 0:
                                  P.op("sp", k.dma("sp", dstv(j - 1, 2049, 2050), buf[:, kk, :, 0:1], slow=True), reads=rd, dsem=res)

                  groups = [(0, list(range(g * 4, g * 4 + 4))) for g in range(16)] + [(1, list(range(g * 4 + 3, g * 4 - 1, -1))) for g in range(15, -1, -1)]

                  def gloads(gi):
                      di, cs = groups[gi]
                      cb_, b = min(cs), gi % NG
                      P.op("sp", k.dma("sp", qT4[b][:], D["qAT"][cb_:cb_ + 4].rearrange("c p k t -> p c k t")), reads=["d_fm0", "d_fm1"], writes=[f"qT4_{b}"], dsem=f"qT4_{b}")
                      P.op("sp", k.dma("sp", kT4[b][:], D["kAT"][cb_:cb_ + 4].rearrange("c p k t -> p c k t")), reads=["d_fm2", "d_fm3"], writes=[f"kT4_{b}"], dsem=f"kT4_{b}")
                      P.op("sp", k.dma("sp", ktk4[b][:], D["ktok"][cb_:cb_ + 4].rearrange("c p e -> p c e")), writes=[f"ktk4_{b}"], dsem=f"ktk4_{b}")
                      P.op("sp", k.dma("sp", vas4[b][:], D["vA"][cb_:cb_ + 4].rearrange("c p e -> p c e")), writes=[f"vas4_{b}"], dsem=f"vas4_{b}")

                  def gloads_h(gi):
                      di, cs = groups[gi]
                      cb_, b = min(cs), gi % NG
                      if di == 1:
                          P.op("sp", k.dma("sp", hfc4[b][:], D["hf"][cb_:cb_ + 4].rearrange("c p e -> p c e")), reads=["d_hf"], writes=[f"hfc4_{b}"], dsem=f"hfc4_{b}")
                          P.op("sp", k.dma("sp", ogc4[b][:], D["og"][cb_:cb_ + 4].rearrange("c p e -> p c e")), writes=[f"ogc4_{b}"], dsem=f"ogc4_{b}")

                  mcnt = {}
                  pend = [None]
                  gloads(0)
                  si = 0
                  for gi, (di, cs) in enumerate(groups):
                      if gi == 16 and pend[0] is not None:
                          pend[0]()
                          pend[0] = None
                      gloads_h(gi)
                      if gi + 1 < len(groups):
                          gloads(gi + 1)
                      if gi < len(mod2_steps):
                          mod2_steps[gi]()
                      cb_, b = min(cs), gi % NG
                      if gi in (0, 16):
                          P.op("dve", k.ms("dve", C[:], 0.0), writes=["C"])
                      for c in cs:
                          ix = c - cb_
                          s2 = si % 2
                          si += 1
                          wc = wv[di][:, c:c + 1]
                          dc = decv[di][:, c:c + 1]
                          rq, rk, rkt, rv = f"qT4_{b}", f"kT4_{b}", f"ktk4_{b}", f"vas4_{b}"
                          for kk in range(2):
                              P.op("pe", k.mm(pS[s2][:, 0:128], kT4[b][:, ix, kk, :], qT4[b][:, ix, kk, :], kk == 0, kk == 1),
                                   reads=[rk, rq], writes=[f"pS{s2}"])
                          P.op("dve", k.stt("dve", SPb[s2][:], pS[s2][:, 0:128], wc, masks[di][:], ALU.mult, ALU.mult),
                               reads=[f"pS{s2}", f"w{di}", f"mask{di}"], writes=[f"SPb{s2}"])
                          P.op("dve", k.ts("dve", C[:], C[:], dc, None, ALU.mult), reads=[f"dec{di}"], writes=["C"])
                          P.op("act", k.act(Cd[:], C[:], AF.Copy), reads=["C"], writes=["Cd"])
                          P.op("pe", k.mm(pN[s2][:, 0:257], SPb[s2][:], vas4[b][:, ix, :], True, False), reads=[f"SPb{s2}", rv], writes=[f"pN{s2}"])
                          for kk in range(2):
                              P.op("pe", k.mm(pN[s2][:, 0:257], qT4[b][:, ix, kk, :], Cd[:, kk, :], False, kk == 1), reads=[rq, "Cd"], writes=[f"pN{s2}"])
                          P.op("act", k.act(kw[s2][:], ktk4[b][:, ix, :], AF.Copy, scale=wc), reads=[rkt, f"w{di}"], writes=[f"kw{s2}"])
                          for kk in range(2):
                              P.op("pe", k.mm(pK[kk][:, 0:257], kw[s2][:, kk * 128:(kk + 1) * 128], vas4[b][:, ix, :]), reads=[f"kw{s2}", rv], writes=[f"pK{kk}"])
                          for kk in range(2):
                              P.op("dve", k.tt("dve", C[:, kk, :], pK[kk][:, 0:257], C[:, kk, :], ALU.add), reads=[f"pK{kk}"], writes=["C"])
                          nb_ = gi % 2
                          P.op("act", k.act(num4[nb_][:, ix, :], pN[s2][:, 0:257], AF.Copy), reads=[f"pN{s2}"], writes=[f"num4_{nb_}_{ix}"])

                      def gepi(di=di, gi=gi, b=b, cb_=cb_):
                          nb_ = gi % 2
                          n4 = num4[nb_]
                          rn = [f"num4_{nb_}_{i}" for i in range(4)]
                          P.op("act", k.act(rrg[:], n4[:, :, 256], AF.Abs), reads=rn, writes=["rrg0"])
                          P.op("dve", k.tt("dve", rrg[:], rrg[:], flv[di][:, cb_:cb_ + 4], ALU.max), reads=["rrg0", f"fl{di}"], writes=["rrg"])
                          P.op("dve", k.rcp(rr2g[:], rrg[:]), reads=["rrg"], writes=["rr2g"])
                          if di == 0:
                              for i in range(4):
                                  P.op("act" if i % 2 == 0 else "dve",
                                       k.act(n4[:, i, 0:256], n4[:, i, 0:256], AF.Copy, scale=rr2g[:, i:i + 1]) if i % 2 == 0 else
                                       k.ts("dve", n4[:, i, 0:256], n4[:, i, 0:256], rr2g[:, i:i + 1], None, ALU.mult),
                                       reads=["rr2g"], writes=[f"num4_{nb_}_{i}"])
                              P.op("sp", k.dma("sp", D["hf"][cb_:cb_ + 4].rearrange("c p e -> p c e"), n4[:, :, 0:256]), reads=rn, writes=["d_hf"], dsem=f"num4_{nb_}")
                          else:
                              for i in range(4):
                                  P.op("dve", k.stt("dve", n4[:, i, 0:256], n4[:, i, 0:256], rr2g[:, i:i + 1], hfc4[b][:, i, :], ALU.mult, ALU.add),
                                       reads=["rr2g", f"hfc4_{b}"], writes=[f"num4_{nb_}_{i}"])
                                  P.op("act", k.act(junkg[:], n4[:, i, 0:256], AF.Square, accum=ss4[:, i:i + 1]), reads=[f"num4_{nb_}_{i}"], writes=["junkg", f"ss4_{i}"])
                              P.op("act", k.act(sd4[:], ss4[:], AF.Sqrt, bias=EPS, scale=1.0 / 256), reads=[f"ss4_{i}" for i in range(4)], writes=["sd4"])
                              P.op("dve", k.rcp(rms4[:], sd4[:]), reads=["sd4"], writes=["rms4"])
                              for i in range(4):
                                  P.op("dve", k.stt("dve", ho4[:, i, :], n4[:, i, 0:256], rms4[:, i:i + 1], ogc4[b][:, i, :], ALU.mult, ALU.mult),
                                       reads=[f"num4_{nb_}_{i}", "rms4", f"ogc4_{b}"], writes=[f"ho4_{i}"])
                              for i in range(4):
                                  for kk in range(2):
                                      P.op("pe", k.tr(pT[:, (i * 2 + kk) * 128:(i * 2 + kk + 1) * 128], ho4[:, i, kk * 128:(kk + 1) * 128], ident_b[:]),
                                           reads=[f"ho4_{i}", "ident_b"], writes=["pT"])
                              P.op("act", k.act(mst4[:].rearrange("p i k t -> p (i k t)"), pT[:, 0:1024], AF.Copy), reads=["pT"], writes=["mst4"])
                              g = cb_ // 4
                              buf, res = MS[g % 2], f"MS{g % 2}"
                              for kk in range(2):
                                  for r_ in range(4):
                                      dst = buf[:, kk, r_, :].rearrange("p (i t) -> p i t", i=4)
                                      src = mst4[:, :, kk, :]
                                      evc[0] += 1
                                      if evc[0] % 2 == 0:
                                          P.op("act", k.act(dst, src, AF.Copy, scale=rmask[:, r_:r_ + 1]), reads=["mst4", "rmask"], writes=[f"{res}_{kk}_{r_}"])
                                      else:
                                          P.op("dve", k.ts("dve", dst, src, rmask[:, r_:r_ + 1], None, ALU.mult), reads=["mst4", "rmask"], writes=[f"{res}_{kk}_{r_}"])
                              j, g4 = divmod(g, 4)
                              for kk in range(2):
                                  rd = [f"{res}_{kk}_{r_}" for r_ in range(4)]
                                  dstv = lambda jj, c0, c1, kk=kk: D["mixs"][jj, :, kk * 128:(kk + 1) * 128, c0:c1].rearrange("r f t -> f r t")
                                  P.op("sp", k.dma("sp", dstv(j, 1 + g4 * 512, 1 + (g4 + 1) * 512), buf[:, kk, :, :]), reads=rd, dsem=res)
                                  if g4 == 3 and j < 3:
                                      P.op("sp", k.dma("sp", dstv(j + 1, 0, 1), buf[:, kk, :, 511:512], slow=True), reads=rd, dsem=res)
                                  if g4 == 0 and j > 0:
                                      P.op("sp", k.dma("sp", dstv(j - 1, 2049, 2050), buf[:, kk, :, 0:1], slow=True), reads=rd, dsem=res)
                      if pend[0] is not None:
                          pend[0]()
                      pend[0] = gepi
                  if pend[0] is not None:
                      pend[0]()
                      pend[0] = None

                  _stop(P, "b2")
                  EB = sb("EB", [128, 3200], F32)
                  qh = sb("qh", [128, 64, 128], BF16)
                  kh = sb("kh", [128, 64, 128], BF16)
                  vh = sb("vh", [128, 64, 128], BF16)
                  Es = [sb(f"Es{i}", [128, 640], F32) for i in range(3)]
                  Pt = [sb(f"Pt{i}", [128, 640], BF16) for i in range(3)]
                  rinv = [sb(f"rinv{i}", [128, 128], F32) for i in range(3)]
                  ob = [sb(f"ob{i}", [128, 128], BF16) for i in range(3)]
                  pSn = [(pS[0], "pS0"), (pS[1], "pS1"), (pN[0], "pN0")]
                  pOn = [(pK[0], "pK0"), (pK[1], "pK1"), (pN[1], "pN1")]

                  def kbv(pr):
                      kb = min(max(pr - 2, 0), 59)
                      v = 0 if pr == 0 else 1 if pr == 1 else 3 if pr == 62 else 4 if pr == 63 else 2
                      return kb, v

                  si = 0
                  npend = [None]
                  for hb in range(2):
                      ncnt = {}
                      P.op("sp", k.dma("sp", EB[:, :], D["nab"][hb]), writes=["EB"], dsem="EB")
                      P.op("act", k.act(EB[:, :], EB[:, :], AF.Exp), writes=["EB"])
                      P.op("sp", k.dma("sp", qh[:], D["qBT"][hb].rearrange("c p t -> p c t")), reads=[f"d_fm{4 + 2 * hb}"], writes=["qh"], dsem="qh")
                      P.op("sp", k.dma("sp", kh[:], D["kBT"][hb].rearrange("c p t -> p c t")), reads=[f"d_fm{5 + 2 * hb}"], writes=["kh"], dsem="kh")
                      P.op("sp", k.dma("sp", vh[:], D["vB"][:, :, hb * 128:(hb + 1) * 128].rearrange("c p d -> p c d")), writes=["vh"], dsem="vh")
                      for pr in range(64):
                          s2 = si % 3
                          si += 1
                          kb, v = kbv(pr)
                          (pSb, rS), (pOb, rO) = pSn[s2], pOn[s2]
                          for kt in range(5):
                              dst = pSb[:, kt * 128:(kt + 1) * 128] if kt < 4 else pOb[:, 256:384]
                              P.op("pe", k.mm(dst, kh[:, kb + kt, :], qh[:, pr, :]), reads=["kh", "qh"], writes=[rS if kt < 4 else rO])
                          P.op("act", k.act(Es[s2][:, 0:512], pSb[:, 0:512], AF.Exp, scale=SCALE_B), reads=[rS], writes=[f"Es{s2}"])
                          P.op("act", k.act(Es[s2][:, 512:640], pOb[:, 256:384], AF.Exp, scale=SCALE_B), reads=[rO], writes=[f"Es{s2}"])
                          P.op("dve", k.tt("dve", Pt[s2][:], Es[s2][:], EB[:, v * 640:(v + 1) * 640], ALU.mult), reads=[f"Es{s2}", "EB"], writes=[f"Pt{s2}"])

                          def back(s2=s2, kb=kb, pr=pr, hb=hb, pOb=pOb, rO=rO, ncnt=ncnt):
                              for kt in range(5):
                                  P.op("pe", k.mm(pOb[:, 0:128], vh[:, kb + kt, :], Pt[s2][:, kt * 128:(kt + 1) * 128], kt == 0, kt == 4),
                                       reads=["vh", f"Pt{s2}"], writes=[rO])
                              for kt in range(5):
                                  P.op("pe", k.mm(pOb[:, 128:256], ones_b[:], Pt[s2][:, kt * 128:(kt + 1) * 128], kt == 0, kt == 4),
                                       reads=["ones_b", f"Pt{s2}"], writes=[rO])
                              P.op("dve", k.rcp(rinv[s2][:], pOb[:, 128:256]), reads=[rO], writes=[f"rinv{s2}"])
                              P.op("dve", k.tt("dve", ob[s2][:], pOb[:, 0:128], rinv[s2][:], ALU.mult), reads=[rO, f"rinv{s2}"], writes=[f"ob{s2}"])
                              stage_put(NS, f"NS", 1, 256 + hb * 128, ncnt, lambda kk, s2=s2: ob[s2][:, :], pr, f"ob{s2}")
                          if npend[0] is not None:
                              npend[0]()
                          npend[0] = back
                      if npend[0] is not None:
                          npend[0]()
                          npend[0] = None
                  P.emit()

        if mode == "fused" and not Prog.stopped:
            ccs = nc.alloc_semaphore("cc_sem")
            with nc.Block() as blk:
                def _cc(g):
                    g.collective_compute("ReduceScatter", ALU.add, replica_groups=[[0, 1, 2, 3], [4, 5, 6, 7]],
                                         ins=[D["mixs"].rearrange("j r f t -> (j r f) t").opt()], outs=[D["mixr"].opt()]).then_inc(ccs, 1)
                    g.wait_ge(ccs, 1)
                blk.gpsimd(_cc)
                blk.sync(lambda e: e.wait_ge(ccs, 1))
                blk.tensor(lambda e: e.wait_ge(ccs, 1))
                blk.vector(lambda e: e.wait_ge(ccs, 1))
                blk.scalar(lambda e: e.wait_ge(ccs, 1))

        if do2:
            with contextlib.ExitStack() as st:
                sb = lambda name, shape, dty: st.enter_context(nc.sbuf_tensor(name, shape, dty))
                psum = lambda name, dty=F32, n=512: st.enter_context(nc.psum_tensor(name, [128, n], dty))
                P = Prog(nc, "c")
                if not do1:
                    consts(P)
                tpb = [psum("tp0c"), psum("tp1c")]
                pw = [psum(f"pw{i}") for i in range(4)]
                pmisc = psum("pmisc2")
                Wo = sb("Wo", [128, 16, 2048], BF16)
                P.op("pool", k.dma("pool", Wo[:], D["wout"].rearrange("(k p) n -> p k n", p=128)), writes=["Wo"], dsem="Wo")
                _stop(P, "cX")
                if do1:
                    ga1_bc, Gc, shc = ga1_bcP, Gc2P, shc2P
                else:
                    modrow = sb("modrow2", [1, 2048], F32)
                    mrun = mk_modrows(P, sb, pmisc, "2", 2048)
                    g2c = sb("g2c", [128, 16], F32)
                    P.op("sp", k.dma("sp", g2c[:], D["g2col"]), writes=["g2c"], dsem="g2c")
                    ga1_bc = sb("ga1_bc", [128, 2048], F32)
                    Gc = sb("Gc2", [128, 16], F32)
                    shc = sb("shc2", [128, 16], F32)

                    def bcast(dst, res):
                        for nb in range(4):
                            P.op("pe", k.mm(pw[nb][:, 0:512], ones_f[0:1, :], modrow[0:1, nb * 512:(nb + 1) * 512]), reads=["modrow", "ones_f"], writes=[f"pw{nb}"])
                            P.op("dve", k.cp("dve", dst[:, nb * 512:(nb + 1) * 512], pw[nb][:, 0:512]), reads=[f"pw{nb}"], writes=[res])

                    mrun(D["wada2"], D["bada2"], 0, 2048, modrow)
                    bcast(ga1_bc, "ga1_bc")
                    _stop(P, "cY")
                    mrun(D["wada2"], D["bada2"], 2048, 2048, modrow)
                    row2col(P, pmisc, modrow, 0, 272, "pmisc")
                    P.op("dve", k.cp("dve", shc[:], pmisc[:, 272:288]), reads=["pmisc"], writes=["shc"])
                    _stop(P, "cZ1")
                    mrun(D["wada2"], D["bada2"], 4096, 2048, modrow)
                    _stop(P, "cZ1b")
                    row2col(P, pmisc, modrow, 0, 256, "pmisc")
                    _stop(P, "cZ1c")
                    P.op("dve", k.ts("dve", Gc[:], pmisc[:, 256:272], 1.0, None, ALU.add), reads=["pmisc"], writes=["Gc0"])
                    _stop(P, "cZ1d")
                    P.op("dve", k.tt("dve", Gc[:], Gc[:], g2c[:], ALU.mult), reads=["Gc0", "g2c", "shc"], writes=["Gc"])
                    _stop(P, "cZ2")
                    mrun(D["wada2"], D["bada2"], 6144, 2048, modrow)
                    bcast(ga2_bc, "ga2_bc")
                _stop(P, "c0")
                mixT = [sb(f"mixT{i}", [128, 16, 128], BF16) for i in range(2)]
                xts = [sb(f"x2t{i}", [128, 2048], F32) for i in range(2)]
                x1s = [sb(f"x1s{i}", [128, 2048], F32) for i in range(2)]
                xs = sb("xs2", [128, 2048], F32)
                junk = sb("junk3", [128, 2048], BF16)
                ss, sd, rstd = sb("ss3", [128, 1], F32), sb("sd3", [128, 1], F32), sb("rstd3", [128, 1], F32)
                h2st = [sb(f"h2st{i}", [128, 16, 128], BF16) for i in range(2)]
                mixr3 = D["mixr"].rearrange("(k p) t -> p k t", p=128)

                def tile_info(t):
                    if t < 16:
                        return 128, slice(1 + t * 128, 1 + (t + 1) * 128), slice(t * 128, (t + 1) * 128)
                    return 2, slice(0, 2050, 2049), slice(2048, 2050)

                def loads2(t):
                    M, cs, rs = tile_info(t)
                    b = t % 2
                    if t < 16:
                        P.op("sp", k.dma("sp", mixT[b][:, :, 0:M], mixr3[:, :, cs]), writes=[f"mixT{b}"], dsem=f"mixT{b}")
                    else:
                        P.op("sp", k.dma("sp", mixT[b][:, :, 0:1], mixr3[:, :, 0:1], slow=True), writes=[f"mixT{b}"], dsem=f"mixT{b}")
                        P.op("sp", k.dma("sp", mixT[b][:, :, 1:2], mixr3[:, :, 2049:2050], slow=True), writes=[f"mixT{b}x"], dsem=f"mixT{b}")
                    P.op("sp", k.dma("sp", xts[b][0:M, :], D["xtok"][rs, :]), writes=[f"x2t{b}"], dsem=f"x2t{b}")

                loads2(0)
                for t in range(17):
                    if t == 1:
                        _stop(P, "c1")
                    if t == 16:
                        _stop(P, "c16")
                    if t + 1 < 17:
                        loads2(t + 1)
                    M, cs, rs = tile_info(t)
                    b = t % 2
                    for nb in range(4):
                        for fc in range(16):
                            P.op("pe", k.mm(pw[nb][0:M, 0:512], mixT[b][:, fc, 0:M], Wo[:, fc, nb * 512:(nb + 1) * 512], fc == 0, fc == 15),
                                 reads=[f"mixT{b}", f"mixT{b}x", "Wo"], writes=[f"pw{nb}"])
                        P.op("dve", k.tt("dve", x1s[b][0:M, nb * 512:(nb + 1) * 512], pw[nb][0:M, 0:512], ga1_bc[0:M, nb * 512:(nb + 1) * 512], ALU.mult),
                             reads=[f"pw{nb}", "ga1_bc"], writes=[f"x1s{b}"])
                    P.op("dve", k.tt("dve", x1s[b][0:M, :], x1s[b][0:M, :], xts[b][0:M, :], ALU.add), reads=[f"x2t{b}"], writes=[f"x1s{b}"])
                    if t < 16:
                        P.op("sp", k.dma("sp", D["x1"][rs, :], x1s[b][0:M, :]), reads=[f"x1s{b}"], writes=["d_x1"], dsem=f"x1s{b}")
                    norm_T(P, x1s[b], xs, M, ss, sd, rstd, junk, tpb, Gc, shc,
                           lambda kk, b=b, M=M: h2st[b][:, kk, 0:M], f"x1s{b}", (lambda kk, b=b: f"h2st{b}_{kk}"), t)
                    if t < 16:
                        P.op("sp", k.dma("sp", D["h2T"][:, :, cs], h2st[b][:, :, 0:M]), reads=[f"h2st{b}_{kk_}" for kk_ in range(16)], writes=["d_h2T"], dsem=f"h2st{b}")
                    else:
                        P.op("sp", k.dma("sp", D["h2T"][:, :, 0:1], h2st[b][:, :, 0:1], slow=True), reads=[f"h2st{b}_{kk_}" for kk_ in range(16)], writes=["d_h2T"], dsem=f"h2st{b}")
                        P.op("sp", k.dma("sp", D["h2T"][:, :, 2049:2050], h2st[b][:, :, 1:2], slow=True), reads=[f"h2st{b}_{kk_}" for kk_ in range(16)], writes=["d_h2Tx"], dsem=f"h2st{b}")
                P.emit()

            _stop(P, "c")
            with contextlib.ExitStack() as st:
                sb = lambda name, shape, dty: st.enter_context(nc.sbuf_tensor(name, shape, dty))
                psum = lambda name, dty=F32, n=512: st.enter_context(nc.psum_tensor(name, [128, n], dty))
                P = Prog(nc, "d")
                pU = [psum("pU0"), psum("pU1")]
                pG = [psum("pG0"), psum("pG1")]
                pX = psum("pX")
                pE = [psum(f"pE{i}") for i in range(3)]
                AT = sb("AT", [128, 44, 1024], BF16)
                arena = sb("arena", [128, 22528], BF16)
                h2blk = arena[:, 0:16 * 1026].rearrange("p (k t) -> p k t", k=16)
                wdh = [arena[:, i * 11264:(i + 1) * 11264].rearrange("p (f c) -> p f c", f=22) for i in range(2)]
                wug = [sb(f"wug{i}", [128, 16, 256], BF16) for i in range(2)]
                gsbs = [sb(f"gsb{i}", [128, 1026], F32) for i in range(2)]
                accs = [sb(f"acc{i}", [128, 1024], F32) for i in range(2)]
                cw = sb("cw", [128, 44, 3], F32)
                cb = sb("cb", [128, 44], F32)
                flg = sb("flg", [128, 2], F32)
                x1p = [sb(f"x1p{i}", [128, 512], F32) for i in range(3)]
                zst = [sb(f"zst{i}", [128, 512], F32) for i in range(3)]
                P.op("sp", k.dma("sp", cw[:], D["convw"]), writes=["cw"], dsem="cw")
                P.op("sp", k.dma("sp", cb[:], D["convb"]), writes=["cw"], dsem="cb")
                P.op("sp", k.dma("sp", flg[:], D["flags"]), writes=["cw"], dsem="flg")
                wi = 0
                for tbk in range(2):
                    P.op("sp", k.dma("sp", h2blk, D["h2T"][:, :, tbk * 1024:tbk * 1024 + 1026]), reads=["d_h2T", "d_h2Tx"], writes=["arena", "wdh0", "wdh1"], dsem="h2blk")
                    for ft in range(44):
                        if ft == 1 and tbk == 0:
                            _stop(P, "d0")
                        b = ft % 2
                        P.op("pool", k.dma("pool", wug[b][:], D["wup"][ft]), writes=[f"wug{b}"], dsem=f"wug{b}")
                        for sbk in range(2):
                            cols = slice(1 + sbk * 512, 1 + (sbk + 1) * 512)
                            for kk in range(16):
                                P.op("pe", k.mm(pG[sbk][:, 0:512], wug[b][:, kk, 128:256], h2blk[:, kk, cols], kk == 0, kk == 15),
                                     reads=[f"wug{b}", "arena"], writes=[f"pG{sbk}"])
                        for kk in range(16):
                            P.op("pe", k.mm(pX[:, 0:2], wug[b][:, kk, 128:256], h2blk[:, kk, 0:1026:1025], kk == 0, kk == 15),
                                 reads=[f"wug{b}", "arena"], writes=["pX"])
                        for sbk in range(2):
                            cols = slice(1 + sbk * 512, 1 + (sbk + 1) * 512)
                            for kk in range(16):
                                P.op("pe", k.mm(pU[sbk][:, 0:512], wug[b][:, kk, 0:128], h2blk[:, kk, cols], kk == 0, kk == 15),
                                     reads=[f"wug{b}", "arena"], writes=[f"pU{sbk}"])
                        gs, ac = gsbs[b], accs[b]
                        P.op("act", k.act(gs[:, 1:513], pG[0][:, 0:512], AF.Copy), reads=["pG0"], writes=[f"gsb{b}"])
                        P.op("act", k.act(gs[:, 513:1025], pG[1][:, 0:512], AF.Copy), reads=["pG1"], writes=[f"gsb{b}"])
                        if tbk == 0:
                            P.op("act", k.act(gs[:, 0:1], pX[:, 0:1], AF.Copy, scale=flg[:, 0:1]), reads=["pX", "cw"], writes=[f"gsb{b}"])
                            P.op("act", k.act(gs[:, 1025:1026], pX[:, 1:2], AF.Copy), reads=["pX"], writes=[f"gsb{b}"])
                        else:
                            P.op("act", k.act(gs[:, 0:1], pX[:, 0:1], AF.Copy), reads=["pX"], writes=[f"gsb{b}"])
                            P.op("act", k.act(gs[:, 1025:1026], pX[:, 1:2], AF.Copy, scale=flg[:, 1:2]), reads=["pX", "cw"], writes=[f"gsb{b}"])
                        P.op("dve", k.ts("dve", ac[:], gs[:, 1:1025], cw[:, ft, 1:2], cb[:, ft:ft + 1], ALU.mult, ALU.add), reads=[f"gsb{b}", "cw"], writes=[f"acc{b}"])
                        P.op("dve", k.stt("dve", ac[:], gs[:, 0:1024], cw[:, ft, 0:1], ac[:], ALU.mult, ALU.add), reads=[f"gsb{b}", "cw"], writes=[f"acc{b}"])
                        P.op("dve", k.stt("dve", ac[:], gs[:, 2:1026], cw[:, ft, 2:3], ac[:], ALU.mult, ALU.add), reads=[f"gsb{b}", "cw"], writes=[f"acc{b}"])
                        P.op("act", k.act(ac[:], ac[:], AF.Gelu), writes=[f"acc{b}"])
                        P.op("dve", k.tt("dve", AT[:, ft, 0:512], pU[0][:, 0:512], ac[:, 0:512], ALU.mult), reads=[f"acc{b}", "pU0"], writes=["AT"])
                        P.op("dve", k.tt("dve", AT[:, ft, 512:1024], pU[1][:, 0:512], ac[:, 512:1024], ALU.mult), reads=[f"acc{b}", "pU1"], writes=["AT"])
                    if tbk == 0:
                        _stop(P, "d1")
                    accs_ps = [(pU[0], "pU0"), (pU[1], "pU1"), (pG[0], "pG0"), (pG[1], "pG1"), (pX, "pX"), (pE[0], "pE0"), (pE[1], "pE1"), (pE[2], "pE2")]
                    for nb in range(4):
                        for hf in range(2):
                            wb = wi % 2
                            wi += 1
                            P.op("pool", k.dma("pool", wdh[wb], D["wdn"][nb, hf]), writes=["arena", f"wdh{wb}"], dsem=f"wdh{wb}")
                            for tt in range(8):
                                for f in range(22):
                                    ft = hf * 22 + f
                                    P.op("pe", k.mm(accs_ps[tt][0][:, 0:512], AT[:, ft, tt * 128:(tt + 1) * 128], wdh[wb][:, f, :], ft == 0, ft == 43),
                                         reads=["AT", f"wdh{wb}"], writes=[accs_ps[tt][1]])
                        for tt in range(8):
                            row0 = tbk * 1024 + tt * 128
                            xb_ = (nb * 8 + tt) % 3
                            zb = (nb * 8 + tt) % 3
                            cs_ = slice(nb * 512, (nb + 1) * 512)
                            P.op("sp", k.dma("sp", x1p[xb_][:], D["x1"][row0:row0 + 128, cs_]), writes=[f"x1p{xb_}"], dsem=f"x1p{xb_}")
                            P.op("dve", k.tt("dve", zst[zb][:], accs_ps[tt][0][:, 0:512], ga2_bc[:, cs_], ALU.mult), reads=[accs_ps[tt][1]], writes=[f"zst{zb}"])
                            P.op("dve", k.tt("dve", zst[zb][:], zst[zb][:], x1p[xb_][:], ALU.add), reads=[f"x1p{xb_}"], writes=[f"zst{zb}"])
                            P.op("sp", k.dma("sp", D["z"][row0:row0 + 128, cs_], zst[zb][:]), reads=[f"zst{zb}"], writes=["d_z"], dsem=f"zst{zb}")
                P.emit()

            _stop(P, "d")
            with contextlib.ExitStack() as st:
                sb = lambda name, shape, dty: st.enter_context(nc.sbuf_tensor(name, shape, dty))
                P = Prog(nc, "e")
                gf = sb("gf", [128, 2048], F32)
                P.op("sp", k.dma("sp", gf[:], D["gfin"]), writes=["gf"], dsem="gf")
                zt = [sb(f"zt{i}", [128, 2048], F32) for i in range(2)]
                ot = [sb(f"ot{i}", [128, 2048], F32) for i in range(2)]
                junk = sb("junk4", [128, 2048], BF16)
                ss, sd, rstd = sb("ss4", [128, 1], F32), sb("sd4", [128, 1], F32), sb("rstd4", [128, 1], F32)
                P.op("sp", k.dma("sp", zt[0][:], D["z"][0:128, :]), writes=["zt0"], dsem="zt0")
                for t in range(16):
                    b = t % 2
                    if t + 1 < 16:
                        P.op("sp", k.dma("sp", zt[1 - b][:], D["z"][(t + 1) * 128:(t + 2) * 128, :]), writes=[f"zt{1 - b}"], dsem=f"zt{1 - b}")
                    P.op("act", k.act(junk[:], zt[b][:], AF.Square, accum=ss[:, 0:1]), reads=[f"zt{b}"], writes=["junk", "ss"])
                    P.op("act", k.act(sd[:], ss[:], AF.Sqrt, bias=EPS, scale=1.0 / 2048), reads=["ss"], writes=["sd"])
                    P.op("dve", k.rcp(rstd[:], sd[:]), reads=["sd"], writes=["rstd"])
                    P.op("dve", k.stt("dve", ot[b][:], zt[b][:], rstd[:, 0:1], gf[:], ALU.mult, ALU.mult), reads=[f"zt{b}", "rstd", "gf"], writes=[f"ot{b}"])
                    P.op("sp", k.dma("sp", D["out"][t * 128:(t + 1) * 128, :], ot[b][:]), reads=[f"ot{b}"], dsem=f"ot{b}")
                P.emit()
    return nc


def _col(v):
    return np.ascontiguousarray(v.reshape(16, 128).T)


def _nab_tables(rpb_l, heads):
    NEG = np.float32(-30000.0)
    out = np.full((len(heads), 128, 5, 5, 128), NEG, np.float32)
    p = np.arange(128)
    q = np.arange(128)
    for v, pr in enumerate((0, 1, 10, 62, 63)):
        kb = min(max(pr - 2, 0), 59)
        r = 2 * pr + q // 64
        c = q % 64
        rs = np.clip(r - 4, 0, 120)
        cs = np.clip(c - 8, 0, 48)
        for kt in range(5):
            krow = 2 * (kb + kt) + p // 64
            kc = p % 64
            ok = ((krow[:, None] >= rs[None, :]) & (krow[:, None] < rs[None, :] + 8)
                  & (kc[:, None] >= cs[None, :]) & (kc[:, None] < cs[None, :] + 16))
            dr = np.clip(krow[:, None] - r[None, :] + 7, 0, 14)
            dc = np.clip(kc[:, None] - c[None, :] + 15, 0, 30)
            for hi, h in enumerate(heads):
                vals = rpb_l[h][dr, dc]
                out[hi, :, v, kt, :] = np.where(ok, vals, NEG)
    return out.reshape(len(heads), 128, 3200)


def _inputs_h1(inp, j):
    b, hq = divmod(j, 4)
    w_in = inp["w_in"][0]
    A = 1024
    qa = lambda h: slice(h * 256, (h + 1) * 256)
    cols = []
    cols += list(range(0 * A + hq * 256, 0 * A + (hq + 1) * 256))
    cols += list(range(1 * A + hq * 256, 1 * A + (hq + 1) * 256))
    gb = 4 * A + 16
    for hl in range(2):
        h = 2 * hq + hl
        cols += list(range(gb + h * 128, gb + (h + 1) * 128))
        cols += list(range(gb + 1024 + h * 128, gb + 1024 + (h + 1) * 128))
    cols += list(range(2 * A + hq * 256, 2 * A + (hq + 1) * 256))
    cols += list(range(3 * A + hq * 256, 3 * A + (hq + 1) * 256))
    cols += list(range(1 * A + hq * 256, 1 * A + (hq + 1) * 256))
    for hl in range(2):
        h = 2 * hq + hl
        cols += list(range(gb + 2048 + h * 128, gb + 2048 + (h + 1) * 128))
    gcols = [4 * A + g * 4 + hq for g in range(4)]
    cols += gcols
    tri = np.triu(np.ones((128, 128), np.float32))
    return {
        "ccol": _col(inp["c"][b]),
        "ident": np.eye(128, dtype=np.float32),
        "xb": np.ascontiguousarray(inp["x"][b]),
        "wada1": np.ascontiguousarray(inp["w_ada"][0][:, 0:4096]),
        "bada1": np.ascontiguousarray(inp["b_ada"][0][None, 0:4096]),
        "g1col": _col(inp["g_norm1"][0]),
        "win": np.ascontiguousarray(w_in[:, cols]),
        "bg": np.ascontiguousarray(np.broadcast_to(inp["b_gates"][0][[g * 4 + hq for g in range(4)]][None, :], (128, 4))),
        "gha": np.ascontiguousarray(np.broadcast_to(inp["g_head_a"][0][hq * 256:(hq + 1) * 256][None, :], (128, 256))),
        "nab": _nab_tables(inp["rpb"][0], [2 * hq, 2 * hq + 1]),
        "trif": tri,
        "trib": np.ascontiguousarray(tri.T),
        "rmask": np.ascontiguousarray(np.broadcast_to(np.eye(4, dtype=np.float32)[hq][None, :], (128, 4))),
    }


def _inputs_h2(inp, j, mixr=None):
    b, q = divmod(j, 4)
    t0 = q * 2048
    x = inp["x"][b]
    xtok = np.zeros((2050, 2048), np.float32)
    xtok[0:2048] = x[t0:t0 + 2048]
    if t0 > 0:
        xtok[2048] = x[t0 - 1]
    if t0 + 2048 < NTOK:
        xtok[2049] = x[t0 + 2048]
    flags = np.zeros((128, 2), np.float32)
    flags[:, 0] = 1.0 if t0 > 0 else 0.0
    flags[:, 1] = 1.0 if t0 + 2048 < NTOK else 0.0
    rows = []
    for r in range(4):
        rows += list(range(r * 256, (r + 1) * 256))
        rows += list(range(1024 + 2 * r * 128, 1024 + (2 * r + 2) * 128))
    w_up = inp["w_up"][0]
    wup = np.empty((44, 128, 16, 256), np.float32)
    wu = w_up[:, 0:5632].reshape(16, 128, 44, 128)
    wg = w_up[:, 5632:].reshape(16, 128, 44, 128)
    wup[:, :, :, 0:128] = wu.transpose(2, 1, 0, 3)
    wup[:, :, :, 128:256] = wg.transpose(2, 1, 0, 3)
    wd = inp["w_down"][0].reshape(2, 22, 128, 4, 512)
    wdn = np.ascontiguousarray(wd.transpose(3, 0, 2, 1, 4))
    d = {
        "ccol": _col(inp["c"][b]),
        "ident": np.eye(128, dtype=np.float32),
        "xtok": xtok,
        "flags": flags,
        "wada2": np.ascontiguousarray(inp["w_ada"][0][:, 4096:]),
        "bada2": np.ascontiguousarray(inp["b_ada"][0][None, 4096:]),
        "wout": np.ascontiguousarray(inp["w_out"][0][rows, :]),
        "g2col": _col(inp["g_norm2"][0]),
        "wup": wup,
        "convw": np.ascontiguousarray(inp["conv_w"][0].reshape(3, 44, 128).transpose(2, 1, 0)),
        "convb": np.ascontiguousarray(inp["conv_b"][0].reshape(44, 128).T),
        "wdn": wdn,
        "gfin": np.ascontiguousarray(np.broadcast_to(inp["g_final"][None, :], (128, 2048))),
    }
    if mixr is not None:
        d["mixr"] = mixr
    return d


MODE = "fused"
_NC_CACHE = {}


def _get_nc(mode):
    if mode not in _NC_CACHE:
        _NC_CACHE[mode] = build(mode)
    return _NC_CACHE[mode]


def run_h1(inp, cores=range(8)):
    nc = _get_nc("h1")
    cores = list(cores)
    maps = [_inputs_h1(inp, j) for j in cores]
    res = run_bass_kernel_spmd(nc, maps, core_ids=list(range(len(cores))))
    return [np.asarray(r["mixs"]) for r in res.results]


def exchange(mixs):
    out = []
    for j in range(8):
        b, q = divmod(j, 4)
        out.append(np.ascontiguousarray(np.concatenate([mixs[b * 4 + r][q, r] for r in range(4)], axis=0)))
    return out


def run_h2(inp, mixr):
    nc = _get_nc("h2")
    maps = [_inputs_h2(inp, j, mixr[j]) for j in range(8)]
    res = run_bass_kernel_spmd(nc, maps, core_ids=list(range(8)))
    return [np.asarray(r["out"]) for r in res.results]


def kernel(**inputs):
    inp = {k_: np.asarray(v) for k_, v in inputs.items()}
    if MODE == "fused":
        nc = _get_nc("fused")
        maps = []
        for j in range(8):
            d = _inputs_h1(inp, j)
            d.update(_inputs_h2(inp, j))
            maps.append(d)
        res = run_bass_kernel_spmd(nc, maps, core_ids=list(range(8)))
        outs = [np.asarray(r["out"]) for r in res.results]
    else:
        outs = run_h2(inp, exchange(run_h1(inp)))
    out = np.stack(outs, 0).reshape(2, NTOK, 2048)
    return out.astype(np.float32)
```

```python
import contextlib
import numpy as np
import ml_dtypes
import concourse.bass as bass
import concourse.mybir as mybir
from concourse.bass_utils import run_bass_kernel_spmd

F32 = mybir.dt.float32
BF16 = mybir.dt.bfloat16
AF = mybir.ActivationFunctionType
ALU = mybir.AluOpType
AX = mybir.AxisListType
EPS = 1e-6
ENGS = ("pe", "act", "dve", "pool", "sp")


class _Op:
    __slots__ = ("eng", "fn", "deps", "ms", "val", "dsem", "idx")

    def __init__(self, eng, fn, dsem, idx):
        self.eng, self.fn, self.dsem, self.idx = eng, fn, dsem, idx
        self.deps, self.ms, self.val = [], False, None


class Prog:
    stopped = False
    pool = {}

    def __init__(self, nc, tag):
        self.nc, self.tag = nc, tag
        self.ops, self.lastw, self.readers, self.dsems = [], {}, {}, {}

    def op(self, eng, fn, reads=(), writes=(), dsem=None):
        if Prog.stopped:
            return None
        o = _Op(eng, fn, dsem, len(self.ops))
        deps = {}
        for r in reads:
            w = self.lastw.get(r)
            if w is not None:
                deps[w.idx] = w
        for r in writes:
            w = self.lastw.get(r)
            if w is not None:
                deps[w.idx] = w
            for rd in self.readers.get(r, ()):
                deps[rd.idx] = rd
        for d in deps.values():
            if d.eng == "pe" and eng == "pe" and d.dsem is None and dsem is None:
                continue
            o.deps.append(d)
            d.ms = True
        for r in writes:
            self.lastw[r] = o
            self.readers[r] = []
        for r in reads:
            if r not in writes:
                self.readers.setdefault(r, []).append(o)
        if dsem is not None:
            self.dsems.setdefault(dsem, 0)
        self.ops.append(o)
        return o

    def emit(self):
        if Prog.stopped:
            return
        nc = self.nc
        G = Prog.pool.setdefault(id(nc), {"es": {}, "ec": {e: 0 for e in ENGS}, "slots": [], "sc": []})
        for e in ENGS:
            if e not in G["es"]:
                G["es"][e] = nc.alloc_semaphore(f"s_{e}")
        slot = {}
        for i, kname in enumerate(self.dsems):
            if i >= len(G["slots"]):
                G["slots"].append(nc.alloc_semaphore(f"d_{i}"))
                G["sc"].append(0)
            slot[kname] = i
        per = {e: [o for o in self.ops if o.eng == e] for e in ENGS}
        for e in ENGS:
            if per[e]:
                per[e][-1].ms = True
        cnt = dict(G["ec"])
        dcnt = {kname: G["sc"][i] for kname, i in slot.items()}
        for o in self.ops:
            if o.dsem is not None:
                dcnt[o.dsem] += 16
                o.val = dcnt[o.dsem]
            elif o.ms:
                cnt[o.eng] += 1
                o.val = cnt[o.eng]
        esem = G["es"]
        dsem = {kname: G["slots"][i] for kname, i in slot.items()}

        def run(e, engobj):
            waited = {}
            for o in per[e]:
                for d in o.deps:
                    if d.dsem is not None:
                        key, sem = ("d", d.dsem), dsem[d.dsem]
                    else:
                        key, sem = ("e", d.eng), esem[d.eng]
                    if waited.get(key, 0) < d.val:
                        engobj.wait_ge(sem, d.val)
                        waited[key] = d.val
                ins = o.fn()
                if o.dsem is not None:
                    ins.then_inc(dsem[o.dsem], 16)
                elif o.ms:
                    ins.then_inc(esem[e], 1)
            for kname, v in dcnt.items():
                if v > G["sc"][slot[kname]] and waited.get(("d", kname), 0) < v:
                    engobj.wait_ge(dsem[kname], v)
            for e2 in ENGS:
                if cnt[e2] > G["ec"][e2] and waited.get(("e", e2), 0) < cnt[e2]:
                    engobj.wait_ge(esem[e2], cnt[e2])

        with nc.Block() as block:
            block.tensor(lambda eng: run("pe", eng))
            block.scalar(lambda eng: run("act", eng))
            block.vector(lambda eng: run("dve", eng))
            block.gpsimd(lambda eng: run("pool", eng))
            block.sync(lambda eng: run("sp", eng))
        for kname, i in slot.items():
            G["sc"][i] = dcnt[kname]
        G["ec"] = cnt
        nc.all_engine_barrier()


class K:
    def __init__(self, nc):
        self.nc = nc
        self.v = {"dve": nc.vector, "pool": nc.gpsimd}
        self.q = {"sp": nc.sync, "pool": nc.gpsimd, "act": nc.scalar}

    def mm(self, out, lhsT, rhs, start=True, stop=True):
        return lambda: self.nc.tensor.matmul(out, lhsT, rhs, start=start, stop=stop)

    def tr(self, out, in_, ident):
        return lambda: self.nc.tensor.transpose(out, in_, ident)

    def act(self, out, in_, func, bias=None, scale=None, accum=None):
        kw = {}
        if bias is not None:
            kw["bias"] = bias
        if scale is not None:
            kw["scale"] = scale
        if accum is not None:
            kw["accum_out"] = accum
        return lambda: self.nc.scalar.activation(out, in_, func, **kw)

    def dma(self, q, out, in_, slow=False):
        if slow:
            return lambda: self.q[q].dma_start(out=out, in_=in_, allow_slow_non_contiguous=True)
        return lambda: self.q[q].dma_start(out=out, in_=in_)

    def ts(self, e, out, in0, s1, s2, op0, op1=None):
        if op1 is None:
            return lambda: self.v[e].tensor_scalar(out, in0, s1, None, op0)
        return lambda: self.v[e].tensor_scalar(out, in0, s1, s2, op0, op1)

    def tt(self, e, out, in0, in1, op):
        return lambda: self.v[e].tensor_tensor(out, in0, in1, op)

    def stt(self, e, out, in0, scalar, in1, op0, op1):
        return lambda: self.v[e].scalar_tensor_tensor(out, in0, scalar, in1, op0, op1)

    def cp(self, e, out, in_):
        return lambda: self.v[e].tensor_copy(out, in_)

    def ms(self, e, ap, c):
        return lambda: self.v[e].memset(ap, c)

    def rcp(self, out, in_):
        return lambda: self.nc.vector.reciprocal(out, in_)

    def rmax(self, out, in_):
        return lambda: self.nc.vector.reduce_max(out, in_, AX.X)


NTOK = 8192
TRAILER = "none"
STOP_AFTER = ""


class _Stop(Exception):
    pass


def _stop(P, tag):
    if STOP_AFTER == tag:
        P.emit()
        Prog.stopped = True
NT = 64
NWIN = 2052
SCALE_B = 128 ** -0.5


def build(mode):
    Prog.stopped = False
    nc = bass.Bass("TRN2", target_bir_lowering=False)
    k = K(nc)
    dt = lambda name, shape, dty, kind: nc.dram_tensor(name, shape, dty, kind=kind).ap()
    IN, OUT, INT = "ExternalInput", "ExternalOutput", "Internal"
    do1 = mode in ("h1", "fused")
    do2 = mode in ("h2", "fused")
    D = {}
    D["ccol"] = dt("ccol", [128, 16], F32, IN)
    D["ident"] = dt("ident", [128, 128], F32, IN)
    if do1:
        D["xb"] = dt("xb", [NTOK, 2048], F32, IN)
        D["wada1"] = dt("wada1", [2048, 4096], F32, IN)
        D["bada1"] = dt("bada1", [1, 4096], F32, IN)
        D["g1col"] = dt("g1col", [128, 16], F32, IN)
        D["win"] = dt("win", [2048, NWIN], F32, IN)
        D["bg"] = dt("bg", [128, 4], F32, IN)
        D["gha"] = dt("gha", [128, 256], F32, IN)
        D["nab"] = dt("nab", [2, 128, 3200], F32, IN)
        D["trif"] = dt("trif", [128, 128], F32, IN)
        D["trib"] = dt("trib", [128, 128], F32, IN)
        D["qAT"] = dt("qAT_d", [NT, 128, 2, 128], BF16, INT)
        D["kAT"] = dt("kAT_d", [NT, 128, 2, 128], BF16, INT)
        D["qBT"] = dt("qBT_d", [2, NT, 128, 128], BF16, INT)
        D["kBT"] = dt("kBT_d", [2, NT, 128, 128], BF16, INT)
        D["vA"] = dt("vA_d", [NT, 128, 257], BF16, INT)
        D["og"] = dt("og_d", [NT, 128, 256], F32, INT)
        D["ktok"] = dt("ktok_d", [NT, 128, 256], BF16, INT)
        D["vB"] = dt("vB_d", [NT, 128, 256], BF16, INT)
        D["hf"] = dt("hf_d", [NT, 128, 256], F32, INT)
        D["mixs"] = dt("mixs", [4, 4, 512, 2050], BF16, OUT if mode == "h1" else INT)
        D["rmask"] = dt("rmask", [128, 4], F32, IN)
    if do2:
        D["mixr"] = dt("mixr", [2048, 2050], BF16, IN if mode == "h2" else INT)
        D["xtok"] = dt("xtok", [2050, 2048], F32, IN)
        D["flags"] = dt("flags", [128, 2], F32, IN)
        D["wada2"] = dt("wada2", [2048, 8192], F32, IN)
        D["bada2"] = dt("bada2", [1, 8192], F32, IN)
        D["wout"] = dt("wout", [2048, 2048], F32, IN)
        D["g2col"] = dt("g2col", [128, 16], F32, IN)
        D["wup"] = dt("wup", [44, 128, 16, 256], F32, IN)
        D["convw"] = dt("convw", [128, 44, 3], F32, IN)
        D["convb"] = dt("convb", [128, 44], F32, IN)
        D["wdn"] = dt("wdn", [4, 2, 128, 22, 512], F32, IN)
        D["gfin"] = dt("gfin", [128, 2048], F32, IN)
        D["h2T"] = dt("h2T_d", [128, 16, 2050], BF16, INT)
        D["x1"] = dt("x1_d", [2048, 2048], F32, INT)
        D["z"] = dt("z_d", [2048, 2048], F32, INT)
        D["out"] = dt("out", [2048, 2048], F32, OUT)

    with contextlib.ExitStack() as top:
        gsb = lambda name, shape, dty: top.enter_context(nc.sbuf_tensor(name, shape, dty))
        ident_f = gsb("ident_f", [128, 128], F32)
        ident_b = gsb("ident_b", [128, 128], BF16)
        ones_f = gsb("ones_f", [128, 128], F32)
        ones_b = gsb("ones_b", [128, 128], BF16)
        gates_sb = gsb("gates_sb", [128, NT, 4], F32)
        ga2_bc = gsb("ga2_bc", [128, 2048], F32)
        ga1_bcP = gsb("ga1_bcP", [128, 2048], F32)
        Gc2P = gsb("Gc2P", [128, 16], F32)
        shc2P = gsb("shc2P", [128, 16], F32)

        def consts(P):
            P.op("sp", k.dma("sp", ident_f[:], D["ident"]), writes=["ident_f"], dsem="c_idf")
            P.op("pool", k.dma("pool", ident_b[:], D["ident"]), writes=["ident_b"], dsem="c_idb")
            P.op("dve", k.ms("dve", ones_f[:], 1.0), writes=["ones_f"])
            P.op("dve", k.ms("dve", ones_b[:], 1.0), writes=["ones_b"])

        def mk_modrows(P, sb, psb, tag, nmax):
            cc = sb("cc" + tag, [128, 16], F32)
            scb = sb("scb" + tag, [128, 16], BF16)
            badar = sb("badar" + tag, [1, nmax], F32)
            wst = [sb(f"wst{tag}{i}", [128, 16, 256], BF16) for i in range(2)]
            P.op("sp", k.dma("sp", cc[:], D["ccol"]), writes=["cc"], dsem="cc")
            P.op("act", k.act(scb[:], cc[:], AF.Silu), reads=["cc"], writes=["scb"])
            state = {"i": 0}

            def run(wada, bada, c0, ncols, modrow):
                P.op("sp", k.dma("sp", badar[0:1, 0:ncols], bada[:, c0:c0 + ncols]), writes=["badar"], dsem="badar")
                for cb in range(ncols // 256):
                    b = state["i"] % 2
                    state["i"] += 1
                    src = wada[:, c0 + cb * 256:c0 + (cb + 1) * 256].rearrange("(k p) n -> p k n", p=128)
                    P.op("pool", k.dma("pool", wst[b][:], src), writes=[f"wst{b}"], dsem=f"wst{b}")
                    for kk in range(16):
                        P.op("pe", k.mm(psb[0:1, 0:256], scb[:, kk:kk + 1], wst[b][:, kk, :], kk == 0, kk == 15),
                             reads=["scb", f"wst{b}"], writes=["pmisc"])
                    P.op("dve", k.tt("dve", modrow[0:1, cb * 256:(cb + 1) * 256], psb[0:1, 0:256],
                                     badar[0:1, cb * 256:(cb + 1) * 256], ALU.add),
                         reads=["pmisc", "badar"], writes=["modrow"])
            return run

        def row2col(P, psb, modrow, off, col0, res):
            for kk in range(16):
                P.op("pe", k.mm(psb[:, col0 + kk:col0 + kk + 1], modrow[0:1, off + kk * 128:off + (kk + 1) * 128],
                                ones_f[0:1, 0:1]), reads=["modrow", "modrow2", "ones_f"], writes=[res])

        def norm_units(P, xt, xs, M, ss, sd, rstd, junk, tpb, Gc, shc, dst_fn, rx, rdst, tag=""):
            rxs = rx if xs is xt else "xs_shared"

            def u0():
                P.op("act", k.act(junk[0:M, :], xt[0:M, :], AF.Square, accum=ss[0:M, 0:1]), reads=[rx], writes=["junk", "ss" + tag])
                P.op("act", k.act(sd[0:M, 0:1], ss[0:M, 0:1], AF.Sqrt, bias=EPS, scale=1.0 / 2048), reads=["ss" + tag], writes=["sd" + tag])
                P.op("dve", k.rcp(rstd[0:M, 0:1], sd[0:M, 0:1]), reads=["sd" + tag], writes=["rstd" + tag])
                P.op("dve", k.ts("dve", xs[0:M, :], xt[0:M, :], rstd[0:M, 0:1], None, ALU.mult), reads=[rx, "rstd" + tag], writes=[rxs])

            def ug(g):
                def f():
                    bank = tpb[g % 2]
                    for j in range(4):
                        kk = g * 4 + j
                        P.op("pe", k.tr(bank[:, j * 128:j * 128 + M], xs[0:M, kk * 128:(kk + 1) * 128], ident_f[0:M, 0:M]),
                             reads=[rxs, "ident_f"], writes=[f"tp{g % 2}"])
                    for j in range(4):
                        kk = g * 4 + j
                        src = bank[:, j * 128:j * 128 + M]
                        if g % 2 == 0:
                            P.op("act", k.act(dst_fn(kk), src, AF.Identity, bias=shc[:, kk:kk + 1], scale=Gc[:, kk:kk + 1]),
                                 reads=[f"tp{g % 2}", "Gc"], writes=[rdst(kk)])
                        else:
                            P.op("dve", k.ts("dve", dst_fn(kk), src, Gc[:, kk:kk + 1], shc[:, kk:kk + 1], ALU.mult, ALU.add),
                                 reads=[f"tp{g % 2}", "Gc"], writes=[rdst(kk)])
                return f
            return [u0] + [ug(g) for g in range(4)]

        def norm_T(P, xt, xs, M, ss, sd, rstd, junk, tpb, Gc, shc, dst_fn, rx, rdst, ev_i):
            for u in norm_units(P, xt, xs, M, ss, sd, rstd, junk, tpb, Gc, shc, dst_fn, rx, rdst):
                u()

        if do1:
            with contextlib.ExitStack() as st:
                sb = lambda name, shape, dty: st.enter_context(nc.sbuf_tensor(name, shape, dty))
                psum = lambda name, dty=F32, n=512: st.enter_context(nc.psum_tensor(name, [128, n], dty))
                P = Prog(nc, "a")
                consts(P)
                tpb = [psum("tp0"), psum("tp1")]
                fmb = [psum("fm0"), psum("fm1")]
                tmb = [psum("tm0"), psum("tm1")]
                pmisc = psum("pmisc")
                Wb = sb("Wb", [128, 16, NWIN], BF16)
                P.op("pool", k.dma("pool", Wb[:], D["win"].rearrange("(k p) n -> p k n", p=128)), writes=["Wb"], dsem="Wb")
                modrow = sb("modrow", [1, 4096], F32)
                mk_modrows(P, sb, pmisc, "1", 4096)(D["wada1"], D["bada1"], 0, 4096, modrow)
                g1c = sb("g1c", [128, 16], F32)
                P.op("sp", k.dma("sp", g1c[:], D["g1col"]), writes=["g1c"], dsem="g1c")
                row2col(P, pmisc, modrow, 2048, 256, "pmisc")
                row2col(P, pmisc, modrow, 0, 272, "pmisc")
                Gc = sb("Gc", [128, 16], F32)
                shc = sb("shc", [128, 16], F32)
                P.op("dve", k.ts("dve", Gc[:], pmisc[:, 256:272], 1.0, None, ALU.add), reads=["pmisc"], writes=["Gc0"])
                P.op("dve", k.tt("dve", Gc[:], Gc[:], g1c[:], ALU.mult), reads=["Gc0", "g1c"], writes=["Gc"])
                P.op("dve", k.cp("dve", shc[:], pmisc[:, 272:288]), reads=["pmisc"], writes=["Gc"])
                bgs = sb("bgs", [128, 4], F32)
                ghas = sb("ghas", [128, 256], F32)
                P.op("sp", k.dma("sp", ghas[:], D["gha"]), writes=["ghas"], dsem="ghas")

                xts = [sb(f"xt{i}", [128, 2048], F32) for i in range(2)]
                hT = [sb(f"hT{i}", [128, 16, 512], BF16) for i in range(2)]
                junk = sb("junk", [128, 2048], BF16)
                ss = sb("ss", [128, 1], F32)
                sd = sb("sd", [128, 1], F32)
                rstd = sb("rstd", [128, 1], F32)
                fmst = [sb(f"fmst{i}", [128, 512], BF16) for i in range(4)]
                vst = [sb(f"vst{i}", [128, 257], BF16) for i in range(2)]
                ogs = [sb(f"ogs{i}", [128, 256], F32) for i in range(2)]
                kts = [sb(f"kts{i}", [128, 256], BF16) for i in range(2)]
                vbs = [sb(f"vbs{i}", [128, 256], BF16) for i in range(2)]
                for i in range(2):
                    P.op("dve", k.ms("dve", vst[i][:, 256:257], 1.0), writes=[f"vst{i}"])

                def load_x(t):
                    P.op("sp", k.dma("sp", xts[t % 2][:], D["xb"][t * 128:(t + 1) * 128, :]), writes=[f"xt{t % 2}"], dsem=f"xt{t % 2}")

                ssL = [sb(f"ssL{i}", [128, 1], F32) for i in range(2)]
                sdL = [sb(f"sdL{i}", [128, 1], F32) for i in range(2)]
                rstdL = [sb(f"rstdL{i}", [128, 1], F32) for i in range(2)]

                def prep_units(tg):
                    us = []
                    g = tg % 2
                    for ti in range(4):
                        t = tg * 4 + ti
                        xt = xts[t % 2]
                        tu = norm_units(P, xt, xt, 128, ssL[t % 2], sdL[t % 2], rstdL[t % 2], junk, tpb, Gc, shc,
                                        lambda kk, g=g, ti=ti: hT[g][:, kk, ti * 128:(ti + 1) * 128], f"xt{t % 2}",
                                        (lambda kk, g=g, ti=ti: f"hT{g}_{ti}_{kk}"), tag=str(t % 2))
                        us.append(tu[0])
                        us.append(tu[1])
                        u2 = tu[2]
                        us.append((lambda t=t, u2=u2: (load_x(t + 1), u2())) if t + 1 < NT else u2)
                        us.extend(tu[3:])
                    return us

                pending_units = []

                def pop_unit():
                    if pending_units:
                        pending_units.pop(0)()

                fmi = 0
                load_x(0)
                for u in prep_units(0):
                    u()
                for tg in range(16):
                    g = tg % 2
                    pending_units = prep_units(tg + 1) if tg + 1 < 16 else []
                    for fc in range(8):
                        bank = fmb[fc % 2]
                        for kk in range(16):
                            P.op("pe", k.mm(bank[:, 0:512], Wb[:, kk, fc * 128:(fc + 1) * 128], hT[g][:, kk, :], kk == 0, kk == 15),
                                 reads=["Wb"] + [f"hT{g}_{ti_}_{kk}" for ti_ in range(4)], writes=[f"fm{fc % 2}"])
                        fb = fmi % 4
                        fmi += 1
                        scl = 0.0625 if fc in (2, 3) else 1.0
                        if fc % 2 == 0:
                            P.op("act", k.act(fmst[fb][:], bank[:, 0:512], AF.Copy, scale=scl), reads=[f"fm{fc % 2}"], writes=[f"fmst{fb}"])
                        else:
                            P.op("dve", k.ts("dve", fmst[fb][:], bank[:, 0:512], scl, None, ALU.mult), reads=[f"fm{fc % 2}"], writes=[f"fmst{fb}"])
                        src = fmst[fb][:].rearrange("p (c t) -> p c t", c=4)
                        c0 = tg * 4
                        if fc < 4:
                            arr = D["qAT"] if fc < 2 else D["kAT"]
                            dst = arr[c0:c0 + 4, :, fc % 2, :].rearrange("c p t -> p c t")
                        else:
                            arr = D["qBT"] if fc % 2 == 0 else D["kBT"]
                            dst = arr[(fc - 4) // 2, c0:c0 + 4, :, :].rearrange("c p t -> p c t")
                        P.op("sp", k.dma("sp", dst, src), reads=[f"fmst{fb}"], writes=[f"d_fm{fc}"], dsem=f"fmst{fb}")
                        pop_unit()
                    for ti in range(4):
                        t = tg * 4 + ti
                        b = t % 2
                        lhs = lambda kk: hT[g][:, kk, ti * 128:(ti + 1) * 128]
                        for kk in range(16):
                            P.op("pe", k.mm(tmb[0][:, 0:512], lhs(kk), Wb[:, kk, 1024:1536], kk == 0, kk == 15),
                                 reads=["Wb", f"hT{g}_{ti}_{kk}"], writes=["tm0"])
                        pop_unit()
                        for kk in range(16):
                            P.op("pe", k.mm(tmb[1][:, 0:512], lhs(kk), Wb[:, kk, 1536:2048], kk == 0, kk == 15),
                                 reads=["Wb", f"hT{g}_{ti}_{kk}"], writes=["tm1"])
                        pop_unit()
                        for kk in range(16):
                            P.op("pe", k.mm(pmisc[:, 0:4], lhs(kk), Wb[:, kk, 2048:2052], kk == 0, kk == 15),
                                 reads=["Wb", f"hT{g}_{ti}_{kk}"], writes=["pmisc"])
                        pop_unit()
                        P.op("act", k.act(vst[b][:, 0:256], tmb[0][:, 0:256], AF.Copy), reads=["tm0"], writes=[f"vst{b}"])
                        P.op("act", k.act(ogs[b][:], tmb[0][:, 256:512], AF.Sigmoid), reads=["tm0"], writes=[f"ogs{b}"])
                        P.op("pool", k.tt("pool", ogs[b][:], ogs[b][:], ghas[:], ALU.mult), reads=["ghas"], writes=[f"ogs{b}"])
                        P.op("dve", k.ts("dve", kts[b][:], tmb[1][:, 0:256], 0.0625, None, ALU.mult), reads=["tm1"], writes=[f"kts{b}"])
                        P.op("dve", k.cp("dve", vbs[b][:], tmb[1][:, 256:512]), reads=["tm1"], writes=[f"vbs{b}"])
                        P.op("dve", k.cp("dve", gates_sb[:, t, :], pmisc[:, 0:4]), reads=["pmisc"], writes=["gates"])
                        P.op("sp", k.dma("sp", D["vA"][t], vst[b][:]), reads=[f"vst{b}"], writes=["d_vA"], dsem=f"vst{b}")
                        P.op("sp", k.dma("sp", D["og"][t], ogs[b][:]), reads=[f"ogs{b}"], writes=["d_og"], dsem=f"ogs{b}")
                        P.op("sp", k.dma("sp", D["ktok"][t], kts[b][:]), reads=[f"kts{b}"], writes=["d_ktok"], dsem=f"kts{b}")
                        P.op("sp", k.dma("sp", D["vB"][t], vbs[b][:]), reads=[f"vbs{b}"], writes=["d_vB"], dsem=f"vbs{b}")
                    while pending_units:
                        pop_unit()
                P.emit()

            with contextlib.ExitStack() as st:
              if STOP_AFTER != "a":
                  sb = lambda name, shape, dty: st.enter_context(nc.sbuf_tensor(name, shape, dty))
                  psum = lambda name, dty=F32, n=512: st.enter_context(nc.psum_tensor(name, [128, n], dty))
                  P = Prog(nc, "b")
                  pA = psum("pA")
                  pS = [psum("pS0"), psum("pS1")]
                  pN = [psum("pN0"), psum("pN1")]
                  pK = [psum("pK0"), psum("pK1")]
                  pT = psum("pT", BF16, 1024)
                  if STOP_AFTER == "bY":
                      tmpo = sb("tmpo", [128, 64], F32)
                      P.op("pe", k.mm(pA[:, 0:64], ones_f[:], ident_f[:, 0:64]), reads=["ones_f"], writes=["pA"])
                      P.op("dve", k.cp("dve", tmpo[:], pA[:, 0:64]), reads=["pA"], writes=["tmpo"])
                  _stop(P, "bY")
                  _stop(P, "bX")
                  bgs = sb("bgs2", [128, 4], F32)
                  P.op("sp", k.dma("sp", bgs[:], D["bg"]), writes=["bgs"], dsem="bgs")
                  masks = [sb("maskf", [128, 128], F32), sb("maskb", [128, 128], F32)]
                  P.op("sp", k.dma("sp", masks[0][:], D["trif"]), writes=["mask0"], dsem="mask0")
                  P.op("sp", k.dma("sp", masks[1][:], D["trib"]), writes=["mask1"], dsem="mask1")
                  zeros_b = sb("zeros_b", [128, 16, 1], BF16)
                  rmask = sb("rmask_s", [128, 4], F32)
                  P.op("sp", k.dma("sp", rmask[:], D["rmask"]), writes=["rmask"], dsem="rmask")
                  P.op("dve", k.ms("dve", zeros_b[:], 0.0), writes=["zeros_b"])
                  P.op("sp", k.dma("sp", D["mixs"][0, :, :, 0:1].rearrange("r (k p) o -> p (r k) o", p=128), zeros_b[:], slow=True), reads=["zeros_b"], dsem="zp0")
                  P.op("sp", k.dma("sp", D["mixs"][3, :, :, 2049:2050].rearrange("r (k p) o -> p (r k) o", p=128), zeros_b[:], slow=True), reads=["zeros_b"], dsem="zp1")
                  _stop(P, "b0")
                  wv, flv, decv = [], [], []
                  sm = lambda name: sb(name, [128, 64], F32)
                  for di in range(2):
                      gi, gf = 2 * di, 2 * di + 1
                      e = "dve"
                      zf, li, az, ez, lz, mz, lf, u, bc = [sm(f"{n}{di}") for n in ("zf", "li", "az", "ez", "lz", "mz", "lf", "u", "bc")]
                      w_, fl_, dec_ = sm(f"w{di}"), sm(f"fl{di}"), sm(f"dec{di}")
                      t1, t2 = sm(f"t1{di}"), sm(f"t2{di}")
                      rows = sb(f"rows{di}", [1, 6, 64], F32)
                      umc = sb(f"umc{di}", [64, 1], F32)
                      R = lambda n: f"{n}{di}"
                      P.op(e, k.ts(e, zf[:], gates_sb[:, :, gf], bgs[:, gf:gf + 1], None, ALU.add), reads=["gates", "bgs"], writes=[R("zf")])
                      P.op(e, k.ts(e, li[:], gates_sb[:, :, gi], bgs[:, gi:gi + 1], None, ALU.add), reads=["gates", "bgs"], writes=[R("li")])
                      P.op("act", k.act(az[:], zf[:], AF.Abs), reads=[R("zf")], writes=[R("az")])
                      P.op("act", k.act(ez[:], az[:], AF.Exp, scale=-1.0), reads=[R("az")], writes=[R("ez")])
                      P.op("act", k.act(lz[:], ez[:], AF.Ln, bias=1.0), reads=[R("ez")], writes=[R("lz")])
                      P.op(e, k.ts(e, mz[:], zf[:], 0.0, None, ALU.min), reads=[R("zf")], writes=[R("mz")])
                      P.op(e, k.tt(e, lf[:], mz[:], lz[:], ALU.subtract), reads=[R("mz"), R("lz")], writes=[R("lf")])
                      if di == 0:
                          _stop(P, "b1_1")
                      P.op("pe", k.mm(pA[:, 0:64], masks[di][:], lf[:]), reads=[R("lf"), f"mask{di}"], writes=["pA"])
                      P.op("pe", k.mm(pA[:, 64:128], ones_f[:], lf[:]), reads=[R("lf"), "ones_f"], writes=["pA"])
                      P.op(e, k.cp(e, bc[:], pA[:, 0:64]), reads=["pA"], writes=[R("bc")])
                      P.op(e, k.tt(e, u[:], li[:], bc[:], ALU.subtract), reads=[R("li"), R("bc")], writes=[R("u")])
                      P.op(e, k.cp(e, rows[0:1, 1, :], pA[0:1, 64:128]), reads=["pA"], writes=[R("gsum")])
                      if di == 0:
                          _stop(P, "b1_1a")
                      P.op("pe", k.tr(pA[0:64, 128:256], u[:], ident_f[:]), reads=[R("u"), "ident_f"], writes=["pA"])
                      P.op(e, k.rmax(umc[:], pA[0:64, 128:256]), reads=["pA"], writes=[R("umc")])
                      if di == 0:
                          _stop(P, "b1_1b")
                      P.op("pe", k.mm(pA[0:1, 256:320], umc[:], ident_f[0:64, 0:64]), reads=[R("umc"), "ident_f"], writes=["pA"])
                      P.op(e, k.cp(e, rows[0:1, 0, :], pA[0:1, 256:320]), reads=["pA"], writes=[R("umax")])
                      se = "dve"
                      if di == 0:
                          _stop(P, "b1_2")
                      sc = sb(f"scan{di}", [1, 4, 64], F32)
                      tmp = sb(f"scant{di}", [1, 64], F32)
                      P.op(se, k.cp(se, sc[0:1, 0, :], rows[0:1, 1, :]), reads=[R("gsum")], writes=[R("scan")])
                      P.op(se, k.tt(se, sc[0:1, 1, :], rows[0:1, 0, :], rows[0:1, 1, :], ALU.add), reads=[R("umax"), R("gsum")], writes=[R("scan")])
                      cur, nxt = 0, 2
                      for d_ in (1, 2, 4, 8, 16, 32):
                          n_ = 64 - d_
                          if di == 0:
                              lo, hi = slice(0, n_), slice(d_, 64)
                          else:
                              lo, hi = slice(d_, 64), slice(0, n_)
                          Gc_, Hc_, Gn_, Hn_ = sc[0:1, cur, :], sc[0:1, cur + 1, :], sc[0:1, nxt, :], sc[0:1, nxt + 1, :]
                          P.op(se, k.cp(se, sc[0:1, nxt:nxt + 2, :], sc[0:1, cur:cur + 2, :]), writes=[R("scan")])
                          P.op(se, k.tt(se, tmp[0:1, 0:n_], sc[0:1, cur + 1, lo], sc[0:1, cur, hi], ALU.add), writes=[R("scan")])
                          P.op(se, k.tt(se, sc[0:1, nxt + 1, hi], tmp[0:1, 0:n_], sc[0:1, cur + 1, hi], ALU.max), writes=[R("scan")])
                          P.op(se, k.tt(se, sc[0:1, nxt, hi], sc[0:1, cur, lo], sc[0:1, cur, hi], ALU.add), writes=[R("scan")])
                          cur, nxt = nxt, cur
                      P.op(se, k.ms(se, rows[0:1, 2, :], -1e30), writes=[R("scan")])
                      if di == 0:
                          P.op(se, k.cp(se, rows[0:1, 2, 1:64], sc[0:1, cur + 1, 0:63]), writes=[R("scan")])
                      else:
                          P.op(se, k.cp(se, rows[0:1, 2, 0:63], sc[0:1, cur + 1, 1:64]), writes=[R("scan")])
                      P.op(se, k.tt(se, rows[0:1, 3, :], rows[0:1, 2, :], rows[0:1, 0, :], ALU.max), writes=[R("scan")])
                      if di == 0:
                          _stop(P, "b1_3")
                      P.op(e, k.tt(e, rows[0:1, 4, :], rows[0:1, 2, :], rows[0:1, 3, :], ALU.subtract), reads=[R("scan")], writes=[R("dd")])
                      P.op("act", k.act(rows[0:1, 5, :], rows[0:1, 4, :], AF.Exp), reads=[R("dd")], writes=[R("decr")])
                      P.op("pe", k.mm(pA[:, 320:384], ones_f[0:1, :], rows[0:1, 3, :]), reads=[R("scan"), "ones_f"], writes=["pA"])
                      P.op("pe", k.mm(pA[:, 384:448], ones_f[0:1, :], rows[0:1, 5, :]), reads=[R("decr"), "ones_f"], writes=["pA"])
                      P.op(e, k.cp(e, dec_[:], pA[:, 384:448]), reads=["pA"], writes=[R("dec")])
                      P.op(e, k.cp(e, t2[:], pA[:, 320:384]), reads=["pA"], writes=[R("t2")])
                      P.op(e, k.tt(e, t1[:], u[:], t2[:], ALU.subtract), reads=[R("u"), R("t2")], writes=[R("t1")])
                      P.op("act", k.act(w_[:], t1[:], AF.Exp), reads=[R("t1")], writes=[R("w")])
                      P.op(e, k.tt(e, t2[:], bc[:], t2[:], ALU.add), reads=[R("bc")], writes=[R("t2")])
                      P.op("act", k.act(fl_[:], t2[:], AF.Exp, scale=-1.0), reads=[R("t2")], writes=[R("fl")])
                      wv.append(w_)
                      flv.append(fl_)
                      decv.append(dec_)

                  _stop(P, "b1")
                  mod2_steps = []
                  if mode == "fused":
                      cc2 = sb("cc2b", [128, 16], F32)
                      scb2 = sb("scb2b", [128, 16], BF16)
                      badar2 = sb("badar2b", [1, 256], F32)
                      modrow2 = sb("modrow2b", [1, 2048], F32)
                      g2cb = sb("g2cb", [128, 16], F32)
                      wst2 = [sb(f"wst2b{i}", [128, 16, 256], BF16) for i in range(2)]
                      P.op("sp", k.dma("sp", cc2[:], D["ccol"]), writes=["cc2"], dsem="cc2")
                      P.op("sp", k.dma("sp", g2cb[:], D["g2col"]), writes=["g2cb"], dsem="g2cb")
                      P.op("act", k.act(scb2[:], cc2[:], AF.Silu), reads=["cc2"], writes=["scb2"])

                      def mod2_block(part, cb):
                          def f():
                              c0 = part * 2048
                              P.op("sp", k.dma("sp", badar2[0:1, :], D["bada2"][:, c0 + cb * 256:c0 + (cb + 1) * 256]), writes=["badar2"], dsem="badar2")
                              bi = (part * 8 + cb) % 2
                              src = D["wada2"][:, c0 + cb * 256:c0 + (cb + 1) * 256].rearrange("(k p) n -> p k n", p=128)
                              P.op("pool", k.dma("pool", wst2[bi][:], src), writes=[f"wst2_{bi}"], dsem=f"wst2_{bi}")
                              for kk in range(16):
                                  P.op("pe", k.mm(pA[0:1, 0:256], scb2[:, kk:kk + 1], wst2[bi][:, kk, :], kk == 0, kk == 15),
                                       reads=["scb2", f"wst2_{bi}"], writes=["pA"])
                              P.op("dve", k.tt("dve", modrow2[0:1, cb * 256:(cb + 1) * 256], pA[0:1, 0:256], badar2[0:1, 0:256], ALU.add),
                                   reads=["pA", "badar2"], writes=["modrow2"])
                              if cb == 7:
                                  if part in (0, 3):
                                      dst = ga1_bcP if part == 0 else ga2_bc
                                      for nb in range(4):
                                          P.op("pe", k.mm(pA[:, 0:512], ones_f[0:1, :], modrow2[0:1, nb * 512:(nb + 1) * 512]), reads=["modrow2"], writes=["pA"])
                                          P.op("dve", k.cp("dve", dst[:, nb * 512:(nb + 1) * 512], pA[:, 0:512]), reads=["pA"], writes=[f"modout{part}"])
                                  elif part == 1:
                                      row2col(P, pA, modrow2, 0, 272, "pA")
                                      P.op("dve", k.cp("dve", shc2P[:], pA[:, 272:288]), reads=["pA"], writes=["shc2P"])
                                  else:
                                      row2col(P, pA, modrow2, 0, 256, "pA")
                                      P.op("dve", k.ts("dve", Gc2P[:], pA[:, 256:272], 1.0, None, ALU.add), reads=["pA"], writes=["Gc2P0"])
                                      P.op("dve", k.tt("dve", Gc2P[:], Gc2P[:], g2cb[:], ALU.mult), reads=["Gc2P0", "g2cb"], writes=["Gc2P"])
                          return f
                      mod2_steps = [mod2_block(p_, c_) for p_ in range(4) for c_ in range(8)]

                  NG = 2
                  qT4 = [sb(f"qT4_{i}", [128, 4, 2, 128], BF16) for i in range(NG)]
                  kT4 = [sb(f"kT4_{i}", [128, 4, 2, 128], BF16) for i in range(NG)]
                  ktk4 = [sb(f"ktk4_{i}", [128, 4, 256], BF16) for i in range(NG)]
                  vas4 = [sb(f"vas4_{i}", [128, 4, 257], BF16) for i in range(NG)]
                  hfc4 = [sb(f"hfc4_{i}", [128, 4, 256], F32) for i in range(NG)]
                  ogc4 = [sb(f"ogc4_{i}", [128, 4, 256], F32) for i in range(NG)]
                  hst4 = [sb(f"hst4_{i}", [128, 4, 256], F32) for i in range(2)]
                  SPb = [sb(f"SPb{i}", [128, 128], BF16) for i in range(2)]
                  kw = [sb(f"kw{i}", [128, 256], BF16) for i in range(2)]
                  C = sb("C", [128, 2, 257], F32)
                  Cd = sb("Cd", [128, 2, 257], BF16)
                  rrL = [sb(f"rr_{i}", [128, 1], F32) for i in range(2)]
                  rr2L = [sb(f"rr2_{i}", [128, 1], F32) for i in range(2)]
                  hsL = [sb(f"hs_{i}", [128, 256], F32) for i in range(2)]
                  junk2L = [sb(f"junk2_{i}", [128, 256], BF16) for i in range(2)]
                  ss2L = [sb(f"ss2_{i}", [128, 1], F32) for i in range(2)]
                  sd2L = [sb(f"sd2_{i}", [128, 1], F32) for i in range(2)]
                  rmsL = [sb(f"rms_{i}", [128, 1], F32) for i in range(2)]
                  hoL = [sb(f"ho_{i}", [128, 256], BF16) for i in range(2)]
                  mst = [sb(f"mst{i}", [128, 256], BF16) for i in range(2)]
                  MS = [sb(f"MS{i}", [128, 2, 4, 512], BF16) for i in range(2)]
                  NS = [sb(f"NS{i}", [128, 1, 4, 512], BF16) for i in range(2)]
                  evc = [0]

                  def stage_put(bufs, name, nk, f0, cnt, src_fn, tt, rsrc):
                      g, idx = divmod(tt, 4)
                      buf, res = bufs[g % 2], f"{name}{g % 2}"
                      allres = [f"{res}_{kk}_{r_}_{i_}" for kk in range(nk) for r_ in range(4) for i_ in range(4)]
                      for kk in range(nk):
                          for r_ in range(4):
                              dst = buf[:, kk, r_, idx * 128:(idx + 1) * 128]
                              evc[0] += 1
                              wres = [f"{res}_{kk}_{r_}_{idx}"]
                              if evc[0] % 2 == 0:
                                  P.op("act", k.act(dst, src_fn(kk), AF.Copy, scale=rmask[:, r_:r_ + 1]), reads=[rsrc, "rmask"], writes=wres)
                              else:
                                  P.op("dve", k.ts("dve", dst, src_fn(kk), rmask[:, r_:r_ + 1], None, ALU.mult), reads=[rsrc, "rmask"], writes=wres)
                      cnt[g] = cnt.get(g, 0) + 1
                      if cnt[g] == 4:
                          j, g4 = divmod(g, 4)
                          for kk in range(nk):
                              rd = [f"{res}_{kk}_{r_}_{i_}" for r_ in range(4) for i_ in range(4)]
                              dstv = lambda jj, c0, c1, kk=kk: D["mixs"][jj, :, f0 + kk * 128:f0 + (kk + 1) * 128, c0:c1].rearrange("r f t -> f r t")
                              P.op("sp", k.dma("sp", dstv(j, 1 + g4 * 512, 1 + (g4 + 1) * 512), buf[:, kk, :, :]), reads=rd, dsem=res)
                              if g4 == 3 and j < 3:
                                  P.op("sp", k.dma("sp", dstv(j + 1, 0, 1), buf[:, kk, :, 511:512], slow=True), reads=rd, dsem=res)
                              if g4 == 0 and j > 0:
                                  P.op("sp", k.dma("sp", dstv(j - 1, 2049, 2050), buf[:, kk, :, 0:1], slow=True), reads=rd, dsem=res)

                  groups = [(0, list(range(g * 4, g * 4 + 4))) for g in range(16)] + [(1, list(range(g * 4 + 3, g * 4 - 1, -1))) for g in range(15, -1, -1)]

                  def gloads(gi):
                      di, cs = groups[gi]
                      cb_, b = min(cs), gi % NG
                      P.op("sp", k.dma("sp", qT4[b][:], D["qAT"][cb_:cb_ + 4].rearrange("c p k t -> p c k t")), reads=["d_fm0", "d_fm1"], writes=[f"qT4_{b}"], dsem=f"qT4_{b}")
                      P.op("sp", k.dma("sp", kT4[b][:], D["kAT"][cb_:cb_ + 4].rearrange("c p k t -> p c k t")), reads=["d_fm2", "d_fm3"], writes=[f"kT4_{b}"], dsem=f"kT4_{b}")
                      P.op("sp", k.dma("sp", ktk4[b][:], D["ktok"][cb_:cb_ + 4].rearrange("c p e -> p c e")), writes=[f"ktk4_{b}"], dsem=f"ktk4_{b}")
                      P.op("sp", k.dma("sp", vas4[b][:], D["vA"][cb_:cb_ + 4].rearrange("c p e -> p c e")), writes=[f"vas4_{b}"], dsem=f"vas4_{b}")

                  def gloads_h(gi):
                      di, cs = groups[gi]
                      cb_, b = min(cs), gi % NG
                      if di == 1:
                          P.op("sp", k.dma("sp", hfc4[b][:], D["hf"][cb_:cb_ + 4].rearrange("c p e -> p c e")), reads=["d_hf"], writes=[f"hfc4_{b}"], dsem=f"hfc4_{b}")
                          P.op("sp", k.dma("sp", ogc4[b][:], D["og"][cb_:cb_ + 4].rearrange("c p e -> p c e")), writes=[f"ogc4_{b}"], dsem=f"ogc4_{b}")

                  mcnt = {}
                  pend = [None]
                  gloads(0)
                  si = 0
                  for gi, (di, cs) in enumerate(groups):
                      if gi == 16 and pend[0] is not None:
                          pend[0]()
                          pend[0] = None
                      gloads_h(gi)
                      if gi + 1 < len(groups):
                          gloads(gi + 1)
                      if gi < len(mod2_steps):
                          mod2_steps[gi]()
                      cb_, b = min(cs), gi % NG
                      if gi in (0, 16):
                          P.op("dve", k.ms("dve", C[:], 0.0), writes=["C"])
                      for c in cs:
                          ix = c - cb_
                          s2 = si % 2
                          si += 1
                          wc = wv[di][:, c:c + 1]
                          dc = decv[di][:, c:c + 1]
                          rq, rk, rkt, rv = f"qT4_{b}", f"kT4_{b}", f"ktk4_{b}", f"vas4_{b}"
                          for kk in range(2):
                              P.op("pe", k.mm(pS[s2][:, 0:128], kT4[b][:, ix, kk, :], qT4[b][:, ix, kk, :], kk == 0, kk == 1),
                                   reads=[rk, rq], writes=[f"pS{s2}"])
                          P.op("dve", k.stt("dve", SPb[s2][:], pS[s2][:, 0:128], wc, masks[di][:], ALU.mult, ALU.mult),
                               reads=[f"pS{s2}", f"w{di}", f"mask{di}"], writes=[f"SPb{s2}"])
                          P.op("dve", k.ts("dve", C[:], C[:], dc, None, ALU.mult), reads=[f"dec{di}"], writes=["C"])
                          P.op("act", k.act(Cd[:], C[:], AF.Copy), reads=["C"], writes=["Cd"])
                          P.op("pe", k.mm(pN[s2][:, 0:257], SPb[s2][:], vas4[b][:, ix, :], True, False), reads=[f"SPb{s2}", rv], writes=[f"pN{s2}"])
                          for kk in range(2):
                              P.op("pe", k.mm(pN[s2][:, 0:257], qT4[b][:, ix, kk, :], Cd[:, kk, :], False, kk == 1), reads=[rq, "Cd"], writes=[f"pN{s2}"])
                          P.op("act", k.act(kw[s2][:], ktk4[b][:, ix, :], AF.Copy, scale=wc), reads=[rkt, f"w{di}"], writes=[f"kw{s2}"])
                          for kk in range(2):
                              P.op("pe", k.mm(pK[kk][:, 0:257], kw[s2][:, kk * 128:(kk + 1) * 128], vas4[b][:, ix, :]), reads=[f"kw{s2}", rv], writes=[f"pK{kk}"])
                          for kk in range(2):
                              P.op("dve", k.tt("dve", C[:, kk, :], pK[kk][:, 0:257], C[:, kk, :], ALU.add), reads=[f"pK{kk}"], writes=["C"])
                          def epi(di=di, c=c, gi=gi, ix=ix, s2=s2, b=b, cb_=cb_, last=(c == cs[-1])):
                              rr, rr2, hs, junk2, ss2, sd2, rms, ho = rrL[s2], rr2L[s2], hsL[s2], junk2L[s2], ss2L[s2], sd2L[s2], rmsL[s2], hoL[s2]
                              T = lambda n: f"{n}_{s2}"
                              P.op("act", k.act(rr[:], pN[s2][:, 256:257], AF.Abs), reads=[f"pN{s2}"], writes=[T("rr0")])
                              P.op("dve", k.ts("dve", rr[:], rr[:], flv[di][:, c:c + 1], None, ALU.max), reads=[T("rr0"), f"fl{di}"], writes=[T("rr")])
                              P.op("dve", k.rcp(rr2[:], rr[:]), reads=[T("rr")], writes=[T("rr2")])
                              if di == 0:
                                  hb_ = gi % 2
                                  P.op("act", k.act(hst4[hb_][:, ix, :], pN[s2][:, 0:256], AF.Copy, scale=rr2[:, 0:1]), reads=[f"pN{s2}", T("rr2")], writes=[f"hst4_{hb_}"])
                                  if last:
                                      P.op("sp", k.dma("sp", D["hf"][cb_:cb_ + 4].rearrange("c p e -> p c e"), hst4[hb_][:]), reads=[f"hst4_{hb_}"], writes=["d_hf"], dsem=f"hst4_{hb_}")
                              else:
                                  P.op("dve", k.stt("dve", hs[:], pN[s2][:, 0:256], rr2[:, 0:1], hfc4[b][:, ix, :], ALU.mult, ALU.add),
                                       reads=[f"pN{s2}", T("rr2"), f"hfc4_{b}"], writes=[T("hs")])
                                  P.op("act", k.act(junk2[:], hs[:], AF.Square, accum=ss2[:, 0:1]), reads=[T("hs")], writes=[T("junk2"), T("ss2")])
                                  P.op("act", k.act(sd2[:], ss2[:], AF.Sqrt, bias=EPS, scale=1.0 / 256), reads=[T("ss2")], writes=[T("sd2")])
                                  P.op("dve", k.rcp(rms[:], sd2[:]), reads=[T("sd2")], writes=[T("rms")])
                                  P.op("dve", k.stt("dve", ho[:], hs[:], rms[:, 0:1], ogc4[b][:, ix, :], ALU.mult, ALU.mult), reads=[T("hs"), T("rms"), f"ogc4_{b}"], writes=[T("ho")])
                                  for kk in range(2):
                                      P.op("pe", k.tr(pT[:, kk * 128:(kk + 1) * 128], ho[:, kk * 128:(kk + 1) * 128], ident_b[:]), reads=[T("ho"), "ident_b"], writes=["pT"])
                                  P.op("act", k.act(mst[s2][:], pT[:, 0:256], AF.Copy), reads=["pT"], writes=[f"mst{s2}"])
                                  stage_put(MS, "MS", 2, 0, mcnt, lambda kk, s2=s2: mst[s2][:, kk * 128:(kk + 1) * 128], c, f"mst{s2}")
                          if pend[0] is not None:
                              pend[0]()
                          pend[0] = epi
                  if pend[0] is not None:
                      pend[0]()
                      pend[0] = None

                  _stop(P, "b2")
                  EB = sb("EB", [128, 3200], F32)
                  qh = sb("qh", [128, 64, 128], BF16)
                  kh = sb("kh", [128, 64, 128], BF16)
                  vh = sb("vh", [128, 64, 128], BF16)
                  Es = [sb(f"Es{i}", [128, 640], F32) for i in range(3)]
                  Pt = [sb(f"Pt{i}", [128, 640], BF16) for i in range(3)]
                  rinv = [sb(f"rinv{i}", [128, 128], F32) for i in range(3)]
                  ob = [sb(f"ob{i}", [128, 128], BF16) for i in range(3)]
                  pSn = [(pS[0], "pS0"), (pS[1], "pS1"), (pN[0], "pN0")]
                  pOn = [(pK[0], "pK0"), (pK[1], "pK1"), (pN[1], "pN1")]

                  def kbv(pr):
                      kb = min(max(pr - 2, 0), 59)
                      v = 0 if pr == 0 else 1 if pr == 1 else 3 if pr == 62 else 4 if pr == 63 else 2
                      return kb, v

                  si = 0
                  npend = [None]
                  for hb in range(2):
                      ncnt = {}
                      P.op("sp", k.dma("sp", EB[:, :], D["nab"][hb]), writes=["EB"], dsem="EB")
                      P.op("act", k.act(EB[:, :], EB[:, :], AF.Exp), writes=["EB"])
                      P.op("sp", k.dma("sp", qh[:], D["qBT"][hb].rearrange("c p t -> p c t")), reads=[f"d_fm{4 + 2 * hb}"], writes=["qh"], dsem="qh")
                      P.op("sp", k.dma("sp", kh[:], D["kBT"][hb].rearrange("c p t -> p c t")), reads=[f"d_fm{5 + 2 * hb}"], writes=["kh"], dsem="kh")
                      P.op("sp", k.dma("sp", vh[:], D["vB"][:, :, hb * 128:(hb + 1) * 128].rearrange("c p d -> p c d")), writes=["vh"], dsem="vh")
                      for pr in range(64):
                          s2 = si % 3
                          si += 1
                          kb, v = kbv(pr)
                          (pSb, rS), (pOb, rO) = pSn[s2], pOn[s2]
                          for kt in range(5):
                              dst = pSb[:, kt * 128:(kt + 1) * 128] if kt < 4 else pOb[:, 256:384]
                              P.op("pe", k.mm(dst, kh[:, kb + kt, :], qh[:, pr, :]), reads=["kh", "qh"], writes=[rS if kt < 4 else rO])
                          P.op("act", k.act(Es[s2][:, 0:512], pSb[:, 0:512], AF.Exp, scale=SCALE_B), reads=[rS], writes=[f"Es{s2}"])
                          P.op("act", k.act(Es[s2][:, 512:640], pOb[:, 256:384], AF.Exp, scale=SCALE_B), reads=[rO], writes=[f"Es{s2}"])
                          P.op("dve", k.tt("dve", Pt[s2][:], Es[s2][:], EB[:, v * 640:(v + 1) * 640], ALU.mult), reads=[f"Es{s2}", "EB"], writes=[f"Pt{s2}"])

                          def back(s2=s2, kb=kb, pr=pr, hb=hb, pOb=pOb, rO=rO, ncnt=ncnt):
                              for kt in range(5):
                                  P.op("pe", k.mm(pOb[:, 0:128], vh[:, kb + kt, :], Pt[s2][:, kt * 128:(kt + 1) * 128], kt == 0, kt == 4),
                                       reads=["vh", f"Pt{s2}"], writes=[rO])
                              for kt in range(5):
                                  P.op("pe", k.mm(pOb[:, 128:256], ones_b[:], Pt[s2][:, kt * 128:(kt + 1) * 128], kt == 0, kt == 4),
                                       reads=["ones_b", f"Pt{s2}"], writes=[rO])
                              P.op("dve", k.rcp(rinv[s2][:], pOb[:, 128:256]), reads=[rO], writes=[f"rinv{s2}"])
                              P.op("dve", k.tt("dve", ob[s2][:], pOb[:, 0:128], rinv[s2][:], ALU.mult), reads=[rO, f"rinv{s2}"], writes=[f"ob{s2}"])
                              stage_put(NS, f"NS", 1, 256 + hb * 128, ncnt, lambda kk, s2=s2: ob[s2][:, :], pr, f"ob{s2}")
                          if npend[0] is not None:
                              npend[0]()
                          npend[0] = back
                      if npend[0] is not None:
                          npend[0]()
                          npend[0] = None
                  P.emit()

        if mode == "fused" and not Prog.stopped:
            ccs = nc.alloc_semaphore("cc_sem")
            with nc.Block() as blk:
                def _cc(g):
                    g.collective_compute("ReduceScatter", ALU.add, replica_groups=[[0, 1, 2, 3], [4, 5, 6, 7]],
                                         ins=[D["mixs"].rearrange("j r f t -> (j r f) t").opt()], outs=[D["mixr"].opt()]).then_inc(ccs, 1)
                    g.wait_ge(ccs, 1)
                blk.gpsimd(_cc)
                blk.sync(lambda e: e.wait_ge(ccs, 1))
                blk.tensor(lambda e: e.wait_ge(ccs, 1))
                blk.vector(lambda e: e.wait_ge(ccs, 1))
                blk.scalar(lambda e: e.wait_ge(ccs, 1))

        if do2:
            with contextlib.ExitStack() as st:
                sb = lambda name, shape, dty: st.enter_context(nc.sbuf_tensor(name, shape, dty))
                psum = lambda name, dty=F32, n=512: st.enter_context(nc.psum_tensor(name, [128, n], dty))
                P = Prog(nc, "c")
                if not do1:
                    consts(P)
                tpb = [psum("tp0c"), psum("tp1c")]
                pw = [psum(f"pw{i}") for i in range(4)]
                pmisc = psum("pmisc2")
                Wo = sb("Wo", [128, 16, 2048], BF16)
                P.op("pool", k.dma("pool", Wo[:], D["wout"].rearrange("(k p) n -> p k n", p=128)), writes=["Wo"], dsem="Wo")
                _stop(P, "cX")
                if do1:
                    ga1_bc, Gc, shc = ga1_bcP, Gc2P, shc2P
                else:
                    modrow = sb("modrow2", [1, 2048], F32)
                    mrun = mk_modrows(P, sb, pmisc, "2", 2048)
                    g2c = sb("g2c", [128, 16], F32)
                    P.op("sp", k.dma("sp", g2c[:], D["g2col"]), writes=["g2c"], dsem="g2c")
                    ga1_bc = sb("ga1_bc", [128, 2048], F32)
                    Gc = sb("Gc2", [128, 16], F32)
                    shc = sb("shc2", [128, 16], F32)

                    def bcast(dst, res):
                        for nb in range(4):
                            P.op("pe", k.mm(pw[nb][:, 0:512], ones_f[0:1, :], modrow[0:1, nb * 512:(nb + 1) * 512]), reads=["modrow", "ones_f"], writes=[f"pw{nb}"])
                            P.op("dve", k.cp("dve", dst[:, nb * 512:(nb + 1) * 512], pw[nb][:, 0:512]), reads=[f"pw{nb}"], writes=[res])

                    mrun(D["wada2"], D["bada2"], 0, 2048, modrow)
                    bcast(ga1_bc, "ga1_bc")
                    _stop(P, "cY")
                    mrun(D["wada2"], D["bada2"], 2048, 2048, modrow)
                    row2col(P, pmisc, modrow, 0, 272, "pmisc")
                    P.op("dve", k.cp("dve", shc[:], pmisc[:, 272:288]), reads=["pmisc"], writes=["shc"])
                    _stop(P, "cZ1")
                    mrun(D["wada2"], D["bada2"], 4096, 2048, modrow)
                    _stop(P, "cZ1b")
                    row2col(P, pmisc, modrow, 0, 256, "pmisc")
                    _stop(P, "cZ1c")
                    P.op("dve", k.ts("dve", Gc[:], pmisc[:, 256:272], 1.0, None, ALU.add), reads=["pmisc"], writes=["Gc0"])
                    _stop(P, "cZ1d")
                    P.op("dve", k.tt("dve", Gc[:], Gc[:], g2c[:], ALU.mult), reads=["Gc0", "g2c", "shc"], writes=["Gc"])
                    _stop(P, "cZ2")
                    mrun(D["wada2"], D["bada2"], 6144, 2048, modrow)
                    bcast(ga2_bc, "ga2_bc")
                _stop(P, "c0")
                mixT = [sb(f"mixT{i}", [128, 16, 128], BF16) for i in range(2)]
                xts = [sb(f"x2t{i}", [128, 2048], F32) for i in range(2)]
                x1s = [sb(f"x1s{i}", [128, 2048], F32) for i in range(2)]
                xs = sb("xs2", [128, 2048], F32)
                junk = sb("junk3", [128, 2048], BF16)
                ss, sd, rstd = sb("ss3", [128, 1], F32), sb("sd3", [128, 1], F32), sb("rstd3", [128, 1], F32)
                h2st = [sb(f"h2st{i}", [128, 16, 128], BF16) for i in range(2)]
                mixr3 = D["mixr"].rearrange("(k p) t -> p k t", p=128)

                def tile_info(t):
                    if t < 16:
                        return 128, slice(1 + t * 128, 1 + (t + 1) * 128), slice(t * 128, (t + 1) * 128)
                    return 2, slice(0, 2050, 2049), slice(2048, 2050)

                def loads2(t):
                    M, cs, rs = tile_info(t)
                    b = t % 2
                    if t < 16:
                        P.op("sp", k.dma("sp", mixT[b][:, :, 0:M], mixr3[:, :, cs]), writes=[f"mixT{b}"], dsem=f"mixT{b}")
                    else:
                        P.op("sp", k.dma("sp", mixT[b][:, :, 0:1], mixr3[:, :, 0:1], slow=True), writes=[f"mixT{b}"], dsem=f"mixT{b}")
                        P.op("sp", k.dma("sp", mixT[b][:, :, 1:2], mixr3[:, :, 2049:2050], slow=True), writes=[f"mixT{b}x"], dsem=f"mixT{b}")
                    P.op("sp", k.dma("sp", xts[b][0:M, :], D["xtok"][rs, :]), writes=[f"x2t{b}"], dsem=f"x2t{b}")

                loads2(0)
                for t in range(17):
                    if t == 1:
                        _stop(P, "c1")
                    if t == 16:
                        _stop(P, "c16")
                    if t + 1 < 17:
                        loads2(t + 1)
                    M, cs, rs = tile_info(t)
                    b = t % 2
                    for nb in range(4):
                        for fc in range(16):
                            P.op("pe", k.mm(pw[nb][0:M, 0:512], mixT[b][:, fc, 0:M], Wo[:, fc, nb * 512:(nb + 1) * 512], fc == 0, fc == 15),
                                 reads=[f"mixT{b}", f"mixT{b}x", "Wo"], writes=[f"pw{nb}"])
                        P.op("dve", k.tt("dve", x1s[b][0:M, nb * 512:(nb + 1) * 512], pw[nb][0:M, 0:512], ga1_bc[0:M, nb * 512:(nb + 1) * 512], ALU.mult),
                             reads=[f"pw{nb}", "ga1_bc"], writes=[f"x1s{b}"])
                    P.op("dve", k.tt("dve", x1s[b][0:M, :], x1s[b][0:M, :], xts[b][0:M, :], ALU.add), reads=[f"x2t{b}"], writes=[f"x1s{b}"])
                    if t < 16:
                        P.op("sp", k.dma("sp", D["x1"][rs, :], x1s[b][0:M, :]), reads=[f"x1s{b}"], writes=["d_x1"], dsem=f"x1s{b}")
                    norm_T(P, x1s[b], xs, M, ss, sd, rstd, junk, tpb, Gc, shc,
                           lambda kk, b=b, M=M: h2st[b][:, kk, 0:M], f"x1s{b}", (lambda kk, b=b: f"h2st{b}_{kk}"), t)
                    if t < 16:
                        P.op("sp", k.dma("sp", D["h2T"][:, :, cs], h2st[b][:, :, 0:M]), reads=[f"h2st{b}_{kk_}" for kk_ in range(16)], writes=["d_h2T"], dsem=f"h2st{b}")
                    else:
                        P.op("sp", k.dma("sp", D["h2T"][:, :, 0:1], h2st[b][:, :, 0:1], slow=True), reads=[f"h2st{b}_{kk_}" for kk_ in range(16)], writes=["d_h2T"], dsem=f"h2st{b}")
                        P.op("sp", k.dma("sp", D["h2T"][:, :, 2049:2050], h2st[b][:, :, 1:2], slow=True), reads=[f"h2st{b}_{kk_}" for kk_ in range(16)], writes=["d_h2Tx"], dsem=f"h2st{b}")
                P.emit()

            _stop(P, "c")
            with contextlib.ExitStack() as st:
                sb = lambda name, shape, dty: st.enter_context(nc.sbuf_tensor(name, shape, dty))
                psum = lambda name, dty=F32, n=512: st.enter_context(nc.psum_tensor(name, [128, n], dty))
                P = Prog(nc, "d")
                pU = [psum("pU0"), psum("pU1")]
                pG = [psum("pG0"), psum("pG1")]
                pX = psum("pX")
                pE = [psum(f"pE{i}") for i in range(3)]
                AT = sb("AT", [128, 44, 1024], BF16)
                arena = sb("arena", [128, 22528], BF16)
                h2blk = arena[:, 0:16 * 1026].rearrange("p (k t) -> p k t", k=16)
                wdh = [arena[:, i * 11264:(i + 1) * 11264].rearrange("p (f c) -> p f c", f=22) for i in range(2)]
                wug = [sb(f"wug{i}", [128, 16, 256], BF16) for i in range(2)]
                gsbs = [sb(f"gsb{i}", [128, 1026], F32) for i in range(2)]
                accs = [sb(f"acc{i}", [128, 1024], F32) for i in range(2)]
                cw = sb("cw", [128, 44, 3], F32)
                cb = sb("cb", [128, 44], F32)
                flg = sb("flg", [128, 2], F32)
                x1p = [sb(f"x1p{i}", [128, 512], F32) for i in range(3)]
                zst = [sb(f"zst{i}", [128, 512], F32) for i in range(3)]
                P.op("sp", k.dma("sp", cw[:], D["convw"]), writes=["cw"], dsem="cw")
                P.op("sp", k.dma("sp", cb[:], D["convb"]), writes=["cw"], dsem="cb")
                P.op("sp", k.dma("sp", flg[:], D["flags"]), writes=["cw"], dsem="flg")
                wi = 0
                for tbk in range(2):
                    P.op("sp", k.dma("sp", h2blk, D["h2T"][:, :, tbk * 1024:tbk * 1024 + 1026]), reads=["d_h2T", "d_h2Tx"], writes=["arena", "wdh0", "wdh1"], dsem="h2blk")
                    for ft in range(44):
                        if ft == 1 and tbk == 0:
                            _stop(P, "d0")
                        b = ft % 2
                        P.op("pool", k.dma("pool", wug[b][:], D["wup"][ft]), writes=[f"wug{b}"], dsem=f"wug{b}")
                        for sbk in range(2):
                            cols = slice(1 + sbk * 512, 1 + (sbk + 1) * 512)
                            for kk in range(16):
                                P.op("pe", k.mm(pG[sbk][:, 0:512], wug[b][:, kk, 128:256], h2blk[:, kk, cols], kk == 0, kk == 15),
                                     reads=[f"wug{b}", "arena"], writes=[f"pG{sbk}"])
                        for kk in range(16):
                            P.op("pe", k.mm(pX[:, 0:2], wug[b][:, kk, 128:256], h2blk[:, kk, 0:1026:1025], kk == 0, kk == 15),
                                 reads=[f"wug{b}", "arena"], writes=["pX"])
                        for sbk in range(2):
                            cols = slice(1 + sbk * 512, 1 + (sbk + 1) * 512)
                            for kk in range(16):
                                P.op("pe", k.mm(pU[sbk][:, 0:512], wug[b][:, kk, 0:128], h2blk[:, kk, cols], kk == 0, kk == 15),
                                     reads=[f"wug{b}", "arena"], writes=[f"pU{sbk}"])
                        gs, ac = gsbs[b], accs[b]
                        P.op("act", k.act(gs[:, 1:513], pG[0][:, 0:512], AF.Copy), reads=["pG0"], writes=[f"gsb{b}"])
                        P.op("act", k.act(gs[:, 513:1025], pG[1][:, 0:512], AF.Copy), reads=["pG1"], writes=[f"gsb{b}"])
                        if tbk == 0:
                            P.op("act", k.act(gs[:, 0:1], pX[:, 0:1], AF.Copy, scale=flg[:, 0:1]), reads=["pX", "cw"], writes=[f"gsb{b}"])
                            P.op("act", k.act(gs[:, 1025:1026], pX[:, 1:2], AF.Copy), reads=["pX"], writes=[f"gsb{b}"])
                        else:
                            P.op("act", k.act(gs[:, 0:1], pX[:, 0:1], AF.Copy), reads=["pX"], writes=[f"gsb{b}"])
                            P.op("act", k.act(gs[:, 1025:1026], pX[:, 1:2], AF.Copy, scale=flg[:, 1:2]), reads=["pX", "cw"], writes=[f"gsb{b}"])
                        P.op("dve", k.ts("dve", ac[:], gs[:, 1:1025], cw[:, ft, 1:2], cb[:, ft:ft + 1], ALU.mult, ALU.add), reads=[f"gsb{b}", "cw"], writes=[f"acc{b}"])
                        P.op("dve", k.stt("dve", ac[:], gs[:, 0:1024], cw[:, ft, 0:1], ac[:], ALU.mult, ALU.add), reads=[f"gsb{b}", "cw"], writes=[f"acc{b}"])
                        P.op("dve", k.stt("dve", ac[:], gs[:, 2:1026], cw[:, ft, 2:3], ac[:], ALU.mult, ALU.add), reads=[f"gsb{b}", "cw"], writes=[f"acc{b}"])
                        P.op("act", k.act(ac[:], ac[:], AF.Gelu), writes=[f"acc{b}"])
                        P.op("dve", k.tt("dve", AT[:, ft, 0:512], pU[0][:, 0:512], ac[:, 0:512], ALU.mult), reads=[f"acc{b}", "pU0"], writes=["AT"])
                        P.op("dve", k.tt("dve", AT[:, ft, 512:1024], pU[1][:, 0:512], ac[:, 512:1024], ALU.mult), reads=[f"acc{b}", "pU1"], writes=["AT"])
                    if tbk == 0:
                        _stop(P, "d1")
                    accs_ps = [(pU[0], "pU0"), (pU[1], "pU1"), (pG[0], "pG0"), (pG[1], "pG1"), (pX, "pX"), (pE[0], "pE0"), (pE[1], "pE1"), (pE[2], "pE2")]
                    for nb in range(4):
                        for hf in range(2):
                            wb = wi % 2
                            wi += 1
                            P.op("pool", k.dma("pool", wdh[wb], D["wdn"][nb, hf]), writes=["arena", f"wdh{wb}"], dsem=f"wdh{wb}")
                            for tt in range(8):
                                for f in range(22):
                                    ft = hf * 22 + f
                                    P.op("pe", k.mm(accs_ps[tt][0][:, 0:512], AT[:, ft, tt * 128:(tt + 1) * 128], wdh[wb][:, f, :], ft == 0, ft == 43),
                                         reads=["AT", f"wdh{wb}"], writes=[accs_ps[tt][1]])
                        for tt in range(8):
                            row0 = tbk * 1024 + tt * 128
                            xb_ = (nb * 8 + tt) % 3
                            zb = (nb * 8 + tt) % 3
                            cs_ = slice(nb * 512, (nb + 1) * 512)
                            P.op("sp", k.dma("sp", x1p[xb_][:], D["x1"][row0:row0 + 128, cs_]), writes=[f"x1p{xb_}"], dsem=f"x1p{xb_}")
                            P.op("dve", k.tt("dve", zst[zb][:], accs_ps[tt][0][:, 0:512], ga2_bc[:, cs_], ALU.mult), reads=[accs_ps[tt][1]], writes=[f"zst{zb}"])
                            P.op("dve", k.tt("dve", zst[zb][:], zst[zb][:], x1p[xb_][:], ALU.add), reads=[f"x1p{xb_}"], writes=[f"zst{zb}"])
                            P.op("sp", k.dma("sp", D["z"][row0:row0 + 128, cs_], zst[zb][:]), reads=[f"zst{zb}"], writes=["d_z"], dsem=f"zst{zb}")
                P.emit()

            _stop(P, "d")
            with contextlib.ExitStack() as st:
                sb = lambda name, shape, dty: st.enter_context(nc.sbuf_tensor(name, shape, dty))
                P = Prog(nc, "e")
                gf = sb("gf", [128, 2048], F32)
                P.op("sp", k.dma("sp", gf[:], D["gfin"]), writes=["gf"], dsem="gf")
                zt = [sb(f"zt{i}", [128, 2048], F32) for i in range(2)]
                ot = [sb(f"ot{i}", [128, 2048], F32) for i in range(2)]
                junk = sb("junk4", [128, 2048], BF16)
                ss, sd, rstd = sb("ss4", [128, 1], F32), sb("sd4", [128, 1], F32), sb("rstd4", [128, 1], F32)
                P.op("sp", k.dma("sp", zt[0][:], D["z"][0:128, :]), writes=["zt0"], dsem="zt0")
                for t in range(16):
                    b = t % 2
                    if t + 1 < 16:
                        P.op("sp", k.dma("sp", zt[1 - b][:], D["z"][(t + 1) * 128:(t + 2) * 128, :]), writes=[f"zt{1 - b}"], dsem=f"zt{1 - b}")
                    P.op("act", k.act(junk[:], zt[b][:], AF.Square, accum=ss[:, 0:1]), reads=[f"zt{b}"], writes=["junk", "ss"])
                    P.op("act", k.act(sd[:], ss[:], AF.Sqrt, bias=EPS, scale=1.0 / 2048), reads=["ss"], writes=["sd"])
                    P.op("dve", k.rcp(rstd[:], sd[:]), reads=["sd"], writes=["rstd"])
                    P.op("dve", k.stt("dve", ot[b][:], zt[b][:], rstd[:, 0:1], gf[:], ALU.mult, ALU.mult), reads=[f"zt{b}", "rstd", "gf"], writes=[f"ot{b}"])
                    P.op("sp", k.dma("sp", D["out"][t * 128:(t + 1) * 128, :], ot[b][:]), reads=[f"ot{b}"], dsem=f"ot{b}")
                P.emit()
    return nc


def _col(v):
    return np.ascontiguousarray(v.reshape(16, 128).T)


def _nab_tables(rpb_l, heads):
    NEG = np.float32(-30000.0)
    out = np.full((len(heads), 128, 5, 5, 128), NEG, np.float32)
    p = np.arange(128)
    q = np.arange(128)
    for v, pr in enumerate((0, 1, 10, 62, 63)):
        kb = min(max(pr - 2, 0), 59)
        r = 2 * pr + q // 64
        c = q % 64
        rs = np.clip(r - 4, 0, 120)
        cs = np.clip(c - 8, 0, 48)
        for kt in range(5):
            krow = 2 * (kb + kt) + p // 64
            kc = p % 64
            ok = ((krow[:, None] >= rs[None, :]) & (krow[:, None] < rs[None, :] + 8)
                  & (kc[:, None] >= cs[None, :]) & (kc[:, None] < cs[None, :] + 16))
            dr = np.clip(krow[:, None] - r[None, :] + 7, 0, 14)
            dc = np.clip(kc[:, None] - c[None, :] + 15, 0, 30)
            for hi, h in enumerate(heads):
                vals = rpb_l[h][dr, dc]
                out[hi, :, v, kt, :] = np.where(ok, vals, NEG)
    return out.reshape(len(heads), 128, 3200)


def _inputs_h1(inp, j):
    b, hq = divmod(j, 4)
    w_in = inp["w_in"][0]
    A = 1024
    qa = lambda h: slice(h * 256, (h + 1) * 256)
    cols = []
    cols += list(range(0 * A + hq * 256, 0 * A + (hq + 1) * 256))
    cols += list(range(1 * A + hq * 256, 1 * A + (hq + 1) * 256))
    gb = 4 * A + 16
    for hl in range(2):
        h = 2 * hq + hl
        cols += list(range(gb + h * 128, gb + (h + 1) * 128))
        cols += list(range(gb + 1024 + h * 128, gb + 1024 + (h + 1) * 128))
    cols += list(range(2 * A + hq * 256, 2 * A + (hq + 1) * 256))
    cols += list(range(3 * A + hq * 256, 3 * A + (hq + 1) * 256))
    cols += list(range(1 * A + hq * 256, 1 * A + (hq + 1) * 256))
    for hl in range(2):
        h = 2 * hq + hl
        cols += list(range(gb + 2048 + h * 128, gb + 2048 + (h + 1) * 128))
    gcols = [4 * A + g * 4 + hq for g in range(4)]
    cols += gcols
    tri = np.triu(np.ones((128, 128), np.float32))
    return {
        "ccol": _col(inp["c"][b]),
        "ident": np.eye(128, dtype=np.float32),
        "xb": np.ascontiguousarray(inp["x"][b]),
        "wada1": np.ascontiguousarray(inp["w_ada"][0][:, 0:4096]),
        "bada1": np.ascontiguousarray(inp["b_ada"][0][None, 0:4096]),
        "g1col": _col(inp["g_norm1"][0]),
        "win": np.ascontiguousarray(w_in[:, cols]),
        "bg": np.ascontiguousarray(np.broadcast_to(inp["b_gates"][0][[g * 4 + hq for g in range(4)]][None, :], (128, 4))),
        "gha": np.ascontiguousarray(np.broadcast_to(inp["g_head_a"][0][hq * 256:(hq + 1) * 256][None, :], (128, 256))),
        "nab": _nab_tables(inp["rpb"][0], [2 * hq, 2 * hq + 1]),
        "trif": tri,
        "trib": np.ascontiguousarray(tri.T),
        "rmask": np.ascontiguousarray(np.broadcast_to(np.eye(4, dtype=np.float32)[hq][None, :], (128, 4))),
    }


def _inputs_h2(inp, j, mixr=None):
    b, q = divmod(j, 4)
    t0 = q * 2048
    x = inp["x"][b]
    xtok = np.zeros((2050, 2048), np.float32)
    xtok[0:2048] = x[t0:t0 + 2048]
    if t0 > 0:
        xtok[2048] = x[t0 - 1]
    if t0 + 2048 < NTOK:
        xtok[2049] = x[t0 + 2048]
    flags = np.zeros((128, 2), np.float32)
    flags[:, 0] = 1.0 if t0 > 0 else 0.0
    flags[:, 1] = 1.0 if t0 + 2048 < NTOK else 0.0
    rows = []
    for r in range(4):
        rows += list(range(r * 256, (r + 1) * 256))
        rows += list(range(1024 + 2 * r * 128, 1024 + (2 * r + 2) * 128))
    w_up = inp["w_up"][0]
    wup = np.empty((44, 128, 16, 256), np.float32)
    wu = w_up[:, 0:5632].reshape(16, 128, 44, 128)
    wg = w_up[:, 5632:].reshape(16, 128, 44, 128)
    wup[:, :, :, 0:128] = wu.transpose(2, 1, 0, 3)
    wup[:, :, :, 128:256] = wg.transpose(2, 1, 0, 3)
    wd = inp["w_down"][0].reshape(2, 22, 128, 4, 512)
    wdn = np.ascontiguousarray(wd.transpose(3, 0, 2, 1, 4))
    d = {
        "ccol": _col(inp["c"][b]),
        "ident": np.eye(128, dtype=np.float32),
        "xtok": xtok,
        "flags": flags,
        "wada2": np.ascontiguousarray(inp["w_ada"][0][:, 4096:]),
        "bada2": np.ascontiguousarray(inp["b_ada"][0][None, 4096:]),
        "wout": np.ascontiguousarray(inp["w_out"][0][rows, :]),
        "g2col": _col(inp["g_norm2"][0]),
        "wup": wup,
        "convw": np.ascontiguousarray(inp["conv_w"][0].reshape(3, 44, 128).transpose(2, 1, 0)),
        "convb": np.ascontiguousarray(inp["conv_b"][0].reshape(44, 128).T),
        "wdn": wdn,
        "gfin": np.ascontiguousarray(np.broadcast_to(inp["g_final"][None, :], (128, 2048))),
    }
    if mixr is not None:
        d["mixr"] = mixr
    return d


MODE = "fused"
_NC_CACHE = {}


def _get_nc(mode):
    if mode not in _NC_CACHE:
        _NC_CACHE[mode] = build(mode)
    return _NC_CACHE[mode]


def run_h1(inp, cores=range(8)):
    nc = _get_nc("h1")
    cores = list(cores)
    maps = [_inputs_h1(inp, j) for j in cores]
    res = run_bass_kernel_spmd(nc, maps, core_ids=list(range(len(cores))))
    return [np.asarray(r["mixs"]) for r in res.results]


def exchange(mixs):
    out = []
    for j in range(8):
        b, q = divmod(j, 4)
        out.append(np.ascontiguousarray(np.concatenate([mixs[b * 4 + r][q, r] for r in range(4)], axis=0)))
    return out


def run_h2(inp, mixr):
    nc = _get_nc("h2")
    maps = [_inputs_h2(inp, j, mixr[j]) for j in range(8)]
    res = run_bass_kernel_spmd(nc, maps, core_ids=list(range(8)))
    return [np.asarray(r["out"]) for r in res.results]


def kernel(**inputs):
    inp = {k_: np.asarray(v) for k_, v in inputs.items()}
    if MODE == "fused":
        nc = _get_nc("fused")
        maps = []
        for j in range(8):
            d = _inputs_h1(inp, j)
            d.update(_inputs_h2(inp, j))
            maps.append(d)
        res = run_bass_kernel_spmd(nc, maps, core_ids=list(range(8)))
        outs = [np.asarray(r["out"]) for r in res.results]
    else:
        outs = run_h2(inp, exchange(run_h1(inp)))
    out = np.stack(outs, 0).reshape(2, NTOK, 2048)
    return out.astype(np.float32)
```

```python
import contextlib
import numpy as np
import ml_dtypes
import concourse.bass as bass
import concourse.mybir as mybir
from concourse.bass_utils import run_bass_kernel_spmd

F32 = mybir.dt.float32
BF16 = mybir.dt.bfloat16
AF = mybir.ActivationFunctionType
ALU = mybir.AluOpType
AX = mybir.AxisListType
EPS = 1e-6
ENGS = ("pe", "act", "dve", "pool", "sp")


class _Op:
    __slots__ = ("eng", "fn", "deps", "ms", "val", "dsem", "idx")

    def __init__(self, eng, fn, dsem, idx):
        self.eng, self.fn, self.dsem, self.idx = eng, fn, dsem, idx
        self.deps, self.ms, self.val = [], False, None


class Prog:
    stopped = False
    pool = {}

    def __init__(self, nc, tag):
        self.nc, self.tag = nc, tag
        self.ops, self.lastw, self.readers, self.dsems = [], {}, {}, {}

    def op(self, eng, fn, reads=(), writes=(), dsem=None):
        if Prog.stopped:
            return None
        o = _Op(eng, fn, dsem, len(self.ops))
        deps = {}
        for r in reads:
            w = self.lastw.get(r)
            if w is not None:
                deps[w.idx] = w
        for r in writes:
            w = self.lastw.get(r)
            if w is not None:
                deps[w.idx] = w
            for rd in self.readers.get(r, ()):
                deps[rd.idx] = rd
        for d in deps.values():
            if d.eng == "pe" and eng == "pe" and d.dsem is None and dsem is None:
                continue
            o.deps.append(d)
            d.ms = True
        for r in writes:
            self.lastw[r] = o
            self.readers[r] = []
        for r in reads:
            if r not in writes:
                self.readers.setdefault(r, []).append(o)
        if dsem is not None:
            self.dsems.setdefault(dsem, 0)
        self.ops.append(o)
        return o

    def emit(self):
        if Prog.stopped:
            return
        nc = self.nc
        G = Prog.pool.setdefault(id(nc), {"es": {}, "ec": {e: 0 for e in ENGS}, "slots": [], "sc": []})
        for e in ENGS:
            if e not in G["es"]:
                G["es"][e] = nc.alloc_semaphore(f"s_{e}")
        slot = {}
        for i, kname in enumerate(self.dsems):
            if i >= len(G["slots"]):
                G["slots"].append(nc.alloc_semaphore(f"d_{i}"))
                G["sc"].append(0)
            slot[kname] = i
        per = {e: [o for o in self.ops if o.eng == e] for e in ENGS}
        for e in ENGS:
            if per[e]:
                per[e][-1].ms = True
        cnt = dict(G["ec"])
        dcnt = {kname: G["sc"][i] for kname, i in slot.items()}
        for o in self.ops:
            if o.dsem is not None:
                dcnt[o.dsem] += 16
                o.val = dcnt[o.dsem]
            elif o.ms:
                cnt[o.eng] += 1
                o.val = cnt[o.eng]
        esem = G["es"]
        dsem = {kname: G["slots"][i] for kname, i in slot.items()}

        def run(e, engobj):
            waited = {}
            for o in per[e]:
                for d in o.deps:
                    if d.dsem is not None:
                        key, sem = ("d", d.dsem), dsem[d.dsem]
                    else:
                        key, sem = ("e", d.eng), esem[d.eng]
                    if waited.get(key, 0) < d.val:
                        engobj.wait_ge(sem, d.val)
                        waited[key] = d.val
                ins = o.fn()
                if o.dsem is not None:
                    ins.then_inc(dsem[o.dsem], 16)
                elif o.ms:
                    ins.then_inc(esem[e], 1)
            for kname, v in dcnt.items():
                if v > G["sc"][slot[kname]] and waited.get(("d", kname), 0) < v:
                    engobj.wait_ge(dsem[kname], v)
            for e2 in ENGS:
                if cnt[e2] > G["ec"][e2] and waited.get(("e", e2), 0) < cnt[e2]:
                    engobj.wait_ge(esem[e2], cnt[e2])

        with nc.Block() as block:
            block.tensor(lambda eng: run("pe", eng))
            block.scalar(lambda eng: run("act", eng))
            block.vector(lambda eng: run("dve", eng))
            block.gpsimd(lambda eng: run("pool", eng))
            block.sync(lambda eng: run("sp", eng))
        for kname, i in slot.items():
            G["sc"][i] = dcnt[kname]
        G["ec"] = cnt
        nc.all_engine_barrier()


class K:
    def __init__(self, nc):
        self.nc = nc
        self.v = {"dve": nc.vector, "pool": nc.gpsimd}
        self.q = {"sp": nc.sync, "pool": nc.gpsimd, "act": nc.scalar}

    def mm(self, out, lhsT, rhs, start=True, stop=True):
        return lambda: self.nc.tensor.matmul(out, lhsT, rhs, start=start, stop=stop)

    def tr(self, out, in_, ident):
        return lambda: self.nc.tensor.transpose(out, in_, ident)

    def act(self, out, in_, func, bias=None, scale=None, accum=None):
        kw = {}
        if bias is not None:
            kw["bias"] = bias
        if scale is not None:
            kw["scale"] = scale
        if accum is not None:
            kw["accum_out"] = accum
        return lambda: self.nc.scalar.activation(out, in_, func, **kw)

    def dma(self, q, out, in_, slow=False):
        if slow:
            return lambda: self.q[q].dma_start(out=out, in_=in_, allow_slow_non_contiguous=True)
        return lambda: self.q[q].dma_start(out=out, in_=in_)

    def ts(self, e, out, in0, s1, s2, op0, op1=None):
        if op1 is None:
            return lambda: self.v[e].tensor_scalar(out, in0, s1, None, op0)
        return lambda: self.v[e].tensor_scalar(out, in0, s1, s2, op0, op1)

    def tt(self, e, out, in0, in1, op):
        return lambda: self.v[e].tensor_tensor(out, in0, in1, op)

    def stt(self, e, out, in0, scalar, in1, op0, op1):
        return lambda: self.v[e].scalar_tensor_tensor(out, in0, scalar, in1, op0, op1)

    def cp(self, e, out, in_):
        return lambda: self.v[e].tensor_copy(out, in_)

    def ms(self, e, ap, c):
        return lambda: self.v[e].memset(ap, c)

    def rcp(self, out, in_):
        return lambda: self.nc.vector.reciprocal(out, in_)

    def rmax(self, out, in_):
        return lambda: self.nc.vector.reduce_max(out, in_, AX.X)


NTOK = 8192
TRAILER = "none"
STOP_AFTER = ""


class _Stop(Exception):
    pass


def _stop(P, tag):
    if STOP_AFTER == tag:
        P.emit()
        Prog.stopped = True
NT = 64
NWIN = 2052
SCALE_B = 128 ** -0.5


def build(mode):
    Prog.stopped = False
    nc = bass.Bass("TRN2", target_bir_lowering=False)
    k = K(nc)
    dt = lambda name, shape, dty, kind: nc.dram_tensor(name, shape, dty, kind=kind).ap()
    IN, OUT, INT = "ExternalInput", "ExternalOutput", "Internal"
    do1 = mode in ("h1", "fused")
    do2 = mode in ("h2", "fused")
    D = {}
    D["ccol"] = dt("ccol", [128, 16], F32, IN)
    D["ident"] = dt("ident", [128, 128], F32, IN)
    if do1:
        D["xb"] = dt("xb", [NTOK, 2048], F32, IN)
        D["wada1"] = dt("wada1", [2048, 4096], F32, IN)
        D["bada1"] = dt("bada1", [1, 4096], F32, IN)
        D["g1col"] = dt("g1col", [128, 16], F32, IN)
        D["win"] = dt("win", [2048, NWIN], F32, IN)
        D["bg"] = dt("bg", [128, 4], F32, IN)
        D["gha"] = dt("gha", [128, 256], F32, IN)
        D["nab"] = dt("nab", [2, 128, 3200], F32, IN)
        D["trif"] = dt("trif", [128, 128], F32, IN)
        D["trib"] = dt("trib", [128, 128], F32, IN)
        D["qAT"] = dt("qAT_d", [NT, 128, 2, 128], BF16, INT)
        D["kAT"] = dt("kAT_d", [NT, 128, 2, 128], BF16, INT)
        D["qBT"] = dt("qBT_d", [2, NT, 128, 128], BF16, INT)
        D["kBT"] = dt("kBT_d", [2, NT, 128, 128], BF16, INT)
        D["vA"] = dt("vA_d", [NT, 128, 257], BF16, INT)
        D["og"] = dt("og_d", [NT, 128, 256], F32, INT)
        D["ktok"] = dt("ktok_d", [NT, 128, 256], BF16, INT)
        D["vB"] = dt("vB_d", [NT, 128, 256], BF16, INT)
        D["hf"] = dt("hf_d", [NT, 128, 256], F32, INT)
        D["mixs"] = dt("mixs", [4, 4, 512, 2050], BF16, OUT if mode == "h1" else INT)
        D["rmask"] = dt("rmask", [128, 4], F32, IN)
    if do2:
        D["mixr"] = dt("mixr", [2048, 2050], BF16, IN if mode == "h2" else INT)
        D["xtok"] = dt("xtok", [2050, 2048], F32, IN)
        D["flags"] = dt("flags", [128, 2], F32, IN)
        D["wada2"] = dt("wada2", [2048, 8192], F32, IN)
        D["bada2"] = dt("bada2", [1, 8192], F32, IN)
        D["wout"] = dt("wout", [2048, 2048], F32, IN)
        D["g2col"] = dt("g2col", [128, 16], F32, IN)
        D["wup"] = dt("wup", [44, 128, 16, 256], F32, IN)
        D["convw"] = dt("convw", [128, 44, 3], F32, IN)
        D["convb"] = dt("convb", [128, 44], F32, IN)
        D["wdn"] = dt("wdn", [4, 2, 128, 22, 512], F32, IN)
        D["gfin"] = dt("gfin", [128, 2048], F32, IN)
        D["h2T"] = dt("h2T_d", [128, 16, 2050], BF16, INT)
        D["x1"] = dt("x1_d", [2048, 2048], F32, INT)
        D["z"] = dt("z_d", [2048, 2048], F32, INT)
        D["out"] = dt("out", [2048, 2048], F32, OUT)

    with contextlib.ExitStack() as top:
        gsb = lambda name, shape, dty: top.enter_context(nc.sbuf_tensor(name, shape, dty))
        ident_f = gsb("ident_f", [128, 128], F32)
        ident_b = gsb("ident_b", [128, 128], BF16)
        ones_f = gsb("ones_f", [128, 128], F32)
        ones_b = gsb("ones_b", [128, 128], BF16)
        gates_sb = gsb("gates_sb", [128, NT, 4], F32)
        ga2_bc = gsb("ga2_bc", [128, 2048], F32)
        ga1_bcP = gsb("ga1_bcP", [128, 2048], F32)
        Gc2P = gsb("Gc2P", [128, 16], F32)
        shc2P = gsb("shc2P", [128, 16], F32)

        def consts(P):
            P.op("sp", k.dma("sp", ident_f[:], D["ident"]), writes=["ident_f"], dsem="c_idf")
            P.op("pool", k.dma("pool", ident_b[:], D["ident"]), writes=["ident_b"], dsem="c_idb")
            P.op("dve", k.ms("dve", ones_f[:], 1.0), writes=["ones_f"])
            P.op("dve", k.ms("dve", ones_b[:], 1.0), writes=["ones_b"])

        def mk_modrows(P, sb, psb, tag, nmax):
            cc = sb("cc" + tag, [128, 16], F32)
            scb = sb("scb" + tag, [128, 16], BF16)
            badar = sb("badar" + tag, [1, nmax], F32)
            wst = [sb(f"wst{tag}{i}", [128, 16, 256], BF16) for i in range(2)]
            P.op("sp", k.dma("sp", cc[:], D["ccol"]), writes=["cc"], dsem="cc")
            P.op("act", k.act(scb[:], cc[:], AF.Silu), reads=["cc"], writes=["scb"])
            state = {"i": 0}

            def run(wada, bada, c0, ncols, modrow):
                P.op("sp", k.dma("sp", badar[0:1, 0:ncols], bada[:, c0:c0 + ncols]), writes=["badar"], dsem="badar")
                for cb in range(ncols // 256):
                    b = state["i"] % 2
                    state["i"] += 1
                    src = wada[:, c0 + cb * 256:c0 + (cb + 1) * 256].rearrange("(k p) n -> p k n", p=128)
                    P.op("pool", k.dma("pool", wst[b][:], src), writes=[f"wst{b}"], dsem=f"wst{b}")
                    for kk in range(16):
                        P.op("pe", k.mm(psb[0:1, 0:256], scb[:, kk:kk + 1], wst[b][:, kk, :], kk == 0, kk == 15),
                             reads=["scb", f"wst{b}"], writes=["pmisc"])
                    P.op("dve", k.tt("dve", modrow[0:1, cb * 256:(cb + 1) * 256], psb[0:1, 0:256],
                                     badar[0:1, cb * 256:(cb + 1) * 256], ALU.add),
                         reads=["pmisc", "badar"], writes=["modrow"])
            return run

        def row2col(P, psb, modrow, off, col0, res):
            for kk in range(16):
                P.op("pe", k.mm(psb[:, col0 + kk:col0 + kk + 1], modrow[0:1, off + kk * 128:off + (kk + 1) * 128],
                                ones_f[0:1, 0:1]), reads=["modrow", "modrow2", "ones_f"], writes=[res])

        def norm_units(P, xt, xs, M, ss, sd, rstd, junk, tpb, Gc, shc, dst_fn, rx, rdst, tag=""):
            rxs = rx if xs is xt else "xs_shared"

            def u0():
                P.op("act", k.act(junk[0:M, :], xt[0:M, :], AF.Square, accum=ss[0:M, 0:1]), reads=[rx], writes=["junk", "ss" + tag])
                P.op("act", k.act(sd[0:M, 0:1], ss[0:M, 0:1], AF.Sqrt, bias=EPS, scale=1.0 / 2048), reads=["ss" + tag], writes=["sd" + tag])
                P.op("dve", k.rcp(rstd[0:M, 0:1], sd[0:M, 0:1]), reads=["sd" + tag], writes=["rstd" + tag])
                P.op("dve", k.ts("dve", xs[0:M, :], xt[0:M, :], rstd[0:M, 0:1], None, ALU.mult), reads=[rx, "rstd" + tag], writes=[rxs])

            def ug(g):
                def f():
                    bank = tpb[g % 2]
                    for j in range(4):
                        kk = g * 4 + j
                        P.op("pe", k.tr(bank[:, j * 128:j * 128 + M], xs[0:M, kk * 128:(kk + 1) * 128], ident_f[0:M, 0:M]),
                             reads=[rxs, "ident_f"], writes=[f"tp{g % 2}"])
                    for j in range(4):
                        kk = g * 4 + j
                        src = bank[:, j * 128:j * 128 + M]
                        if g % 2 == 0:
                            P.op("act", k.act(dst_fn(kk), src, AF.Identity, bias=shc[:, kk:kk + 1], scale=Gc[:, kk:kk + 1]),
                                 reads=[f"tp{g % 2}", "Gc"], writes=[rdst(kk)])
                        else:
                            P.op("dve", k.ts("dve", dst_fn(kk), src, Gc[:, kk:kk + 1], shc[:, kk:kk + 1], ALU.mult, ALU.add),
                                 reads=[f"tp{g % 2}", "Gc"], writes=[rdst(kk)])
                return f
            return [u0] + [ug(g) for g in range(4)]

        def norm_T(P, xt, xs, M, ss, sd, rstd, junk, tpb, Gc, shc, dst_fn, rx, rdst, ev_i):
            for u in norm_units(P, xt, xs, M, ss, sd, rstd, junk, tpb, Gc, shc, dst_fn, rx, rdst):
                u()

        if do1:
            with contextlib.ExitStack() as st:
                sb = lambda name, shape, dty: st.enter_context(nc.sbuf_tensor(name, shape, dty))
                psum = lambda name, dty=F32, n=512: st.enter_context(nc.psum_tensor(name, [128, n], dty))
                P = Prog(nc, "a")
                consts(P)
                tpb = [psum("tp0"), psum("tp1")]
                fmb = [psum("fm0"), psum("fm1")]
                tmb = [psum("tm0"), psum("tm1")]
                pmisc = psum("pmisc")
                Wb = sb("Wb", [128, 16, NWIN], BF16)
                P.op("pool", k.dma("pool", Wb[:], D["win"].rearrange("(k p) n -> p k n", p=128)), writes=["Wb"], dsem="Wb")
                modrow = sb("modrow", [1, 4096], F32)
                mk_modrows(P, sb, pmisc, "1", 4096)(D["wada1"], D["bada1"], 0, 4096, modrow)
                g1c = sb("g1c", [128, 16], F32)
                P.op("sp", k.dma("sp", g1c[:], D["g1col"]), writes=["g1c"], dsem="g1c")
                row2col(P, pmisc, modrow, 2048, 256, "pmisc")
                row2col(P, pmisc, modrow, 0, 272, "pmisc")
                Gc = sb("Gc", [128, 16], F32)
                shc = sb("shc", [128, 16], F32)
                P.op("dve", k.ts("dve", Gc[:], pmisc[:, 256:272], 1.0, None, ALU.add), reads=["pmisc"], writes=["Gc0"])
                P.op("dve", k.tt("dve", Gc[:], Gc[:], g1c[:], ALU.mult), reads=["Gc0", "g1c"], writes=["Gc"])
                P.op("dve", k.cp("dve", shc[:], pmisc[:, 272:288]), reads=["pmisc"], writes=["Gc"])
                bgs = sb("bgs", [128, 4], F32)
                ghas = sb("ghas", [128, 256], F32)
                P.op("sp", k.dma("sp", ghas[:], D["gha"]), writes=["ghas"], dsem="ghas")

                xts = [sb(f"xt{i}", [128, 2048], F32) for i in range(2)]
                hT = [sb(f"hT{i}", [128, 16, 512], BF16) for i in range(2)]
                junk = sb("junk", [128, 2048], BF16)
                ss = sb("ss", [128, 1], F32)
                sd = sb("sd", [128, 1], F32)
                rstd = sb("rstd", [128, 1], F32)
                fmst = [sb(f"fmst{i}", [128, 512], BF16) for i in range(4)]
                vst = [sb(f"vst{i}", [128, 257], BF16) for i in range(2)]
                ogs = [sb(f"ogs{i}", [128, 256], F32) for i in range(2)]
                kts = [sb(f"kts{i}", [128, 256], BF16) for i in range(2)]
                vbs = [sb(f"vbs{i}", [128, 256], BF16) for i in range(2)]
                for i in range(2):
                    P.op("dve", k.ms("dve", vst[i][:, 256:257], 1.0), writes=[f"vst{i}"])

                def load_x(t):
                    P.op("sp", k.dma("sp", xts[t % 2][:], D["xb"][t * 128:(t + 1) * 128, :]), writes=[f"xt{t % 2}"], dsem=f"xt{t % 2}")

                ssL = [sb(f"ssL{i}", [128, 1], F32) for i in range(2)]
                sdL = [sb(f"sdL{i}", [128, 1], F32) for i in range(2)]
                rstdL = [sb(f"rstdL{i}", [128, 1], F32) for i in range(2)]

                def prep_units(tg):
                    us = []
                    g = tg % 2
                    for ti in range(4):
                        t = tg * 4 + ti
                        xt = xts[t % 2]
                        tu = norm_units(P, xt, xt, 128, ssL[t % 2], sdL[t % 2], rstdL[t % 2], junk, tpb, Gc, shc,
                                        lambda kk, g=g, ti=ti: hT[g][:, kk, ti * 128:(ti + 1) * 128], f"xt{t % 2}",
                                        (lambda kk, g=g, ti=ti: f"hT{g}_{ti}_{kk}"), tag=str(t % 2))
                        us.append(tu[0])
                        us.append(tu[1])
                        u2 = tu[2]
                        us.append((lambda t=t, u2=u2: (load_x(t + 1), u2())) if t + 1 < NT else u2)
                        us.extend(tu[3:])
                    return us

                pending_units = []

                def pop_unit():
                    if pending_units:
                        pending_units.pop(0)()

                fmi = 0
                load_x(0)
                for u in prep_units(0):
                    u()
                for tg in range(16):
                    g = tg % 2
                    pending_units = prep_units(tg + 1) if tg + 1 < 16 else []
                    for fc in range(8):
                        bank = fmb[fc % 2]
                        for kk in range(16):
                            P.op("pe", k.mm(bank[:, 0:512], Wb[:, kk, fc * 128:(fc + 1) * 128], hT[g][:, kk, :], kk == 0, kk == 15),
                                 reads=["Wb"] + [f"hT{g}_{ti_}_{kk}" for ti_ in range(4)], writes=[f"fm{fc % 2}"])
                        fb = fmi % 4
                        fmi += 1
                        scl = 0.0625 if fc in (2, 3) else 1.0
                        if fc % 2 == 0:
                            P.op("act", k.act(fmst[fb][:], bank[:, 0:512], AF.Copy, scale=scl), reads=[f"fm{fc % 2}"], writes=[f"fmst{fb}"])
                        else:
                            P.op("dve", k.ts("dve", fmst[fb][:], bank[:, 0:512], scl, None, ALU.mult), reads=[f"fm{fc % 2}"], writes=[f"fmst{fb}"])
                        src = fmst[fb][:].rearrange("p (c t) -> p c t", c=4)
                        c0 = tg * 4
                        if fc < 4:
                            arr = D["qAT"] if fc < 2 else D["kAT"]
                            dst = arr[c0:c0 + 4, :, fc % 2, :].rearrange("c p t -> p c t")
                        else:
                            arr = D["qBT"] if fc % 2 == 0 else D["kBT"]
                            dst = arr[(fc - 4) // 2, c0:c0 + 4, :, :].rearrange("c p t -> p c t")
                        P.op("sp", k.dma("sp", dst, src), reads=[f"fmst{fb}"], writes=[f"d_fm{fc}"], dsem=f"fmst{fb}")
                        pop_unit()
                    for ti in range(4):
                        t = tg * 4 + ti
                        b = t % 2
                        lhs = lambda kk: hT[g][:, kk, ti * 128:(ti + 1) * 128]
                        for kk in range(16):
                            P.op("pe", k.mm(tmb[0][:, 0:512], lhs(kk), Wb[:, kk, 1024:1536], kk == 0, kk == 15),
                                 reads=["Wb", f"hT{g}_{ti}_{kk}"], writes=["tm0"])
                        pop_unit()
                        for kk in range(16):
                            P.op("pe", k.mm(tmb[1][:, 0:512], lhs(kk), Wb[:, kk, 1536:2048], kk == 0, kk == 15),
                                 reads=["Wb", f"hT{g}_{ti}_{kk}"], writes=["tm1"])
                        pop_unit()
                        for kk in range(16):
                            P.op("pe", k.mm(pmisc[:, 0:4], lhs(kk), Wb[:, kk, 2048:2052], kk == 0, kk == 15),
                                 reads=["Wb", f"hT{g}_{ti}_{kk}"], writes=["pmisc"])
                        pop_unit()
                        P.op("act", k.act(vst[b][:, 0:256], tmb[0][:, 0:256], AF.Copy), reads=["tm0"], writes=[f"vst{b}"])
                        P.op("act", k.act(ogs[b][:], tmb[0][:, 256:512], AF.Sigmoid), reads=["tm0"], writes=[f"ogs{b}"])
                        P.op("pool", k.tt("pool", ogs[b][:], ogs[b][:], ghas[:], ALU.mult), reads=["ghas"], writes=[f"ogs{b}"])
                        P.op("dve", k.ts("dve", kts[b][:], tmb[1][:, 0:256], 0.0625, None, ALU.mult), reads=["tm1"], writes=[f"kts{b}"])
                        P.op("dve", k.cp("dve", vbs[b][:], tmb[1][:, 256:512]), reads=["tm1"], writes=[f"vbs{b}"])
                        P.op("dve", k.cp("dve", gates_sb[:, t, :], pmisc[:, 0:4]), reads=["pmisc"], writes=["gates"])
                        P.op("sp", k.dma("sp", D["vA"][t], vst[b][:]), reads=[f"vst{b}"], writes=["d_vA"], dsem=f"vst{b}")
                        P.op("sp", k.dma("sp", D["og"][t], ogs[b][:]), reads=[f"ogs{b}"], writes=["d_og"], dsem=f"ogs{b}")
                        P.op("sp", k.dma("sp", D["ktok"][t], kts[b][:]), reads=[f"kts{b}"], writes=["d_ktok"], dsem=f"kts{b}")
                        P.op("sp", k.dma("sp", D["vB"][t], vbs[b][:]), reads=[f"vbs{b}"], writes=["d_vB"], dsem=f"vbs{b}")
                    while pending_units:
                        pop_unit()
                P.emit()

            with contextlib.ExitStack() as st:
              if STOP_AFTER != "a":
                  sb = lambda name, shape, dty: st.enter_context(nc.sbuf_tensor(name, shape, dty))
                  psum = lambda name, dty=F32, n=512: st.enter_context(nc.psum_tensor(name, [128, n], dty))
                  P = Prog(nc, "b")
                  pA = psum("pA")
                  pS = [psum("pS0"), psum("pS1")]
                  pN = [psum("pN0"), psum("pN1")]
                  pK = [psum("pK0"), psum("pK1")]
                  pT = psum("pT", BF16, 1024)
                  if STOP_AFTER == "bY":
                      tmpo = sb("tmpo", [128, 64], F32)
                      P.op("pe", k.mm(pA[:, 0:64], ones_f[:], ident_f[:, 0:64]), reads=["ones_f"], writes=["pA"])
                      P.op("dve", k.cp("dve", tmpo[:], pA[:, 0:64]), reads=["pA"], writes=["tmpo"])
                  _stop(P, "bY")
                  _stop(P, "bX")
                  bgs = sb("bgs2", [128, 4], F32)
                  P.op("sp", k.dma("sp", bgs[:], D["bg"]), writes=["bgs"], dsem="bgs")
                  masks = [sb("maskf", [128, 128], F32), sb("maskb", [128, 128], F32)]
                  P.op("sp", k.dma("sp", masks[0][:], D["trif"]), writes=["mask0"], dsem="mask0")
                  P.op("sp", k.dma("sp", masks[1][:], D["trib"]), writes=["mask1"], dsem="mask1")
                  zeros_b = sb("zeros_b", [128, 16, 1], BF16)
                  rmask = sb("rmask_s", [128, 4], F32)
                  P.op("sp", k.dma("sp", rmask[:], D["rmask"]), writes=["rmask"], dsem="rmask")
                  P.op("dve", k.ms("dve", zeros_b[:], 0.0), writes=["zeros_b"])
                  P.op("sp", k.dma("sp", D["mixs"][0, :, :, 0:1].rearrange("r (k p) o -> p (r k) o", p=128), zeros_b[:], slow=True), reads=["zeros_b"], dsem="zp0")
                  P.op("sp", k.dma("sp", D["mixs"][3, :, :, 2049:2050].rearrange("r (k p) o -> p (r k) o", p=128), zeros_b[:], slow=True), reads=["zeros_b"], dsem="zp1")
                  _stop(P, "b0")
                  wv, flv, decv = [], [], []
                  sm = lambda name: sb(name, [128, 64], F32)
                  for di in range(2):
                      gi, gf = 2 * di, 2 * di + 1
                      e = "dve"
                      zf, li, az, ez, lz, mz, lf, u, bc = [sm(f"{n}{di}") for n in ("zf", "li", "az", "ez", "lz", "mz", "lf", "u", "bc")]
                      w_, fl_, dec_ = sm(f"w{di}"), sm(f"fl{di}"), sm(f"dec{di}")
                      t1, t2 = sm(f"t1{di}"), sm(f"t2{di}")
                      rows = sb(f"rows{di}", [1, 6, 64], F32)
                      umc = sb(f"umc{di}", [64, 1], F32)
                      R = lambda n: f"{n}{di}"
                      P.op(e, k.ts(e, zf[:], gates_sb[:, :, gf], bgs[:, gf:gf + 1], None, ALU.add), reads=["gates", "bgs"], writes=[R("zf")])
                      P.op(e, k.ts(e, li[:], gates_sb[:, :, gi], bgs[:, gi:gi + 1], None, ALU.add), reads=["gates", "bgs"], writes=[R("li")])
                      P.op("act", k.act(az[:], zf[:], AF.Abs), reads=[R("zf")], writes=[R("az")])
                      P.op("act", k.act(ez[:], az[:], AF.Exp, scale=-1.0), reads=[R("az")], writes=[R("ez")])
                      P.op("act", k.act(lz[:], ez[:], AF.Ln, bias=1.0), reads=[R("ez")], writes=[R("lz")])
                      P.op(e, k.ts(e, mz[:], zf[:], 0.0, None, ALU.min), reads=[R("zf")], writes=[R("mz")])
                      P.op(e, k.tt(e, lf[:], mz[:], lz[:], ALU.subtract), reads=[R("mz"), R("lz")], writes=[R("lf")])
                      if di == 0:
                          _stop(P, "b1_1")
                      P.op("pe", k.mm(pA[:, 0:64], masks[di][:], lf[:]), reads=[R("lf"), f"mask{di}"], writes=["pA"])
                      P.op("pe", k.mm(pA[:, 64:128], ones_f[:], lf[:]), reads=[R("lf"), "ones_f"], writes=["pA"])
                      P.op(e, k.cp(e, bc[:], pA[:, 0:64]), reads=["pA"], writes=[R("bc")])
                      P.op(e, k.tt(e, u[:], li[:], bc[:], ALU.subtract), reads=[R("li"), R("bc")], writes=[R("u")])
                      P.op(e, k.cp(e, rows[0:1, 1, :], pA[0:1, 64:128]), reads=["pA"], writes=[R("gsum")])
                      if di == 0:
                          _stop(P, "b1_1a")
                      P.op("pe", k.tr(pA[0:64, 128:256], u[:], ident_f[:]), reads=[R("u"), "ident_f"], writes=["pA"])
                      P.op(e, k.rmax(umc[:], pA[0:64, 128:256]), reads=["pA"], writes=[R("umc")])
                      if di == 0:
                          _stop(P, "b1_1b")
                      P.op("pe", k.mm(pA[0:1, 256:320], umc[:], ident_f[0:64, 0:64]), reads=[R("umc"), "ident_f"], writes=["pA"])
                      P.op(e, k.cp(e, rows[0:1, 0, :], pA[0:1, 256:320]), reads=["pA"], writes=[R("umax")])
                      se = "dve"
                      if di == 0:
                          _stop(P, "b1_2")
                      sc = sb(f"scan{di}", [1, 4, 64], F32)
                      tmp = sb(f"scant{di}", [1, 64], F32)
                      P.op(se, k.cp(se, sc[0:1, 0, :], rows[0:1, 1, :]), reads=[R("gsum")], writes=[R("scan")])
                      P.op(se, k.tt(se, sc[0:1, 1, :], rows[0:1, 0, :], rows[0:1, 1, :], ALU.add), reads=[R("umax"), R("gsum")], writes=[R("scan")])
                      cur, nxt = 0, 2
                      for d_ in (1, 2, 4, 8, 16, 32):
                          n_ = 64 - d_
                          if di == 0:
                              lo, hi = slice(0, n_), slice(d_, 64)
                          else:
                              lo, hi = slice(d_, 64), slice(0, n_)
                          Gc_, Hc_, Gn_, Hn_ = sc[0:1, cur, :], sc[0:1, cur + 1, :], sc[0:1, nxt, :], sc[0:1, nxt + 1, :]
                          P.op(se, k.cp(se, sc[0:1, nxt:nxt + 2, :], sc[0:1, cur:cur + 2, :]), writes=[R("scan")])
                          P.op(se, k.tt(se, tmp[0:1, 0:n_], sc[0:1, cur + 1, lo], sc[0:1, cur, hi], ALU.add), writes=[R("scan")])
                          P.op(se, k.tt(se, sc[0:1, nxt + 1, hi], tmp[0:1, 0:n_], sc[0:1, cur + 1, hi], ALU.max), writes=[R("scan")])
                          P.op(se, k.tt(se, sc[0:1, nxt, hi], sc[0:1, cur, lo], sc[0:1, cur, hi], ALU.add), writes=[R("scan")])
                          cur, nxt = nxt, cur
                      P.op(se, k.ms(se, rows[0:1, 2, :], -1e30), writes=[R("scan")])
                      if di == 0:
                          P.op(se, k.cp(se, rows[0:1, 2, 1:64], sc[0:1, cur + 1, 0:63]), writes=[R("scan")])
                      else:
                          P.op(se, k.cp(se, rows[0:1, 2, 0:63], sc[0:1, cur + 1, 1:64]), writes=[R("scan")])
                      P.op(se, k.tt(se, rows[0:1, 3, :], rows[0:1, 2, :], rows[0:1, 0, :], ALU.max), writes=[R("scan")])
                      if di == 0:
                          _stop(P, "b1_3")
                      P.op(e, k.tt(e, rows[0:1, 4, :], rows[0:1, 2, :], rows[0:1, 3, :], ALU.subtract), reads=[R("scan")], writes=[R("dd")])
                      P.op("act", k.act(rows[0:1, 5, :], rows[0:1, 4, :], AF.Exp), reads=[R("dd")], writes=[R("decr")])
                      P.op("pe", k.mm(pA[:, 320:384], ones_f[0:1, :], rows[0:1, 3, :]), reads=[R("scan"), "ones_f"], writes=["pA"])
                      P.op("pe", k.mm(pA[:, 384:448], ones_f[0:1, :], rows[0:1, 5, :]), reads=[R("decr"), "ones_f"], writes=["pA"])
                      P.op(e, k.cp(e, dec_[:], pA[:, 384:448]), reads=["pA"], writes=[R("dec")])
                      P.op(e, k.cp(e, t2[:], pA[:, 320:384]), reads=["pA"], writes=[R("t2")])
                      P.op(e, k.tt(e, t1[:], u[:], t2[:], ALU.subtract), reads=[R("u"), R("t2")], writes=[R("t1")])
                      P.op("act", k.act(w_[:], t1[:], AF.Exp), reads=[R("t1")], writes=[R("w")])
                      P.op(e, k.tt(e, t2[:], bc[:], t2[:], ALU.add), reads=[R("bc")], writes=[R("t2")])
                      P.op("act", k.act(fl_[:], t2[:], AF.Exp, scale=-1.0), reads=[R("t2")], writes=[R("fl")])
                      wv.append(w_)
                      flv.append(fl_)
                      decv.append(dec_)

                  _stop(P, "b1")
                  mod2_steps = []
                  if mode == "fused":
                      cc2 = sb("cc2b", [128, 16], F32)
                      scb2 = sb("scb2b", [128, 16], BF16)
                      badar2 = sb("badar2b", [1, 256], F32)
                      modrow2 = sb("modrow2b", [1, 2048], F32)
                      g2cb = sb("g2cb", [128, 16], F32)
                      wst2 = [sb(f"wst2b{i}", [128, 16, 256], BF16) for i in range(2)]
                      P.op("sp", k.dma("sp", cc2[:], D["ccol"]), writes=["cc2"], dsem="cc2")
                      P.op("sp", k.dma("sp", g2cb[:], D["g2col"]), writes=["g2cb"], dsem="g2cb")
                      P.op("act", k.act(scb2[:], cc2[:], AF.Silu), reads=["cc2"], writes=["scb2"])

                      def mod2_block(part, cb):
                          def f():
                              c0 = part * 2048
                              P.op("sp", k.dma("sp", badar2[0:1, :], D["bada2"][:, c0 + cb * 256:c0 + (cb + 1) * 256]), writes=["badar2"], dsem="badar2")
                              bi = (part * 8 + cb) % 2
                              src = D["wada2"][:, c0 + cb * 256:c0 + (cb + 1) * 256].rearrange("(k p) n -> p k n", p=128)
                              P.op("pool", k.dma("pool", wst2[bi][:], src), writes=[f"wst2_{bi}"], dsem=f"wst2_{bi}")
                              for kk in range(16):
                                  P.op("pe", k.mm(pA[0:1, 0:256], scb2[:, kk:kk + 1], wst2[bi][:, kk, :], kk == 0, kk == 15),
                                       reads=["scb2", f"wst2_{bi}"], writes=["pA"])
                              P.op("dve", k.tt("dve", modrow2[0:1, cb * 256:(cb + 1) * 256], pA[0:1, 0:256], badar2[0:1, 0:256], ALU.add),
                                   reads=["pA", "badar2"], writes=["modrow2"])
                              if cb == 7:
                                  if part in (0, 3):
                                      dst = ga1_bcP if part == 0 else ga2_bc
                                      for nb in range(4):
                                          P.op("pe", k.mm(pA[:, 0:512], ones_f[0:1, :], modrow2[0:1, nb * 512:(nb + 1) * 512]), reads=["modrow2"], writes=["pA"])
                                          P.op("dve", k.cp("dve", dst[:, nb * 512:(nb + 1) * 512], pA[:, 0:512]), reads=["pA"], writes=[f"modout{part}"])
                                  elif part == 1:
                                      row2col(P, pA, modrow2, 0, 272, "pA")
                                      P.op("dve", k.cp("dve", shc2P[:], pA[:, 272:288]), reads=["pA"], writes=["shc2P"])
                                  else:
                                      row2col(P, pA, modrow2, 0, 256, "pA")
                                      P.op("dve", k.ts("dve", Gc2P[:], pA[:, 256:272], 1.0, None, ALU.add), reads=["pA"], writes=["Gc2P0"])
                                      P.op("dve", k.tt("dve", Gc2P[:], Gc2P[:], g2cb[:], ALU.mult), reads=["Gc2P0", "g2cb"], writes=["Gc2P"])
                          return f
                      mod2_steps = [mod2_block(p_, c_) for p_ in range(4) for c_ in range(8)]

                  NG = 2
                  qT4 = [sb(f"qT4_{i}", [128, 4, 2, 128], BF16) for i in range(NG)]
                  kT4 = [sb(f"kT4_{i}", [128, 4, 2, 128], BF16) for i in range(NG)]
                  ktk4 = [sb(f"ktk4_{i}", [128, 4, 256], BF16) for i in range(NG)]
                  vas4 = [sb(f"vas4_{i}", [128, 4, 257], BF16) for i in range(NG)]
                  hfc4 = [sb(f"hfc4_{i}", [128, 4, 256], F32) for i in range(NG)]
                  ogc4 = [sb(f"ogc4_{i}", [128, 4, 256], F32) for i in range(NG)]
                  hst4 = [sb(f"hst4_{i}", [128, 4, 256], F32) for i in range(2)]
                  SPb = [sb(f"SPb{i}", [128, 128], BF16) for i in range(2)]
                  kw = [sb(f"kw{i}", [128, 256], BF16) for i in range(2)]
                  C = sb("C", [128, 2, 257], F32)
                  Cd = sb("Cd", [128, 2, 257], BF16)
                  rrL = [sb(f"rr_{i}", [128, 1], F32) for i in range(2)]
                  rr2L = [sb(f"rr2_{i}", [128, 1], F32) for i in range(2)]
                  hsL = [sb(f"hs_{i}", [128, 256], F32) for i in range(2)]
                  junk2L = [sb(f"junk2_{i}", [128, 256], BF16) for i in range(2)]
                  ss2L = [sb(f"ss2_{i}", [128, 1], F32) for i in range(2)]
                  sd2L = [sb(f"sd2_{i}", [128, 1], F32) for i in range(2)]
                  rmsL = [sb(f"rms_{i}", [128, 1], F32) for i in range(2)]
                  hoL = [sb(f"ho_{i}", [128, 256], BF16) for i in range(2)]
                  mst = [sb(f"mst{i}", [128, 256], BF16) for i in range(2)]
                  MS = [sb(f"MS{i}", [128, 2, 4, 512], BF16) for i in range(2)]
                  NS = [sb(f"NS{i}", [128, 1, 4, 512], BF16) for i in range(2)]
                  evc = [0]

                  def stage_put(bufs, name, nk, f0, cnt, src_fn, tt, rsrc):
                      g, idx = divmod(tt, 4)
                      buf, res = bufs[g % 2], f"{name}{g % 2}"
                      allres = [f"{res}_{kk}_{r_}_{i_}" for kk in range(nk) for r_ in range(4) for i_ in range(4)]
                      for kk in range(nk):
                          for r_ in range(4):
                              dst = buf[:, kk, r_, idx * 128:(idx + 1) * 128]
                              evc[0] += 1
                              wres = [f"{res}_{kk}_{r_}_{idx}"]
                              if evc[0] % 2 == 0:
                                  P.op("act", k.act(dst, src_fn(kk), AF.Copy, scale=rmask[:, r_:r_ + 1]), reads=[rsrc, "rmask"], writes=wres)
                              else:
                                  P.op("dve", k.ts("dve", dst, src_fn(kk), rmask[:, r_:r_ + 1], None, ALU.mult), reads=[rsrc, "rmask"], writes=wres)
                      cnt[g] = cnt.get(g, 0) + 1
                      if cnt[g] == 4:
                          j, g4 = divmod(g, 4)
                          for kk in range(nk):
                              rd = [f"{res}_{kk}_{r_}_{i_}" for r_ in range(4) for i_ in range(4)]
                              dstv = lambda jj, c0, c1, kk=kk: D["mixs"][jj, :, f0 + kk * 128:f0 + (kk + 1) * 128, c0:c1].rearrange("r f t -> f r t")
                              P.op("sp", k.dma("sp", dstv(j, 1 + g4 * 512, 1 + (g4 + 1) * 512), buf[:, kk, :, :]), reads=rd, dsem=res)
                              if g4 == 3 and j < 3:
                                  P.op("sp", k.dma("sp", dstv(j + 1, 0, 1), buf[:, kk, :, 511:512], slow=True), reads=rd, dsem=res)
                              if g4 == 0 and j > 0:
                                  P.op("sp", k.dma("sp", dstv(j - 1, 2049, 2050), buf[:, kk, :, 0:1], slow=True), reads=rd, dsem=res)

                  groups = [(0, list(range(g * 4, g * 4 + 4))) for g in range(16)] + [(1, list(range(g * 4 + 3, g * 4 - 1, -1))) for g in range(15, -1, -1)]

                  def gloads(gi):
                      di, cs = groups[gi]
                      cb_, b = min(cs), gi % NG
                      P.op("sp", k.dma("sp", qT4[b][:], D["qAT"][cb_:cb_ + 4].rearrange("c p k t -> p c k t")), reads=["d_fm0", "d_fm1"], writes=[f"qT4_{b}"], dsem=f"qT4_{b}")
                      P.op("sp", k.dma("sp", kT4[b][:], D["kAT"][cb_:cb_ + 4].rearrange("c p k t -> p c k t")), reads=["d_fm2", "d_fm3"], writes=[f"kT4_{b}"], dsem=f"kT4_{b}")
                      P.op("sp", k.dma("sp", ktk4[b][:], D["ktok"][cb_:cb_ + 4].rearrange("c p e -> p c e")), writes=[f"ktk4_{b}"], dsem=f"ktk4_{b}")
                      P.op("sp", k.dma("sp", vas4[b][:], D["vA"][cb_:cb_ + 4].rearrange("c p e -> p c e")), writes=[f"vas4_{b}"], dsem=f"vas4_{b}")

                  def gloads_h(gi):
                      di, cs = groups[gi]
                      cb_, b = min(cs), gi % NG
                      if di == 1:
                          P.op("sp", k.dma("sp", hfc4[b][:], D["hf"][cb_:cb_ + 4].rearrange("c p e -> p c e")), reads=["d_hf"], writes=[f"hfc4_{b}"], dsem=f"hfc4_{b}")
                          P.op("sp", k.dma("sp", ogc4[b][:], D["og"][cb_:cb_ + 4].rearrange("c p e -> p c e")), writes=[f"ogc4_{b}"], dsem=f"ogc4_{b}")

                  mcnt = {}
                  pend = [None]
                  gloads(0)
                  si = 0
                  for gi, (di, cs) in enumerate(groups):
                      if gi == 16 and pend[0] is not None:
                          pend[0]()
                          pend[0] = None
                      gloads_h(gi)
                      if gi + 1 < len(groups):
                          gloads(gi + 1)
                      if gi < len(mod2_steps):
                          mod2_steps[gi]()
                      cb_, b = min(cs), gi % NG
                      if gi in (0, 16):
                          P.op("dve", k.ms("dve", C[:], 0.0), writes=["C"])
                      for c in cs:
                          ix = c - cb_
                          s2 = si % 2
                          si += 1
                          wc = wv[di][:, c:c + 1]
                          dc = decv[di][:, c:c + 1]
                          rq, rk, rkt, rv = f"qT4_{b}", f"kT4_{b}", f"ktk4_{b}", f"vas4_{b}"
                          for kk in range(2):
                              P.op("pe", k.mm(pS[s2][:, 0:128], kT4[b][:, ix, kk, :], qT4[b][:, ix, kk, :], kk == 0, kk == 1),
                                   reads=[rk, rq], writes=[f"pS{s2}"])
                          P.op("dve", k.stt("dve", SPb[s2][:], pS[s2][:, 0:128], wc, masks[di][:], ALU.mult, ALU.mult),
                               reads=[f"pS{s2}", f"w{di}", f"mask{di}"], writes=[f"SPb{s2}"])
                          P.op("dve", k.ts("dve", C[:], C[:], dc, None, ALU.mult), reads=[f"dec{di}"], writes=["C"])
                          P.op("act", k.act(Cd[:], C[:], AF.Copy), reads=["C"], writes=["Cd"])
                          P.op("pe", k.mm(pN[s2][:, 0:257], SPb[s2][:], vas4[b][:, ix, :], True, False), reads=[f"SPb{s2}", rv], writes=[f"pN{s2}"])
                          for kk in range(2):
                              P.op("pe", k.mm(pN[s2][:, 0:257], qT4[b][:, ix, kk, :], Cd[:, kk, :], False, kk == 1), reads=[rq, "Cd"], writes=[f"pN{s2}"])
                          P.op("act", k.act(kw[s2][:], ktk4[b][:, ix, :], AF.Copy, scale=wc), reads=[rkt, f"w{di}"], writes=[f"kw{s2}"])
                          for kk in range(2):
                              P.op("pe", k.mm(pK[kk][:, 0:257], kw[s2][:, kk * 128:(kk + 1) * 128], vas4[b][:, ix, :]), reads=[f"kw{s2}", rv], writes=[f"pK{kk}"])
                          for kk in range(2):
                              P.op("dve", k.tt("dve", C[:, kk, :], pK[kk][:, 0:257], C[:, kk, :], ALU.add), reads=[f"pK{kk}"], writes=["C"])
                          def epi(di=di, c=c, gi=gi, ix=ix, s2=s2, b=b, cb_=cb_, last=(c == cs[-1])):
                              rr, rr2, hs, junk2, ss2, sd2, rms, ho = rrL[s2], rr2L[s2], hsL[s2], junk2L[s2], ss2L[s2], sd2L[s2], rmsL[s2], hoL[s2]
                              T = lambda n: f"{n}_{s2}"
                              P.op("act", k.act(rr[:], pN[s2][:, 256:257], AF.Abs), reads=[f"pN{s2}"], writes=[T("rr0")])
                              P.op("dve", k.ts("dve", rr[:], rr[:], flv[di][:, c:c + 1], None, ALU.max), reads=[T("rr0"), f"fl{di}"], writes=[T("rr")])
                              P.op("dve", k.rcp(rr2[:], rr[:]), reads=[T("rr")], writes=[T("rr2")])
                              if di == 0:
                                  hb_ = gi % 2
                                  P.op("act", k.act(hst4[hb_][:, ix, :], pN[s2][:, 0:256], AF.Copy, scale=rr2[:, 0:1]), reads=[f"pN{s2}", T("rr2")], writes=[f"hst4_{hb_}"])
                                  if last:
                                      P.op("sp", k.dma("sp", D["hf"][cb_:cb_ + 4].rearrange("c p e -> p c e"), hst4[hb_][:]), reads=[f"hst4_{hb_}"], writes=["d_hf"], dsem=f"hst4_{hb_}")
                              else:
                                  P.op("dve", k.stt("dve", hs[:], pN[s2][:, 0:256], rr2[:, 0:1], hfc4[b][:, ix, :], ALU.mult, ALU.add),
                                       reads=[f"pN{s2}", T("rr2"), f"hfc4_{b}"], writes=[T("hs")])
                                  P.op("act", k.act(junk2[:], hs[:], AF.Square, accum=ss2[:, 0:1]), reads=[T("hs")], writes=[T("junk2"), T("ss2")])
                                  P.op("act", k.act(sd2[:], ss2[:], AF.Sqrt, bias=EPS, scale=1.0 / 256), reads=[T("ss2")], writes=[T("sd2")])
                                  P.op("dve", k.rcp(rms[:], sd2[:]), reads=[T("sd2")], writes=[T("rms")])
                                  P.op("dve", k.stt("dve", ho[:], hs[:], rms[:, 0:1], ogc4[b][:, ix, :], ALU.mult, ALU.mult), reads=[T("hs"), T("rms"), f"ogc4_{b}"], writes=[T("ho")])
                                  for kk in range(2):
                                      P.op("pe", k.tr(pT[:, kk * 128:(kk + 1) * 128], ho[:, kk * 128:(kk + 1) * 128], ident_b[:]), reads=[T("ho"), "ident_b"], writes=["pT"])
                                  P.op("act", k.act(mst[s2][:], pT[:, 0:256], AF.Copy), reads=["pT"], writes=[f"mst{s2}"])
                                  stage_put(MS, "MS", 2, 0, mcnt, lambda kk, s2=s2: mst[s2][:, kk * 128:(kk + 1) * 128], c, f"mst{s2}")
                          if pend[0] is not None:
                              pend[0]()
                          pend[0] = epi
                  if pend[0] is not None:
                      pend[0]()
                      pend[0] = None

                  _stop(P, "b2")
                  EB = sb("EB", [128, 3200], F32)
                  qh = sb("qh", [128, 64, 128], BF16)
                  kh = sb("kh", [128, 64, 128], BF16)
                  vh = sb("vh", [128, 64, 128], BF16)
                  Es = [sb(f"Es{i}", [128, 640], F32) for i in range(3)]
                  Pt = [sb(f"Pt{i}", [128, 640], BF16) for i in range(3)]
                  rinv = [sb(f"rinv{i}", [128, 128], F32) for i in range(3)]
                  ob = [sb(f"ob{i}", [128, 128], BF16) for i in range(3)]
                  pSn = [(pS[0], "pS0"), (pS[1], "pS1"), (pN[0], "pN0")]
                  pOn = [(pK[0], "pK0"), (pK[1], "pK1"), (pN[1], "pN1")]

                  def kbv(pr):
                      kb = min(max(pr - 2, 0), 59)
                      v = 0 if pr == 0 else 1 if pr == 1 else 3 if pr == 62 else 4 if pr == 63 else 2
                      return kb, v

                  si = 0
                  npend = [None]
                  for hb in range(2):
                      ncnt = {}
                      P.op("sp", k.dma("sp", EB[:, :], D["nab"][hb]), writes=["EB"], dsem="EB")
                      P.op("act", k.act(EB[:, :], EB[:, :], AF.Exp), writes=["EB"])
                      P.op("sp", k.dma("sp", qh[:], D["qBT"][hb].rearrange("c p t -> p c t")), reads=[f"d_fm{4 + 2 * hb}"], writes=["qh"], dsem="qh")
                      P.op("sp", k.dma("sp", kh[:], D["kBT"][hb].rearrange("c p t -> p c t")), reads=[f"d_fm{5 + 2 * hb}"], writes=["kh"], dsem="kh")
                      P.op("sp", k.dma("sp", vh[:], D["vB"][:, :, hb * 128:(hb + 1) * 128].rearrange("c p d -> p c d")), writes=["vh"], dsem="vh")
                      for pr in range(64):
                          s2 = si % 3
                          si += 1
                          kb, v = kbv(pr)
                          (pSb, rS), (pOb, rO) = pSn[s2], pOn[s2]
                          for kt in range(5):
                              dst = pSb[:, kt * 128:(kt + 1) * 128] if kt < 4 else pOb[:, 256:384]
                              P.op("pe", k.mm(dst, kh[:, kb + kt, :], qh[:, pr, :]), reads=["kh", "qh"], writes=[rS if kt < 4 else rO])
                          P.op("act", k.act(Es[s2][:, 0:512], pSb[:, 0:512], AF.Exp, scale=SCALE_B), reads=[rS], writes=[f"Es{s2}"])
                          P.op("act", k.act(Es[s2][:, 512:640], pOb[:, 256:384], AF.Exp, scale=SCALE_B), reads=[rO], writes=[f"Es{s2}"])
                          P.op("dve", k.tt("dve", Pt[s2][:], Es[s2][:], EB[:, v * 640:(v + 1) * 640], ALU.mult), reads=[f"Es{s2}", "EB"], writes=[f"Pt{s2}"])

                          def back(s2=s2, kb=kb, pr=pr, hb=hb, pOb=pOb, rO=rO, ncnt=ncnt):
                              for kt in range(5):
                                  P.op("pe", k.mm(pOb[:, 0:128], vh[:, kb + kt, :], Pt[s2][:, kt * 128:(kt + 1) * 128], kt == 0, kt == 4),
                                       reads=["vh", f"Pt{s2}"], writes=[rO])
                              for kt in range(5):
                                  P.op("pe", k.mm(pOb[:, 128:256], ones_b[:], Pt[s2][:, kt * 128:(kt + 1) * 128], kt == 0, kt == 4),
                                       reads=["ones_b", f"Pt{s2}"], writes=[rO])
                              P.op("dve", k.rcp(rinv[s2][:], pOb[:, 128:256]), reads=[rO], writes=[f"rinv{s2}"])
                              P.op("dve", k.tt("dve", ob[s2][:], pOb[:, 0:128], rinv[s2][:], ALU.mult), reads=[rO, f"rinv{s2}"], writes=[f"ob{s2}"])
                              stage_put(NS, f"NS", 1, 256 + hb * 128, ncnt, lambda kk, s2=s2: ob[s2][:, :], pr, f"ob{s2}")
                          if npend[0] is not None:
                              npend[0]()
                          npend[0] = back
                      if npend[0] is not None:
                          npend[0]()
                          npend[0] = None
                  P.emit()

        if mode == "fused" and not Prog.stopped:
            ccs = nc.alloc_semaphore("cc_sem")
            with nc.Block() as blk:
                def _cc(g):
                    g.collective_compute("ReduceScatter", ALU.add, replica_groups=[[0, 1, 2, 3], [4, 5, 6, 7]],
                                         ins=[D["mixs"].rearrange("j r f t -> (j r f) t").opt()], outs=[D["mixr"].opt()]).then_inc(ccs, 1)
                    g.wait_ge(ccs, 1)
                blk.gpsimd(_cc)
                blk.sync(lambda e: e.wait_ge(ccs, 1))
                blk.tensor(lambda e: e.wait_ge(ccs, 1))
                blk.vector(lambda e: e.wait_ge(ccs, 1))
                blk.scalar(lambda e: e.wait_ge(ccs, 1))

        if do2:
            with contextlib.ExitStack() as st:
                sb = lambda name, shape, dty: st.enter_context(nc.sbuf_tensor(name, shape, dty))
                psum = lambda name, dty=F32, n=512: st.enter_context(nc.psum_tensor(name, [128, n], dty))
                P = Prog(nc, "c")
                if not do1:
                    consts(P)
                tpb = [psum("tp0c"), psum("tp1c")]
                pw = [psum(f"pw{i}") for i in range(4)]
                pmisc = psum("pmisc2")
                Wo = sb("Wo", [128, 16, 2048], BF16)
                P.op("pool", k.dma("pool", Wo[:], D["wout"].rearrange("(k p) n -> p k n", p=128)), writes=["Wo"], dsem="Wo")
                _stop(P, "cX")
                if do1:
                    ga1_bc, Gc, shc = ga1_bcP, Gc2P, shc2P
                else:
                    modrow = sb("modrow2", [1, 2048], F32)
                    mrun = mk_modrows(P, sb, pmisc, "2", 2048)
                    g2c = sb("g2c", [128, 16], F32)
                    P.op("sp", k.dma("sp", g2c[:], D["g2col"]), writes=["g2c"], dsem="g2c")
                    ga1_bc = sb("ga1_bc", [128, 2048], F32)
                    Gc = sb("Gc2", [128, 16], F32)
                    shc = sb("shc2", [128, 16], F32)

                    def bcast(dst, res):
                        for nb in range(4):
                            P.op("pe", k.mm(pw[nb][:, 0:512], ones_f[0:1, :], modrow[0:1, nb * 512:(nb + 1) * 512]), reads=["modrow", "ones_f"], writes=[f"pw{nb}"])
                            P.op("dve", k.cp("dve", dst[:, nb * 512:(nb + 1) * 512], pw[nb][:, 0:512]), reads=[f"pw{nb}"], writes=[res])

                    mrun(D["wada2"], D["bada2"], 0, 2048, modrow)
                    bcast(ga1_bc, "ga1_bc")
                    _stop(P, "cY")
                    mrun(D["wada2"], D["bada2"], 2048, 2048, modrow)
                    row2col(P, pmisc, modrow, 0, 272, "pmisc")
                    P.op("dve", k.cp("dve", shc[:], pmisc[:, 272:288]), reads=["pmisc"], writes=["shc"])
                    _stop(P, "cZ1")
                    mrun(D["wada2"], D["bada2"], 4096, 2048, modrow)
                    _stop(P, "cZ1b")
                    row2col(P, pmisc, modrow, 0, 256, "pmisc")
                    _stop(P, "cZ1c")
                    P.op("dve", k.ts("dve", Gc[:], pmisc[:, 256:272], 1.0, None, ALU.add), reads=["pmisc"], writes=["Gc0"])
                    _stop(P, "cZ1d")
                    P.op("dve", k.tt("dve", Gc[:], Gc[:], g2c[:], ALU.mult), reads=["Gc0", "g2c", "shc"], writes=["Gc"])
                    _stop(P, "cZ2")
                    mrun(D["wada2"], D["bada2"], 6144, 2048, modrow)
                    bcast(ga2_bc, "ga2_bc")
                _stop(P, "c0")
                mixT = [sb(f"mixT{i}", [128, 16, 128], BF16) for i in range(2)]
                xts = [sb(f"x2t{i}", [128, 2048], F32) for i in range(2)]
                x1s = [sb(f"x1s{i}", [128, 2048], F32) for i in range(2)]
                xs = sb("xs2", [128, 2048], F32)
                junk = sb("junk3", [128, 2048], BF16)
                ss, sd, rstd = sb("ss3", [128, 1], F32), sb("sd3", [128, 1], F32), sb("rstd3", [128, 1], F32)
                h2st = [sb(f"h2st{i}", [128, 16, 128], BF16) for i in range(2)]
                mixr3 = D["mixr"].rearrange("(k p) t -> p k t", p=128)

                def tile_info(t):
                    if t < 16:
                        return 128, slice(1 + t * 128, 1 + (t + 1) * 128), slice(t * 128, (t + 1) * 128)
                    return 2, slice(0, 2050, 2049), slice(2048, 2050)

                def loads2(t):
                    M, cs, rs = tile_info(t)
                    b = t % 2
                    if t < 16:
                        P.op("sp", k.dma("sp", mixT[b][:, :, 0:M], mixr3[:, :, cs]), writes=[f"mixT{b}"], dsem=f"mixT{b}")
                    else:
                        P.op("sp", k.dma("sp", mixT[b][:, :, 0:1], mixr3[:, :, 0:1], slow=True), writes=[f"mixT{b}"], dsem=f"mixT{b}")
                        P.op("sp", k.dma("sp", mixT[b][:, :, 1:2], mixr3[:, :, 2049:2050], slow=True), writes=[f"mixT{b}x"], dsem=f"mixT{b}")
                    P.op("sp", k.dma("sp", xts[b][0:M, :], D["xtok"][rs, :]), writes=[f"x2t{b}"], dsem=f"x2t{b}")

                loads2(0)
                pend2 = [None]
                for t in range(17):
                    if t == 1:
                        _stop(P, "c1")
                    if t == 16:
                        _stop(P, "c16")
                    if t + 1 < 17:
                        loads2(t + 1)
                    M, cs, rs = tile_info(t)
                    b = t % 2
                    for nb in range(4):
                        for fc in range(16):
                            P.op("pe", k.mm(pw[nb][0:M, 0:512], mixT[b][:, fc, 0:M], Wo[:, fc, nb * 512:(nb + 1) * 512], fc == 0, fc == 15),
                                 reads=[f"mixT{b}", f"mixT{b}x", "Wo"], writes=[f"pw{nb}"])
                        P.op("dve", k.tt("dve", x1s[b][0:M, nb * 512:(nb + 1) * 512], pw[nb][0:M, 0:512], ga1_bc[0:M, nb * 512:(nb + 1) * 512], ALU.mult),
                             reads=[f"pw{nb}", "ga1_bc"], writes=[f"x1s{b}"])
                    P.op("dve", k.tt("dve", x1s[b][0:M, :], x1s[b][0:M, :], xts[b][0:M, :], ALU.add), reads=[f"x2t{b}"], writes=[f"x1s{b}"])
                    if t < 16:
                        P.op("sp", k.dma("sp", D["x1"][rs, :], x1s[b][0:M, :]), reads=[f"x1s{b}"], writes=["d_x1"], dsem=f"x1s{b}")
                    def tail2(t=t, M=M, cs=cs, b=b):
                        norm_T(P, x1s[b], xs, M, ss, sd, rstd, junk, tpb, Gc, shc,
                               lambda kk, b=b, M=M: h2st[b][:, kk, 0:M], f"x1s{b}", (lambda kk, b=b: f"h2st{b}_{kk}"), t)
                        if t < 16:
                            P.op("sp", k.dma("sp", D["h2T"][:, :, cs], h2st[b][:, :, 0:M]), reads=[f"h2st{b}_{kk_}" for kk_ in range(16)], writes=["d_h2T"], dsem=f"h2st{b}")
                        else:
                            P.op("sp", k.dma("sp", D["h2T"][:, :, 0:1], h2st[b][:, :, 0:1], slow=True), reads=[f"h2st{b}_{kk_}" for kk_ in range(16)], writes=["d_h2T"], dsem=f"h2st{b}")
                            P.op("sp", k.dma("sp", D["h2T"][:, :, 2049:2050], h2st[b][:, :, 1:2], slow=True), reads=[f"h2st{b}_{kk_}" for kk_ in range(16)], writes=["d_h2Tx"], dsem=f"h2st{b}")
                    if pend2[0] is not None:
                        pend2[0]()
                    pend2[0] = tail2
                if pend2[0] is not None:
                    pend2[0]()
                P.emit()

            _stop(P, "c")
            with contextlib.ExitStack() as st:
                sb = lambda name, shape, dty: st.enter_context(nc.sbuf_tensor(name, shape, dty))
                psum = lambda name, dty=F32, n=512: st.enter_context(nc.psum_tensor(name, [128, n], dty))
                P = Prog(nc, "d")
                pU = [psum("pU0"), psum("pU1")]
                pG = [psum("pG0"), psum("pG1")]
                pX = psum("pX")
                pE = [psum(f"pE{i}") for i in range(3)]
                AT = sb("AT", [128, 44, 1024], BF16)
                arena = sb("arena", [128, 22528], BF16)
                h2blk = arena[:, 0:16 * 1026].rearrange("p (k t) -> p k t", k=16)
                wdh = [arena[:, i * 11264:(i + 1) * 11264].rearrange("p (f c) -> p f c", f=22) for i in range(2)]
                wug = [sb(f"wug{i}", [128, 16, 256], BF16) for i in range(2)]
                gsbs = [sb(f"gsb{i}", [128, 1026], F32) for i in range(2)]
                accs = [sb(f"acc{i}", [128, 1024], F32) for i in range(2)]
                cw = sb("cw", [128, 44, 3], F32)
                cb = sb("cb", [128, 44], F32)
                flg = sb("flg", [128, 2], F32)
                x1p = [sb(f"x1p{i}", [128, 512], F32) for i in range(3)]
                zst = [sb(f"zst{i}", [128, 512], F32) for i in range(3)]
                P.op("sp", k.dma("sp", cw[:], D["convw"]), writes=["cw"], dsem="cw")
                P.op("sp", k.dma("sp", cb[:], D["convb"]), writes=["cw"], dsem="cb")
                P.op("sp", k.dma("sp", flg[:], D["flags"]), writes=["cw"], dsem="flg")
                wi = 0
                for tbk in range(2):
                    P.op("sp", k.dma("sp", h2blk, D["h2T"][:, :, tbk * 1024:tbk * 1024 + 1026]), reads=["d_h2T", "d_h2Tx"], writes=["arena", "wdh0", "wdh1"], dsem="h2blk")
                    for ft in range(44):
                        if ft == 1 and tbk == 0:
                            _stop(P, "d0")
                        b = ft % 2
                        P.op("pool", k.dma("pool", wug[b][:], D["wup"][ft]), writes=[f"wug{b}"], dsem=f"wug{b}")
                        for sbk in range(2):
                            cols = slice(1 + sbk * 512, 1 + (sbk + 1) * 512)
                            for kk in range(16):
                                P.op("pe", k.mm(pG[sbk][:, 0:512], wug[b][:, kk, 128:256], h2blk[:, kk, cols], kk == 0, kk == 15),
                                     reads=[f"wug{b}", "arena"], writes=[f"pG{sbk}"])
                        for kk in range(16):
                            P.op("pe", k.mm(pX[:, 0:2], wug[b][:, kk, 128:256], h2blk[:, kk, 0:1026:1025], kk == 0, kk == 15),
                                 reads=[f"wug{b}", "arena"], writes=["pX"])
                        for sbk in range(2):
                            cols = slice(1 + sbk * 512, 1 + (sbk + 1) * 512)
                            for kk in range(16):
                                P.op("pe", k.mm(pU[sbk][:, 0:512], wug[b][:, kk, 0:128], h2blk[:, kk, cols], kk == 0, kk == 15),
                                     reads=[f"wug{b}", "arena"], writes=[f"pU{sbk}"])
                        gs, ac = gsbs[b], accs[b]
                        P.op("act", k.act(gs[:, 1:513], pG[0][:, 0:512], AF.Copy), reads=["pG0"], writes=[f"gsb{b}"])
                        P.op("act", k.act(gs[:, 513:1025], pG[1][:, 0:512], AF.Copy), reads=["pG1"], writes=[f"gsb{b}"])
                        if tbk == 0:
                            P.op("act", k.act(gs[:, 0:1], pX[:, 0:1], AF.Copy, scale=flg[:, 0:1]), reads=["pX", "cw"], writes=[f"gsb{b}"])
                            P.op("act", k.act(gs[:, 1025:1026], pX[:, 1:2], AF.Copy), reads=["pX"], writes=[f"gsb{b}"])
                        else:
                            P.op("act", k.act(gs[:, 0:1], pX[:, 0:1], AF.Copy), reads=["pX"], writes=[f"gsb{b}"])
                            P.op("act", k.act(gs[:, 1025:1026], pX[:, 1:2], AF.Copy, scale=flg[:, 1:2]), reads=["pX", "cw"], writes=[f"gsb{b}"])
                        P.op("dve", k.ts("dve", ac[:], gs[:, 1:1025], cw[:, ft, 1:2], cb[:, ft:ft + 1], ALU.mult, ALU.add), reads=[f"gsb{b}", "cw"], writes=[f"acc{b}"])
                        P.op("dve", k.stt("dve", ac[:], gs[:, 0:1024], cw[:, ft, 0:1], ac[:], ALU.mult, ALU.add), reads=[f"gsb{b}", "cw"], writes=[f"acc{b}"])
                        P.op("dve", k.stt("dve", ac[:], gs[:, 2:1026], cw[:, ft, 2:3], ac[:], ALU.mult, ALU.add), reads=[f"gsb{b}", "cw"], writes=[f"acc{b}"])
                        P.op("act", k.act(ac[:], ac[:], AF.Gelu), writes=[f"acc{b}"])
                        P.op("dve", k.tt("dve", AT[:, ft, 0:512], pU[0][:, 0:512], ac[:, 0:512], ALU.mult), reads=[f"acc{b}", "pU0"], writes=["AT"])
                        P.op("dve", k.tt("dve", AT[:, ft, 512:1024], pU[1][:, 0:512], ac[:, 512:1024], ALU.mult), reads=[f"acc{b}", "pU1"], writes=["AT"])
                    if tbk == 0:
                        _stop(P, "d1")
                    accs_ps = [(pU[0], "pU0"), (pU[1], "pU1"), (pG[0], "pG0"), (pG[1], "pG1"), (pX, "pX"), (pE[0], "pE0"), (pE[1], "pE1"), (pE[2], "pE2")]
                    for nb in range(4):
                        for hf in range(2):
                            wb = wi % 2
                            wi += 1
                            P.op("pool", k.dma("pool", wdh[wb], D["wdn"][nb, hf]), writes=["arena", f"wdh{wb}"], dsem=f"wdh{wb}")
                            for tt in range(8):
                                for f in range(22):
                                    ft = hf * 22 + f
                                    P.op("pe", k.mm(accs_ps[tt][0][:, 0:512], AT[:, ft, tt * 128:(tt + 1) * 128], wdh[wb][:, f, :], ft == 0, ft == 43),
                                         reads=["AT", f"wdh{wb}"], writes=[accs_ps[tt][1]])
                        for tt in range(8):
                            row0 = tbk * 1024 + tt * 128
                            xb_ = (nb * 8 + tt) % 3
                            zb = (nb * 8 + tt) % 3
                            cs_ = slice(nb * 512, (nb + 1) * 512)
                            P.op("sp", k.dma("sp", x1p[xb_][:], D["x1"][row0:row0 + 128, cs_]), writes=[f"x1p{xb_}"], dsem=f"x1p{xb_}")
                            P.op("dve", k.tt("dve", zst[zb][:], accs_ps[tt][0][:, 0:512], ga2_bc[:, cs_], ALU.mult), reads=[accs_ps[tt][1]], writes=[f"zst{zb}"])
                            P.op("dve", k.tt("dve", zst[zb][:], zst[zb][:], x1p[xb_][:], ALU.add), reads=[f"x1p{xb_}"], writes=[f"zst{zb}"])
                            P.op("sp", k.dma("sp", D["z"][row0:row0 + 128, cs_], zst[zb][:]), reads=[f"zst{zb}"], writes=["d_z"], dsem=f"zst{zb}")
                P.emit()

            _stop(P, "d")
            with contextlib.ExitStack() as st:
                sb = lambda name, shape, dty: st.enter_context(nc.sbuf_tensor(name, shape, dty))
                P = Prog(nc, "e")
                gf = sb("gf", [128, 2048], F32)
                P.op("sp", k.dma("sp", gf[:], D["gfin"]), writes=["gf"], dsem="gf")
                zt = [sb(f"zt{i}", [128, 2048], F32) for i in range(2)]
                ot = [sb(f"ot{i}", [128, 2048], F32) for i in range(2)]
                junk = sb("junk4", [128, 2048], BF16)
                ss, sd, rstd = sb("ss4", [128, 1], F32), sb("sd4", [128, 1], F32), sb("rstd4", [128, 1], F32)
                P.op("sp", k.dma("sp", zt[0][:], D["z"][0:128, :]), writes=["zt0"], dsem="zt0")
                for t in range(16):
                    b = t % 2
                    if t + 1 < 16:
                        P.op("sp", k.dma("sp", zt[1 - b][:], D["z"][(t + 1) * 128:(t + 2) * 128, :]), writes=[f"zt{1 - b}"], dsem=f"zt{1 - b}")
                    P.op("act", k.act(junk[:], zt[b][:], AF.Square, accum=ss[:, 0:1]), reads=[f"zt{b}"], writes=["junk", "ss"])
                    P.op("act", k.act(sd[:], ss[:], AF.Sqrt, bias=EPS, scale=1.0 / 2048), reads=["ss"], writes=["sd"])
                    P.op("dve", k.rcp(rstd[:], sd[:]), reads=["sd"], writes=["rstd"])
                    P.op("dve", k.stt("dve", ot[b][:], zt[b][:], rstd[:, 0:1], gf[:], ALU.mult, ALU.mult), reads=[f"zt{b}", "rstd", "gf"], writes=[f"ot{b}"])
                    P.op("sp", k.dma("sp", D["out"][t * 128:(t + 1) * 128, :], ot[b][:]), reads=[f"ot{b}"], dsem=f"ot{b}")
                P.emit()
    return nc


def _col(v):
    return np.ascontiguousarray(v.reshape(16, 128).T)


def _nab_tables(rpb_l, heads):
    NEG = np.float32(-30000.0)
    out = np.full((len(heads), 128, 5, 5, 128), NEG, np.float32)
    p = np.arange(128)
    q = np.arange(128)
    for v, pr in enumerate((0, 1, 10, 62, 63)):
        kb = min(max(pr - 2, 0), 59)
        r = 2 * pr + q // 64
        c = q % 64
        rs = np.clip(r - 4, 0, 120)
        cs = np.clip(c - 8, 0, 48)
        for kt in range(5):
            krow = 2 * (kb + kt) + p // 64
            kc = p % 64
            ok = ((krow[:, None] >= rs[None, :]) & (krow[:, None] < rs[None, :] + 8)
                  & (kc[:, None] >= cs[None, :]) & (kc[:, None] < cs[None, :] + 16))
            dr = np.clip(krow[:, None] - r[None, :] + 7, 0, 14)
            dc = np.clip(kc[:, None] - c[None, :] + 15, 0, 30)
            for hi, h in enumerate(heads):
                vals = rpb_l[h][dr, dc]
                out[hi, :, v, kt, :] = np.where(ok, vals, NEG)
    return out.reshape(len(heads), 128, 3200)


def _inputs_h1(inp, j):
    b, hq = divmod(j, 4)
    w_in = inp["w_in"][0]
    A = 1024
    qa = lambda h: slice(h * 256, (h + 1) * 256)
    cols = []
    cols += list(range(0 * A + hq * 256, 0 * A + (hq + 1) * 256))
    cols += list(range(1 * A + hq * 256, 1 * A + (hq + 1) * 256))
    gb = 4 * A + 16
    for hl in range(2):
        h = 2 * hq + hl
        cols += list(range(gb + h * 128, gb + (h + 1) * 128))
        cols += list(range(gb + 1024 + h * 128, gb + 1024 + (h + 1) * 128))
    cols += list(range(2 * A + hq * 256, 2 * A + (hq + 1) * 256))
    cols += list(range(3 * A + hq * 256, 3 * A + (hq + 1) * 256))
    cols += list(range(1 * A + hq * 256, 1 * A + (hq + 1) * 256))
    for hl in range(2):
        h = 2 * hq + hl
        cols += list(range(gb + 2048 + h * 128, gb + 2048 + (h + 1) * 128))
    gcols = [4 * A + g * 4 + hq for g in range(4)]
    cols += gcols
    tri = np.triu(np.ones((128, 128), np.float32))
    return {
        "ccol": _col(inp["c"][b]),
        "ident": np.eye(128, dtype=np.float32),
        "xb": np.ascontiguousarray(inp["x"][b]),
        "wada1": np.ascontiguousarray(inp["w_ada"][0][:, 0:4096]),
        "bada1": np.ascontiguousarray(inp["b_ada"][0][None, 0:4096]),
        "g1col": _col(inp["g_norm1"][0]),
        "win": np.ascontiguousarray(w_in[:, cols]),
        "bg": np.ascontiguousarray(np.broadcast_to(inp["b_gates"][0][[g * 4 + hq for g in range(4)]][None, :], (128, 4))),
        "gha": np.ascontiguousarray(np.broadcast_to(inp["g_head_a"][0][hq * 256:(hq + 1) * 256][None, :], (128, 256))),
        "nab": _nab_tables(inp["rpb"][0], [2 * hq, 2 * hq + 1]),
        "trif": tri,
        "trib": np.ascontiguousarray(tri.T),
        "rmask": np.ascontiguousarray(np.broadcast_to(np.eye(4, dtype=np.float32)[hq][None, :], (128, 4))),
    }


def _inputs_h2(inp, j, mixr=None):
    b, q = divmod(j, 4)
    t0 = q * 2048
    x = inp["x"][b]
    xtok = np.zeros((2050, 2048), np.float32)
    xtok[0:2048] = x[t0:t0 + 2048]
    if t0 > 0:
        xtok[2048] = x[t0 - 1]
    if t0 + 2048 < NTOK:
        xtok[2049] = x[t0 + 2048]
    flags = np.zeros((128, 2), np.float32)
    flags[:, 0] = 1.0 if t0 > 0 else 0.0
    flags[:, 1] = 1.0 if t0 + 2048 < NTOK else 0.0
    rows = []
    for r in range(4):
        rows += list(range(r * 256, (r + 1) * 256))
        rows += list(range(1024 + 2 * r * 128, 1024 + (2 * r + 2) * 128))
    w_up = inp["w_up"][0]
    wup = np.empty((44, 128, 16, 256), np.float32)
    wu = w_up[:, 0:5632].reshape(16, 128, 44, 128)
    wg = w_up[:, 5632:].reshape(16, 128, 44, 128)
    wup[:, :, :, 0:128] = wu.transpose(2, 1, 0, 3)
    wup[:, :, :, 128:256] = wg.transpose(2, 1, 0, 3)
    wd = inp["w_down"][0].reshape(2, 22, 128, 4, 512)
    wdn = np.ascontiguousarray(wd.transpose(3, 0, 2, 1, 4))
    d = {
        "ccol": _col(inp["c"][b]),
        "ident": np.eye(128, dtype=np.float32),
        "xtok": xtok,
        "flags": flags,
        "wada2": np.ascontiguousarray(inp["w_ada"][0][:, 4096:]),
        "bada2": np.ascontiguousarray(inp["b_ada"][0][None, 4096:]),
        "wout": np.ascontiguousarray(inp["w_out"][0][rows, :]),
        "g2col": _col(inp["g_norm2"][0]),
        "wup": wup,
        "convw": np.ascontiguousarray(inp["conv_w"][0].reshape(3, 44, 128).transpose(2, 1, 0)),
        "convb": np.ascontiguousarray(inp["conv_b"][0].reshape(44, 128).T),
        "wdn": wdn,
        "gfin": np.ascontiguousarray(np.broadcast_to(inp["g_final"][None, :], (128, 2048))),
    }
    if mixr is not None:
        d["mixr"] = mixr
    return d


MODE = "fused"
_NC_CACHE = {}


def _get_nc(mode):
    if mode not in _NC_CACHE:
        _NC_CACHE[mode] = build(mode)
    return _NC_CACHE[mode]


def run_h1(inp, cores=range(8)):
    nc = _get_nc("h1")
    cores = list(cores)
    maps = [_inputs_h1(inp, j) for j in cores]
    res = run_bass_kernel_spmd(nc, maps, core_ids=list(range(len(cores))))
    return [np.asarray(r["mixs"]) for r in res.results]


def exchange(mixs):
    out = []
    for j in range(8):
        b, q = divmod(j, 4)
        out.append(np.ascontiguousarray(np.concatenate([mixs[b * 4 + r][q, r] for r in range(4)], axis=0)))
    return out


def run_h2(inp, mixr):
    nc = _get_nc("h2")
    maps = [_inputs_h2(inp, j, mixr[j]) for j in range(8)]
    res = run_bass_kernel_spmd(nc, maps, core_ids=list(range(8)))
    return [np.asarray(r["out"]) for r in res.results]


def kernel(**inputs):
    inp = {k_: np.asarray(v) for k_, v in inputs.items()}
    if MODE == "fused":
        nc = _get_nc("fused")
        maps = []
        for j in range(8):
            d = _inputs_h1(inp, j)
            d.update(_inputs_h2(inp, j))
            maps.append(d)
        res = run_bass_kernel_spmd(nc, maps, core_ids=list(range(8)))
        outs = [np.asarray(r["out"]) for r in res.results]
    else:
        outs = run_h2(inp, exchange(run_h1(inp)))
    out = np.stack(outs, 0).reshape(2, NTOK, 2048)
    return out.astype(np.float32)
```

```python
import contextlib
import numpy as np
import concourse.bass as bass
import concourse.mybir as mybir
from concourse.bass_utils import run_bass_kernel_spmd

F32 = mybir.dt.float32
BF16 = mybir.dt.bfloat16
AF = mybir.ActivationFunctionType
ALU = mybir.AluOpType
AX = mybir.AxisListType
EPS = 1e-6
ENGS = ("pe", "act", "dve", "pool", "sp")


class _Op:
    __slots__ = ("eng", "fn", "deps", "ms", "val", "dsem", "idx")

    def __init__(self, eng, fn, dsem, idx):
        self.eng, self.fn, self.dsem, self.idx = eng, fn, dsem, idx
        self.deps, self.ms, self.val = [], False, None


class Prog:
    stopped = False
    pool = {}

    def __init__(self, nc, tag):
        self.nc, self.tag = nc, tag
        self.ops, self.lastw, self.readers, self.dsems = [], {}, {}, {}

    def op(self, eng, fn, reads=(), writes=(), dsem=None):
        if Prog.stopped:
            return None
        o = _Op(eng, fn, dsem, len(self.ops))
        deps = {}
        for r in reads:
            w = self.lastw.get(r)
            if w is not None:
                deps[w.idx] = w
        for r in writes:
            w = self.lastw.get(r)
            if w is not None:
                deps[w.idx] = w
            for rd in self.readers.get(r, ()):
                deps[rd.idx] = rd
        for d in deps.values():
            if d.eng == "pe" and eng == "pe" and d.dsem is None and dsem is None:
                continue
            o.deps.append(d)
            d.ms = True
        for r in writes:
            self.lastw[r] = o
            self.readers[r] = []
        for r in reads:
            if r not in writes:
                self.readers.setdefault(r, []).append(o)
        if dsem is not None:
            self.dsems.setdefault(dsem, 0)
        self.ops.append(o)
        return o

    def emit(self):
        if Prog.stopped:
            return
        nc = self.nc
        G = Prog.pool.setdefault(id(nc), {"es": {}, "ec": {e: 0 for e in ENGS}, "slots": [], "sc": [], "cls": []})
        for e in ENGS:
            if e not in G["es"]:
                G["es"][e] = nc.alloc_semaphore(f"s_{e}")
        qcls = {}
        for o in self.ops:
            if o.dsem is not None:
                c_ = "sw" if o.eng == "pool" else "hw"
                assert qcls.setdefault(o.dsem, c_) == c_, o.dsem
        slot, used = {}, set()
        for kname in self.dsems:
            c_ = qcls[kname]
            i = next((i for i in range(len(G["slots"])) if G["cls"][i] == c_ and i not in used), None)
            if i is None:
                i = len(G["slots"])
                G["slots"].append(nc.alloc_semaphore(f"d_{c_}{i}"))
                G["sc"].append(0)
                G["cls"].append(c_)
            used.add(i)
            slot[kname] = i
        per = {e: [o for o in self.ops if o.eng == e] for e in ENGS}
        for e in ENGS:
            if per[e]:
                per[e][-1].ms = True
        cnt = dict(G["ec"])
        dcnt = {kname: G["sc"][i] for kname, i in slot.items()}
        for o in self.ops:
            if o.dsem is not None:
                dcnt[o.dsem] += 16
                o.val = dcnt[o.dsem]
            elif o.ms:
                cnt[o.eng] += 1
                o.val = cnt[o.eng]
        esem = G["es"]
        dsem = {kname: G["slots"][i] for kname, i in slot.items()}

        def run(e, engobj):
            waited = {}
            for o in per[e]:
                for d in o.deps:
                    if d.dsem is not None:
                        key, sem = ("d", d.dsem), dsem[d.dsem]
                    else:
                        key, sem = ("e", d.eng), esem[d.eng]
                    if waited.get(key, 0) < d.val:
                        engobj.wait_ge(sem, d.val)
                        waited[key] = d.val
                ins = o.fn()
                if o.dsem is not None:
                    ins.then_inc(dsem[o.dsem], 16)
                elif o.ms:
                    ins.then_inc(esem[e], 1)
            for kname, v in dcnt.items():
                if v > G["sc"][slot[kname]] and waited.get(("d", kname), 0) < v:
                    engobj.wait_ge(dsem[kname], v)
            for e2 in ENGS:
                if cnt[e2] > G["ec"][e2] and waited.get(("e", e2), 0) < cnt[e2]:
                    engobj.wait_ge(esem[e2], cnt[e2])

        with nc.Block() as block:
            block.tensor(lambda eng: run("pe", eng))
            block.scalar(lambda eng: run("act", eng))
            block.vector(lambda eng: run("dve", eng))
            block.gpsimd(lambda eng: run("pool", eng))
            block.sync(lambda eng: run("sp", eng))
        for kname, i in slot.items():
            G["sc"][i] = dcnt[kname]
        G["ec"] = cnt
        nc.all_engine_barrier()


class K:
    def __init__(self, nc):
        self.nc = nc
        self.v = {"dve": nc.vector, "pool": nc.gpsimd}
        self.q = {"sp": nc.sync, "pool": nc.gpsimd, "act": nc.scalar}

    def mm(self, out, lhsT, rhs, start=True, stop=True):
        return lambda: self.nc.tensor.matmul(out, lhsT, rhs, start=start, stop=stop)

    def tr(self, out, in_, ident):
        return lambda: self.nc.tensor.transpose(out, in_, ident)

    def act(self, out, in_, func, bias=None, scale=None, accum=None):
        kw = {}
        if bias is not None:
            kw["bias"] = bias
        if scale is not None:
            kw["scale"] = scale
        if accum is not None:
            kw["accum_out"] = accum
        return lambda: self.nc.scalar.activation(out, in_, func, **kw)

    def dma(self, q, out, in_, slow=False):
        if slow:
            return lambda: self.q[q].dma_start(out=out, in_=in_, allow_slow_non_contiguous=True)
        return lambda: self.q[q].dma_start(out=out, in_=in_)

    def ts(self, e, out, in0, s1, s2, op0, op1=None):
        if op1 is None:
            return lambda: self.v[e].tensor_scalar(out, in0, s1, None, op0)
        return lambda: self.v[e].tensor_scalar(out, in0, s1, s2, op0, op1)

    def tt(self, e, out, in0, in1, op):
        return lambda: self.v[e].tensor_tensor(out, in0, in1, op)

    def stt(self, e, out, in0, scalar, in1, op0, op1):
        return lambda: self.v[e].scalar_tensor_tensor(out, in0, scalar, in1, op0, op1)

    def cp(self, e, out, in_):
        return lambda: self.v[e].tensor_copy(out, in_)

    def ms(self, e, ap, c):
        return lambda: self.v[e].memset(ap, c)

    def rcp(self, out, in_):
        return lambda: self.nc.vector.reciprocal(out, in_)

    def rmax(self, out, in_):
        return lambda: self.nc.vector.reduce_max(out, in_, AX.X)


NTOK = 8192
TRAILER = "none"
STOP_AFTER = ""


class _Stop(Exception):
    pass


def _stop(P, tag):
    if STOP_AFTER == tag:
        P.emit()
        Prog.stopped = True
NT = 64
NWIN = 2052
SCALE_B = 128 ** -0.5


def build(mode):
    Prog.stopped = False
    nc = bass.Bass("TRN2", target_bir_lowering=False)
    k = K(nc)
    dt = lambda name, shape, dty, kind: nc.dram_tensor(name, shape, dty, kind=kind).ap()
    IN, OUT, INT = "ExternalInput", "ExternalOutput", "Internal"
    do1 = mode in ("h1", "fused")
    do2 = mode in ("h2", "fused")
    D = {}
    D["ccol"] = dt("ccol", [128, 16], F32, IN)
    D["ident"] = dt("ident", [128, 128], F32, IN)
    if do1:
        D["xb"] = dt("xb", [NTOK, 2048], F32, IN)
        D["wada1"] = dt("wada1", [2048, 4096], F32, IN)
        D["bada1"] = dt("bada1", [1, 4096], F32, IN)
        D["g1col"] = dt("g1col", [128, 16], F32, IN)
        D["win"] = dt("win", [2048, NWIN], F32, IN)
        D["bg"] = dt("bg", [128, 4], F32, IN)
        D["gha"] = dt("gha", [128, 256], F32, IN)
        D["nab"] = dt("nab", [2, 128, 3200], F32, IN)
        D["trif"] = dt("trif", [128, 128], F32, IN)
        D["trib"] = dt("trib", [128, 128], F32, IN)
        D["qAT"] = dt("qAT_d", [NT, 128, 2, 128], BF16, INT)
        D["kAT"] = dt("kAT_d", [NT, 128, 2, 128], BF16, INT)
        D["qBT"] = dt("qBT_d", [2, NT, 128, 128], BF16, INT)
        D["kBT"] = dt("kBT_d", [2, NT, 128, 128], BF16, INT)
        D["vA"] = dt("vA_d", [NT, 128, 257], BF16, INT)
        D["og"] = dt("og_d", [NT, 128, 256], F32, INT)
        D["ktok"] = dt("ktok_d", [NT, 128, 256], BF16, INT)
        D["vB"] = dt("vB_d", [NT, 128, 256], BF16, INT)
        D["hf"] = dt("hf_d", [NT, 128, 256], F32, INT)
        D["mixs"] = dt("mixs", [4, 4, 512, 2050], BF16, OUT if mode == "h1" else INT)
        D["rmask"] = dt("rmask", [128, 4], F32, IN)
    if do2:
        D["mixr"] = dt("mixr", [2048, 2050], BF16, IN if mode == "h2" else INT)
        D["xtok"] = dt("xtok", [2050, 2048], F32, IN)
        D["flags"] = dt("flags", [128, 2], F32, IN)
        D["wada2"] = dt("wada2", [2048, 8192], F32, IN)
        D["bada2"] = dt("bada2", [1, 8192], F32, IN)
        D["wout"] = dt("wout", [2048, 2048], F32, IN)
        D["g2col"] = dt("g2col", [128, 16], F32, IN)
        D["wup"] = dt("wup", [44, 128, 16, 256], F32, IN)
        D["convw"] = dt("convw", [128, 44, 3], F32, IN)
        D["convb"] = dt("convb", [128, 44], F32, IN)
        D["wdn"] = dt("wdn", [4, 2, 128, 22, 512], F32, IN)
        D["gfin"] = dt("gfin", [128, 2048], F32, IN)
        D["h2T"] = dt("h2T_d", [128, 16, 2050], BF16, INT)
        D["x1"] = dt("x1_d", [2048, 2048], F32, INT)
        D["z"] = dt("z_d", [2048, 2048], F32, INT)
        D["out"] = dt("out", [2048, 2048], F32, OUT)

    with contextlib.ExitStack() as top:
        gsb = lambda name, shape, dty: top.enter_context(nc.sbuf_tensor(name, shape, dty))
        ident_f = gsb("ident_f", [128, 128], F32)
        ident_b = gsb("ident_b", [128, 128], BF16)
        ones_f = gsb("ones_f", [128, 128], F32)
        ones_b = gsb("ones_b", [128, 128], BF16)
        gates_sb = gsb("gates_sb", [128, NT, 4], F32)
        ga2_bc = gsb("ga2_bc", [128, 2048], F32)
        ga1_bcP = gsb("ga1_bcP", [128, 2048], F32)
        Gc2P = gsb("Gc2P", [128, 16], F32)
        shc2P = gsb("shc2P", [128, 16], F32)

        def consts(P):
            P.op("sp", k.dma("sp", ident_f[:], D["ident"]), writes=["ident_f"], dsem="c_idf")
            P.op("pool", k.dma("pool", ident_b[:], D["ident"]), writes=["ident_b"], dsem="c_idb")
            P.op("dve", k.ms("dve", ones_f[:], 1.0), writes=["ones_f"])
            P.op("dve", k.ms("dve", ones_b[:], 1.0), writes=["ones_b"])

        def mk_modrows(P, sb, psb, tag, nmax):
            cc = sb("cc" + tag, [128, 16], F32)
            scb = sb("scb" + tag, [128, 16], BF16)
            badar = sb("badar" + tag, [1, nmax], F32)
            wst = [sb(f"wst{tag}{i}", [128, 16, 256], BF16) for i in range(2)]
            P.op("sp", k.dma("sp", cc[:], D["ccol"]), writes=["cc"], dsem="cc")
            P.op("act", k.act(scb[:], cc[:], AF.Silu), reads=["cc"], writes=["scb"])
            state = {"i": 0}

            def run(wada, bada, c0, ncols, modrow):
                P.op("sp", k.dma("sp", badar[0:1, 0:ncols], bada[:, c0:c0 + ncols]), writes=["badar"], dsem="badar")
                for cb in range(ncols // 256):
                    b = state["i"] % 2
                    state["i"] += 1
                    src = wada[:, c0 + cb * 256:c0 + (cb + 1) * 256].rearrange("(k p) n -> p k n", p=128)
                    P.op("pool", k.dma("pool", wst[b][:], src), writes=[f"wst{b}"], dsem=f"wst{b}")
                    for kk in range(16):
                        P.op("pe", k.mm(psb[0:1, 0:256], scb[:, kk:kk + 1], wst[b][:, kk, :], kk == 0, kk == 15),
                             reads=["scb", f"wst{b}"], writes=["pmisc"])
                    P.op("dve", k.tt("dve", modrow[0:1, cb * 256:(cb + 1) * 256], psb[0:1, 0:256],
                                     badar[0:1, cb * 256:(cb + 1) * 256], ALU.add),
                         reads=["pmisc", "badar"], writes=["modrow"])
            return run

        def row2col(P, psb, modrow, off, col0, res):
            for kk in range(16):
                P.op("pe", k.mm(psb[:, col0 + kk:col0 + kk + 1], modrow[0:1, off + kk * 128:off + (kk + 1) * 128],
                                ones_f[0:1, 0:1]), reads=["modrow", "modrow2", "ones_f"], writes=[res])

        def norm_units(P, xt, xs, M, ss, sd, rstd, junk, tpb, Gc, shc, dst_fn, rx, rdst, tag=""):
            rxs = rx if xs is xt else "xs_shared"

            def u0():
                P.op("act", k.act(junk[0:M, :], xt[0:M, :], AF.Square, accum=ss[0:M, 0:1]), reads=[rx], writes=["junk", "ss" + tag])
                P.op("act", k.act(sd[0:M, 0:1], ss[0:M, 0:1], AF.Sqrt, bias=EPS, scale=1.0 / 2048), reads=["ss" + tag], writes=["sd" + tag])
                P.op("dve", k.rcp(rstd[0:M, 0:1], sd[0:M, 0:1]), reads=["sd" + tag], writes=["rstd" + tag])
                P.op("dve", k.ts("dve", xs[0:M, :], xt[0:M, :], rstd[0:M, 0:1], None, ALU.mult), reads=[rx, "rstd" + tag], writes=[rxs])

            def ug(g):
                def f():
                    bank = tpb[g % 2]
                    for j in range(4):
                        kk = g * 4 + j
                        P.op("pe", k.tr(bank[:, j * 128:j * 128 + M], xs[0:M, kk * 128:(kk + 1) * 128], ident_f[0:M, 0:M]),
                             reads=[rxs, "ident_f"], writes=[f"tp{g % 2}"])
                    for j in range(4):
                        kk = g * 4 + j
                        src = bank[:, j * 128:j * 128 + M]
                        if g % 2 == 0:
                            P.op("act", k.act(dst_fn(kk), src, AF.Identity, bias=shc[:, kk:kk + 1], scale=Gc[:, kk:kk + 1]),
                                 reads=[f"tp{g % 2}", "Gc"], writes=[rdst(kk)])
                        else:
                            P.op("dve", k.ts("dve", dst_fn(kk), src, Gc[:, kk:kk + 1], shc[:, kk:kk + 1], ALU.mult, ALU.add),
                                 reads=[f"tp{g % 2}", "Gc"], writes=[rdst(kk)])
                return f
            return [u0] + [ug(g) for g in range(4)]

        def norm_T(P, xt, xs, M, ss, sd, rstd, junk, tpb, Gc, shc, dst_fn, rx, rdst, ev_i):
            for u in norm_units(P, xt, xs, M, ss, sd, rstd, junk, tpb, Gc, shc, dst_fn, rx, rdst):
                u()

        if do1:
            with contextlib.ExitStack() as st:
                sb = lambda name, shape, dty: st.enter_context(nc.sbuf_tensor(name, shape, dty))
                psum = lambda name, dty=F32, n=512: st.enter_context(nc.psum_tensor(name, [128, n], dty))
                P = Prog(nc, "a")
                consts(P)
                tpb = [psum("tp0"), psum("tp1")]
                fmb = [psum("fm0"), psum("fm1")]
                tmb = [psum("tm0"), psum("tm1")]
                pmisc = psum("pmisc")
                Wb = sb("Wb", [128, 16, NWIN], BF16)
                P.op("pool", k.dma("pool", Wb[:], D["win"].rearrange("(k p) n -> p k n", p=128)), writes=["Wb"], dsem="Wb")
                modrow = sb("modrow", [1, 4096], F32)
                mk_modrows(P, sb, pmisc, "1", 4096)(D["wada1"], D["bada1"], 0, 4096, modrow)
                g1c = sb("g1c", [128, 16], F32)
                P.op("sp", k.dma("sp", g1c[:], D["g1col"]), writes=["g1c"], dsem="g1c")
                row2col(P, pmisc, modrow, 2048, 256, "pmisc")
                row2col(P, pmisc, modrow, 0, 272, "pmisc")
                Gc = sb("Gc", [128, 16], F32)
                shc = sb("shc", [128, 16], F32)
                P.op("dve", k.ts("dve", Gc[:], pmisc[:, 256:272], 1.0, None, ALU.add), reads=["pmisc"], writes=["Gc0"])
                P.op("dve", k.tt("dve", Gc[:], Gc[:], g1c[:], ALU.mult), reads=["Gc0", "g1c"], writes=["Gc"])
                P.op("dve", k.cp("dve", shc[:], pmisc[:, 272:288]), reads=["pmisc"], writes=["Gc"])
                bgs = sb("bgs", [128, 4], F32)
                ghas = sb("ghas", [128, 256], F32)
                P.op("sp", k.dma("sp", ghas[:], D["gha"]), writes=["ghas"], dsem="ghas")

                xts = [sb(f"xt{i}", [128, 2048], F32) for i in range(2)]
                hT = [sb(f"hT{i}", [128, 16, 512], BF16) for i in range(2)]
                junk = sb("junk", [128, 2048], BF16)
                ss = sb("ss", [128, 1], F32)
                sd = sb("sd", [128, 1], F32)
                rstd = sb("rstd", [128, 1], F32)
                fmst = [sb(f"fmst{i}", [128, 512], BF16) for i in range(4)]
                vst = [sb(f"vst{i}", [128, 257], BF16) for i in range(2)]
                ogs = [sb(f"ogs{i}", [128, 256], F32) for i in range(2)]
                kts = [sb(f"kts{i}", [128, 256], BF16) for i in range(2)]
                vbs = [sb(f"vbs{i}", [128, 256], BF16) for i in range(2)]
                for i in range(2):
                    P.op("dve", k.ms("dve", vst[i][:, 256:257], 1.0), writes=[f"vst{i}"])

                def load_x(t):
                    P.op("sp", k.dma("sp", xts[t % 2][:], D["xb"][t * 128:(t + 1) * 128, :]), writes=[f"xt{t % 2}"], dsem=f"xt{t % 2}")

                ssL = [sb(f"ssL{i}", [128, 1], F32) for i in range(2)]
                sdL = [sb(f"sdL{i}", [128, 1], F32) for i in range(2)]
                rstdL = [sb(f"rstdL{i}", [128, 1], F32) for i in range(2)]

                def prep_units(tg):
                    us = []
                    g = tg % 2
                    for ti in range(4):
                        t = tg * 4 + ti
                        xt = xts[t % 2]
                        tu = norm_units(P, xt, xt, 128, ssL[t % 2], sdL[t % 2], rstdL[t % 2], junk, tpb, Gc, shc,
                                        lambda kk, g=g, ti=ti: hT[g][:, kk, ti * 128:(ti + 1) * 128], f"xt{t % 2}",
                                        (lambda kk, g=g, ti=ti: f"hT{g}_{ti}_{kk}"), tag=str(t % 2))
                        us.append(tu[0])
                        us.append(tu[1])
                        u2 = tu[2]
                        us.append((lambda t=t, u2=u2: (load_x(t + 1), u2())) if t + 1 < NT else u2)
                        us.extend(tu[3:])
                    return us

                pending_units = []

                def pop_unit():
                    if pending_units:
                        pending_units.pop(0)()

                fmi = 0
                load_x(0)
                for u in prep_units(0):
                    u()
                for tg in range(16):
                    g = tg % 2
                    pending_units = prep_units(tg + 1) if tg + 1 < 16 else []
                    for fc in range(8):
                        bank = fmb[fc % 2]
                        for kk in range(16):
                            P.op("pe", k.mm(bank[:, 0:512], Wb[:, kk, fc * 128:(fc + 1) * 128], hT[g][:, kk, :], kk == 0, kk == 15),
                                 reads=["Wb"] + [f"hT{g}_{ti_}_{kk}" for ti_ in range(4)], writes=[f"fm{fc % 2}"])
                        fb = fmi % 4
                        fmi += 1
                        scl = 0.0625 if fc in (2, 3) else 1.0
                        if fc % 2 == 0:
                            P.op("act", k.act(fmst[fb][:], bank[:, 0:512], AF.Copy, scale=scl), reads=[f"fm{fc % 2}"], writes=[f"fmst{fb}"])
                        else:
                            P.op("dve", k.ts("dve", fmst[fb][:], bank[:, 0:512], scl, None, ALU.mult), reads=[f"fm{fc % 2}"], writes=[f"fmst{fb}"])
                        src = fmst[fb][:].rearrange("p (c t) -> p c t", c=4)
                        c0 = tg * 4
                        if fc < 4:
                            arr = D["qAT"] if fc < 2 else D["kAT"]
                            dst = arr[c0:c0 + 4, :, fc % 2, :].rearrange("c p t -> p c t")
                        else:
                            arr = D["qBT"] if fc % 2 == 0 else D["kBT"]
                            dst = arr[(fc - 4) // 2, c0:c0 + 4, :, :].rearrange("c p t -> p c t")
                        P.op("sp", k.dma("sp", dst, src), reads=[f"fmst{fb}"], writes=[f"d_fm{fc}"], dsem=f"fmst{fb}")
                        pop_unit()
                    for ti in range(4):
                        t = tg * 4 + ti
                        b = t % 2
                        lhs = lambda kk: hT[g][:, kk, ti * 128:(ti + 1) * 128]
                        for kk in range(16):
                            P.op("pe", k.mm(tmb[0][:, 0:512], lhs(kk), Wb[:, kk, 1024:1536], kk == 0, kk == 15),
                                 reads=["Wb", f"hT{g}_{ti}_{kk}"], writes=["tm0"])
                        pop_unit()
                        for kk in range(16):
                            P.op("pe", k.mm(tmb[1][:, 0:512], lhs(kk), Wb[:, kk, 1536:2048], kk == 0, kk == 15),
                                 reads=["Wb", f"hT{g}_{ti}_{kk}"], writes=["tm1"])
                        pop_unit()
                        for kk in range(16):
                            P.op("pe", k.mm(pmisc[:, 0:4], lhs(kk), Wb[:, kk, 2048:2052], kk == 0, kk == 15),
                                 reads=["Wb", f"hT{g}_{ti}_{kk}"], writes=["pmisc"])
                        pop_unit()
                        P.op("act", k.act(vst[b][:, 0:256], tmb[0][:, 0:256], AF.Copy), reads=["tm0"], writes=[f"vst{b}"])
                        P.op("act", k.act(ogs[b][:], tmb[0][:, 256:512], AF.Sigmoid), reads=["tm0"], writes=[f"ogs{b}"])
                        P.op("pool", k.tt("pool", ogs[b][:], ogs[b][:], ghas[:], ALU.mult), reads=["ghas"], writes=[f"ogs{b}"])
                        P.op("dve", k.ts("dve", kts[b][:], tmb[1][:, 0:256], 0.0625, None, ALU.mult), reads=["tm1"], writes=[f"kts{b}"])
                        P.op("dve", k.cp("dve", vbs[b][:], tmb[1][:, 256:512]), reads=["tm1"], writes=[f"vbs{b}"])
                        P.op("dve", k.cp("dve", gates_sb[:, t, :], pmisc[:, 0:4]), reads=["pmisc"], writes=["gates"])
                        P.op("sp", k.dma("sp", D["vA"][t], vst[b][:]), reads=[f"vst{b}"], writes=["d_vA"], dsem=f"vst{b}")
                        P.op("sp", k.dma("sp", D["og"][t], ogs[b][:]), reads=[f"ogs{b}"], writes=["d_og"], dsem=f"ogs{b}")
                        P.op("sp", k.dma("sp", D["ktok"][t], kts[b][:]), reads=[f"kts{b}"], writes=["d_ktok"], dsem=f"kts{b}")
                        P.op("sp", k.dma("sp", D["vB"][t], vbs[b][:]), reads=[f"vbs{b}"], writes=["d_vB"], dsem=f"vbs{b}")
                    while pending_units:
                        pop_unit()
                P.emit()

            with contextlib.ExitStack() as st:
              if STOP_AFTER != "a":
                  sb = lambda name, shape, dty: st.enter_context(nc.sbuf_tensor(name, shape, dty))
                  psum = lambda name, dty=F32, n=512: st.enter_context(nc.psum_tensor(name, [128, n], dty))
                  P = Prog(nc, "b")
                  pA = psum("pA")
                  pS = [psum("pS0"), psum("pS1")]
                  pN = [psum("pN0"), psum("pN1")]
                  pK = [psum("pK0"), psum("pK1")]
                  pT = psum("pT", BF16, 1024)
                  if STOP_AFTER == "bY":
                      tmpo = sb("tmpo", [128, 64], F32)
                      P.op("pe", k.mm(pA[:, 0:64], ones_f[:], ident_f[:, 0:64]), reads=["ones_f"], writes=["pA"])
                      P.op("dve", k.cp("dve", tmpo[:], pA[:, 0:64]), reads=["pA"], writes=["tmpo"])
                  _stop(P, "bY")
                  _stop(P, "bX")
                  bgs = sb("bgs2", [128, 4], F32)
                  P.op("sp", k.dma("sp", bgs[:], D["bg"]), writes=["bgs"], dsem="bgs")
                  masks = [sb("maskf", [128, 128], F32), sb("maskb", [128, 128], F32)]
                  P.op("sp", k.dma("sp", masks[0][:], D["trif"]), writes=["mask0"], dsem="mask0")
                  P.op("sp", k.dma("sp", masks[1][:], D["trib"]), writes=["mask1"], dsem="mask1")
                  zeros_b = sb("zeros_b", [128, 16, 1], BF16)
                  rmask = sb("rmask_s", [128, 4], F32)
                  P.op("sp", k.dma("sp", rmask[:], D["rmask"]), writes=["rmask"], dsem="rmask")
                  P.op("dve", k.ms("dve", zeros_b[:], 0.0), writes=["zeros_b"])
                  P.op("sp", k.dma("sp", D["mixs"][0, :, :, 0:1].rearrange("r (k p) o -> p (r k) o", p=128), zeros_b[:], slow=True), reads=["zeros_b"], dsem="zp0")
                  P.op("sp", k.dma("sp", D["mixs"][3, :, :, 2049:2050].rearrange("r (k p) o -> p (r k) o", p=128), zeros_b[:], slow=True), reads=["zeros_b"], dsem="zp1")
                  _stop(P, "b0")
                  wv, flv, decv = [], [], []
                  sm = lambda name: sb(name, [128, 64], F32)
                  for di in range(2):
                      gi, gf = 2 * di, 2 * di + 1
                      e = "dve"
                      zf, li, az, ez, lz, mz, lf, u, bc = [sm(f"{n}{di}") for n in ("zf", "li", "az", "ez", "lz", "mz", "lf", "u", "bc")]
                      w_, fl_, dec_ = sm(f"w{di}"), sm(f"fl{di}"), sm(f"dec{di}")
                      t1, t2 = sm(f"t1{di}"), sm(f"t2{di}")
                      rows = sb(f"rows{di}", [1, 6, 64], F32)
                      umc = sb(f"umc{di}", [64, 1], F32)
                      R = lambda n: f"{n}{di}"
                      P.op(e, k.ts(e, zf[:], gates_sb[:, :, gf], bgs[:, gf:gf + 1], None, ALU.add), reads=["gates", "bgs"], writes=[R("zf")])
                      P.op(e, k.ts(e, li[:], gates_sb[:, :, gi], bgs[:, gi:gi + 1], None, ALU.add), reads=["gates", "bgs"], writes=[R("li")])
                      P.op("act", k.act(az[:], zf[:], AF.Abs), reads=[R("zf")], writes=[R("az")])
                      P.op("act", k.act(ez[:], az[:], AF.Exp, scale=-1.0), reads=[R("az")], writes=[R("ez")])
                      P.op("act", k.act(lz[:], ez[:], AF.Ln, bias=1.0), reads=[R("ez")], writes=[R("lz")])
                      P.op(e, k.ts(e, mz[:], zf[:], 0.0, None, ALU.min), reads=[R("zf")], writes=[R("mz")])
                      P.op(e, k.tt(e, lf[:], mz[:], lz[:], ALU.subtract), reads=[R("mz"), R("lz")], writes=[R("lf")])
                      if di == 0:
                          _stop(P, "b1_1")
                      P.op("pe", k.mm(pA[:, 0:64], masks[di][:], lf[:]), reads=[R("lf"), f"mask{di}"], writes=["pA"])
                      P.op("pe", k.mm(pA[:, 64:128], ones_f[:], lf[:]), reads=[R("lf"), "ones_f"], writes=["pA"])
                      P.op(e, k.cp(e, bc[:], pA[:, 0:64]), reads=["pA"], writes=[R("bc")])
                      P.op(e, k.tt(e, u[:], li[:], bc[:], ALU.subtract), reads=[R("li"), R("bc")], writes=[R("u")])
                      P.op(e, k.cp(e, rows[0:1, 1, :], pA[0:1, 64:128]), reads=["pA"], writes=[R("gsum")])
                      if di == 0:
                          _stop(P, "b1_1a")
                      P.op("pe", k.tr(pA[0:64, 128:256], u[:], ident_f[:]), reads=[R("u"), "ident_f"], writes=["pA"])
                      P.op(e, k.rmax(umc[:], pA[0:64, 128:256]), reads=["pA"], writes=[R("umc")])
                      if di == 0:
                          _stop(P, "b1_1b")
                      P.op("pe", k.mm(pA[0:1, 256:320], umc[:], ident_f[0:64, 0:64]), reads=[R("umc"), "ident_f"], writes=["pA"])
                      P.op(e, k.cp(e, rows[0:1, 0, :], pA[0:1, 256:320]), reads=["pA"], writes=[R("umax")])
                      se = "dve"
                      if di == 0:
                          _stop(P, "b1_2")
                      sc = sb(f"scan{di}", [1, 4, 64], F32)
                      tmp = sb(f"scant{di}", [1, 64], F32)
                      P.op(se, k.cp(se, sc[0:1, 0, :], rows[0:1, 1, :]), reads=[R("gsum")], writes=[R("scan")])
                      P.op(se, k.tt(se, sc[0:1, 1, :], rows[0:1, 0, :], rows[0:1, 1, :], ALU.add), reads=[R("umax"), R("gsum")], writes=[R("scan")])
                      cur, nxt = 0, 2
                      for d_ in (1, 2, 4, 8, 16, 32):
                          n_ = 64 - d_
                          if di == 0:
                              lo, hi = slice(0, n_), slice(d_, 64)
                          else:
                              lo, hi = slice(d_, 64), slice(0, n_)
                          Gc_, Hc_, Gn_, Hn_ = sc[0:1, cur, :], sc[0:1, cur + 1, :], sc[0:1, nxt, :], sc[0:1, nxt + 1, :]
                          P.op(se, k.cp(se, sc[0:1, nxt:nxt + 2, :], sc[0:1, cur:cur + 2, :]), writes=[R("scan")])
                          P.op(se, k.tt(se, tmp[0:1, 0:n_], sc[0:1, cur + 1, lo], sc[0:1, cur, hi], ALU.add), writes=[R("scan")])
                          P.op(se, k.tt(se, sc[0:1, nxt + 1, hi], tmp[0:1, 0:n_], sc[0:1, cur + 1, hi], ALU.max), writes=[R("scan")])
                          P.op(se, k.tt(se, sc[0:1, nxt, hi], sc[0:1, cur, lo], sc[0:1, cur, hi], ALU.add), writes=[R("scan")])
                          cur, nxt = nxt, cur
                      P.op(se, k.ms(se, rows[0:1, 2, :], -1e30), writes=[R("scan")])
                      if di == 0:
                          P.op(se, k.cp(se, rows[0:1, 2, 1:64], sc[0:1, cur + 1, 0:63]), writes=[R("scan")])
                      else:
                          P.op(se, k.cp(se, rows[0:1, 2, 0:63], sc[0:1, cur + 1, 1:64]), writes=[R("scan")])
                      P.op(se, k.tt(se, rows[0:1, 3, :], rows[0:1, 2, :], rows[0:1, 0, :], ALU.max), writes=[R("scan")])
                      if di == 0:
                          _stop(P, "b1_3")
                      P.op(e, k.tt(e, rows[0:1, 4, :], rows[0:1, 2, :], rows[0:1, 3, :], ALU.subtract), reads=[R("scan")], writes=[R("dd")])
                      P.op("act", k.act(rows[0:1, 5, :], rows[0:1, 4, :], AF.Exp), reads=[R("dd")], writes=[R("decr")])
                      P.op("pe", k.mm(pA[:, 320:384], ones_f[0:1, :], rows[0:1, 3, :]), reads=[R("scan"), "ones_f"], writes=["pA"])
                      P.op("pe", k.mm(pA[:, 384:448], ones_f[0:1, :], rows[0:1, 5, :]), reads=[R("decr"), "ones_f"], writes=["pA"])
                      P.op(e, k.cp(e, dec_[:], pA[:, 384:448]), reads=["pA"], writes=[R("dec")])
                      P.op(e, k.cp(e, t2[:], pA[:, 320:384]), reads=["pA"], writes=[R("t2")])
                      P.op(e, k.tt(e, t1[:], u[:], t2[:], ALU.subtract), reads=[R("u"), R("t2")], writes=[R("t1")])
                      P.op("act", k.act(w_[:], t1[:], AF.Exp), reads=[R("t1")], writes=[R("w")])
                      P.op(e, k.tt(e, t2[:], bc[:], t2[:], ALU.add), reads=[R("bc")], writes=[R("t2")])
                      P.op("act", k.act(fl_[:], t2[:], AF.Exp, scale=-1.0), reads=[R("t2")], writes=[R("fl")])
                      wv.append(w_)
                      flv.append(fl_)
                      decv.append(dec_)

                  _stop(P, "b1")
                  mod2_steps = []
                  if mode == "fused":
                      cc2 = sb("cc2b", [128, 16], F32)
                      scb2 = sb("scb2b", [128, 16], BF16)
                      badar2 = sb("badar2b", [1, 256], F32)
                      modrow2 = sb("modrow2b", [1, 2048], F32)
                      g2cb = sb("g2cb", [128, 16], F32)
                      wst2 = [sb(f"wst2b{i}", [128, 16, 256], BF16) for i in range(2)]
                      P.op("sp", k.dma("sp", cc2[:], D["ccol"]), writes=["cc2"], dsem="cc2")
                      P.op("sp", k.dma("sp", g2cb[:], D["g2col"]), writes=["g2cb"], dsem="g2cb")
                      P.op("act", k.act(scb2[:], cc2[:], AF.Silu), reads=["cc2"], writes=["scb2"])

                      def mod2_block(part, cb):
                          def f():
                              c0 = part * 2048
                              P.op("sp", k.dma("sp", badar2[0:1, :], D["bada2"][:, c0 + cb * 256:c0 + (cb + 1) * 256]), writes=["badar2"], dsem="badar2")
                              bi = (part * 8 + cb) % 2
                              src = D["wada2"][:, c0 + cb * 256:c0 + (cb + 1) * 256].rearrange("(k p) n -> p k n", p=128)
                              P.op("pool", k.dma("pool", wst2[bi][:], src), writes=[f"wst2_{bi}"], dsem=f"wst2_{bi}")
                              for kk in range(16):
                                  P.op("pe", k.mm(pA[0:1, 0:256], scb2[:, kk:kk + 1], wst2[bi][:, kk, :], kk == 0, kk == 15),
                                       reads=["scb2", f"wst2_{bi}"], writes=["pA"])
                              P.op("dve", k.tt("dve", modrow2[0:1, cb * 256:(cb + 1) * 256], pA[0:1, 0:256], badar2[0:1, 0:256], ALU.add),
                                   reads=["pA", "badar2"], writes=["modrow2"])
                              if cb == 7:
                                  if part in (0, 3):
                                      dst = ga1_bcP if part == 0 else ga2_bc
                                      for nb in range(4):
                                          P.op("pe", k.mm(pA[:, 0:512], ones_f[0:1, :], modrow2[0:1, nb * 512:(nb + 1) * 512]), reads=["modrow2"], writes=["pA"])
                                          P.op("dve", k.cp("dve", dst[:, nb * 512:(nb + 1) * 512], pA[:, 0:512]), reads=["pA"], writes=[f"modout{part}"])
                                  elif part == 1:
                                      row2col(P, pA, modrow2, 0, 272, "pA")
                                      P.op("dve", k.cp("dve", shc2P[:], pA[:, 272:288]), reads=["pA"], writes=["shc2P"])
                                  else:
                                      row2col(P, pA, modrow2, 0, 256, "pA")
                                      P.op("dve", k.ts("dve", Gc2P[:], pA[:, 256:272], 1.0, None, ALU.add), reads=["pA"], writes=["Gc2P0"])
                                      P.op("dve", k.tt("dve", Gc2P[:], Gc2P[:], g2cb[:], ALU.mult), reads=["Gc2P0", "g2cb"], writes=["Gc2P"])
                          return f
                      mod2_steps = [mod2_block(p_, c_) for p_ in range(4) for c_ in range(8)]

                  NG = 2
                  qT4 = [sb(f"qT4_{i}", [128, 4, 2, 128], BF16) for i in range(NG)]
                  kT4 = [sb(f"kT4_{i}", [128, 4, 2, 128], BF16) for i in range(NG)]
                  ktk4 = [sb(f"ktk4_{i}", [128, 4, 256], BF16) for i in range(NG)]
                  vas4 = [sb(f"vas4_{i}", [128, 4, 257], BF16) for i in range(NG)]
                  hfc4 = [sb(f"hfc4_{i}", [128, 4, 256], F32) for i in range(NG)]
                  ogc4 = [sb(f"ogc4_{i}", [128, 4, 256], F32) for i in range(NG)]
                  hst4 = [sb(f"hst4_{i}", [128, 4, 256], F32) for i in range(2)]
                  SPb = [sb(f"SPb{i}", [128, 128], BF16) for i in range(2)]
                  kw = [sb(f"kw{i}", [128, 256], BF16) for i in range(2)]
                  C = sb("C", [128, 2, 257], F32)
                  Cd = sb("Cd", [128, 2, 257], BF16)
                  rrL = [sb(f"rr_{i}", [128, 1], F32) for i in range(2)]
                  rr2L = [sb(f"rr2_{i}", [128, 1], F32) for i in range(2)]
                  hsL = [sb(f"hs_{i}", [128, 256], F32) for i in range(2)]
                  junk2L = [sb(f"junk2_{i}", [128, 256], BF16) for i in range(2)]
                  ss2L = [sb(f"ss2_{i}", [128, 1], F32) for i in range(2)]
                  sd2L = [sb(f"sd2_{i}", [128, 1], F32) for i in range(2)]
                  rmsL = [sb(f"rms_{i}", [128, 1], F32) for i in range(2)]
                  hoL = [sb(f"ho_{i}", [128, 256], BF16) for i in range(2)]
                  mst = [sb(f"mst{i}", [128, 256], BF16) for i in range(2)]
                  MS = [sb(f"MS{i}", [128, 2, 4, 512], BF16) for i in range(2)]
                  NS = [sb(f"NS{i}", [128, 1, 4, 512], BF16) for i in range(2)]
                  evc = [0]

                  def stage_put(bufs, name, nk, f0, cnt, src_fn, tt, rsrc):
                      g, idx = divmod(tt, 4)
                      buf, res = bufs[g % 2], f"{name}{g % 2}"
                      allres = [f"{res}_{kk}_{r_}_{i_}" for kk in range(nk) for r_ in range(4) for i_ in range(4)]
                      for kk in range(nk):
                          for r_ in range(4):
                              dst = buf[:, kk, r_, idx * 128:(idx + 1) * 128]
                              evc[0] += 1
                              wres = [f"{res}_{kk}_{r_}_{idx}"]
                              if evc[0] % 2 == 0:
                                  P.op("act", k.act(dst, src_fn(kk), AF.Copy, scale=rmask[:, r_:r_ + 1]), reads=[rsrc, "rmask"], writes=wres)
                              else:
                                  P.op("dve", k.ts("dve", dst, src_fn(kk), rmask[:, r_:r_ + 1], None, ALU.mult), reads=[rsrc, "rmask"], writes=wres)
                      cnt[g] = cnt.get(g, 0) + 1
                      if cnt[g] == 4:
                          j, g4 = divmod(g, 4)
                          for kk in range(nk):
                              rd = [f"{res}_{kk}_{r_}_{i_}" for r_ in range(4) for i_ in range(4)]
                              dstv = lambda jj, c0, c1, kk=kk: D["mixs"][jj, :, f0 + kk * 128:f0 + (kk + 1) * 128, c0:c1].rearrange("r f t -> f r t")
                              P.op("sp", k.dma("sp", dstv(j, 1 + g4 * 512, 1 + (g4 + 1) * 512), buf[:, kk, :, :]), reads=rd, dsem=res)
                              if g4 == 3 and j < 3:
                                  P.op("sp", k.dma("sp", dstv(j + 1, 0, 1), buf[:, kk, :, 511:512], slow=True), reads=rd, dsem=res)
                              if g4 == 0 and j > 0:
                                  P.op("sp", k.dma("sp", dstv(j - 1, 2049, 2050), buf[:, kk, :, 0:1], slow=True), reads=rd, dsem=res)

                  groups = [(0, list(range(g * 4, g * 4 + 4))) for g in range(16)] + [(1, list(range(g * 4 + 3, g * 4 - 1, -1))) for g in range(15, -1, -1)]

                  def gloads(gi):
                      di, cs = groups[gi]
                      cb_, b = min(cs), gi % NG
                      P.op("sp", k.dma("sp", qT4[b][:], D["qAT"][cb_:cb_ + 4].rearrange("c p k t -> p c k t")), reads=["d_fm0", "d_fm1"], writes=[f"qT4_{b}"], dsem=f"qT4_{b}")
                      P.op("sp", k.dma("sp", kT4[b][:], D["kAT"][cb_:cb_ + 4].rearrange("c p k t -> p c k t")), reads=["d_fm2", "d_fm3"], writes=[f"kT4_{b}"], dsem=f"kT4_{b}")
                      P.op("sp", k.dma("sp", ktk4[b][:], D["ktok"][cb_:cb_ + 4].rearrange("c p e -> p c e")), writes=[f"ktk4_{b}"], dsem=f"ktk4_{b}")
                      P.op("sp", k.dma("sp", vas4[b][:], D["vA"][cb_:cb_ + 4].rearrange("c p e -> p c e")), writes=[f"vas4_{b}"], dsem=f"vas4_{b}")

                  def gloads_h(gi):
                      di, cs = groups[gi]
                      cb_, b = min(cs), gi % NG
                      if di == 1:
                          P.op("sp", k.dma("sp", hfc4[b][:], D["hf"][cb_:cb_ + 4].rearrange("c p e -> p c e")), reads=["d_hf"], writes=[f"hfc4_{b}"], dsem=f"hfc4_{b}")
                          P.op("sp", k.dma("sp", ogc4[b][:], D["og"][cb_:cb_ + 4].rearrange("c p e -> p c e")), writes=[f"ogc4_{b}"], dsem=f"ogc4_{b}")

                  mcnt = {}
                  pend = [None]
                  gloads(0)
                  si = 0
                  for gi, (di, cs) in enumerate(groups):
                      if gi == 16 and pend[0] is not None:
                          pend[0]()
                          pend[0] = None
                      gloads_h(gi)
                      if gi + 1 < len(groups):
                          gloads(gi + 1)
                      if gi < len(mod2_steps):
                          mod2_steps[gi]()
                      cb_, b = min(cs), gi % NG
                      if gi in (0, 16):
                          P.op("dve", k.ms("dve", C[:], 0.0), writes=["C"])
                      for c in cs:
                          ix = c - cb_
                          s2 = si % 2
                          si += 1
                          wc = wv[di][:, c:c + 1]
                          dc = decv[di][:, c:c + 1]
                          rq, rk, rkt, rv = f"qT4_{b}", f"kT4_{b}", f"ktk4_{b}", f"vas4_{b}"
                          for kk in range(2):
                              P.op("pe", k.mm(pS[s2][:, 0:128], kT4[b][:, ix, kk, :], qT4[b][:, ix, kk, :], kk == 0, kk == 1),
                                   reads=[rk, rq], writes=[f"pS{s2}"])
                          P.op("dve", k.stt("dve", SPb[s2][:], pS[s2][:, 0:128], wc, masks[di][:], ALU.mult, ALU.mult),
                               reads=[f"pS{s2}", f"w{di}", f"mask{di}"], writes=[f"SPb{s2}"])
                          P.op("dve", k.ts("dve", C[:], C[:], dc, None, ALU.mult), reads=[f"dec{di}"], writes=["C"])
                          P.op("act", k.act(Cd[:], C[:], AF.Copy), reads=["C"], writes=["Cd"])
                          P.op("pe", k.mm(pN[s2][:, 0:257], SPb[s2][:], vas4[b][:, ix, :], True, False), reads=[f"SPb{s2}", rv], writes=[f"pN{s2}"])
                          for kk in range(2):
                              P.op("pe", k.mm(pN[s2][:, 0:257], qT4[b][:, ix, kk, :], Cd[:, kk, :], False, kk == 1), reads=[rq, "Cd"], writes=[f"pN{s2}"])
                          P.op("act", k.act(kw[s2][:], ktk4[b][:, ix, :], AF.Copy, scale=wc), reads=[rkt, f"w{di}"], writes=[f"kw{s2}"])
                          for kk in range(2):
                              P.op("pe", k.mm(pK[kk][:, 0:257], kw[s2][:, kk * 128:(kk + 1) * 128], vas4[b][:, ix, :]), reads=[f"kw{s2}", rv], writes=[f"pK{kk}"])
                          for kk in range(2):
                              P.op("dve", k.tt("dve", C[:, kk, :], pK[kk][:, 0:257], C[:, kk, :], ALU.add), reads=[f"pK{kk}"], writes=["C"])
                          def epi(di=di, c=c, gi=gi, ix=ix, s2=s2, b=b, cb_=cb_, last=(c == cs[-1])):
                              rr, rr2, hs, junk2, ss2, sd2, rms, ho = rrL[s2], rr2L[s2], hsL[s2], junk2L[s2], ss2L[s2], sd2L[s2], rmsL[s2], hoL[s2]
                              T = lambda n: f"{n}_{s2}"
                              P.op("act", k.act(rr[:], pN[s2][:, 256:257], AF.Abs), reads=[f"pN{s2}"], writes=[T("rr0")])
                              P.op("dve", k.ts("dve", rr[:], rr[:], flv[di][:, c:c + 1], None, ALU.max), reads=[T("rr0"), f"fl{di}"], writes=[T("rr")])
                              P.op("dve", k.rcp(rr2[:], rr[:]), reads=[T("rr")], writes=[T("rr2")])
                              if di == 0:
                                  hb_ = gi % 2
                                  P.op("act", k.act(hst4[hb_][:, ix, :], pN[s2][:, 0:256], AF.Copy, scale=rr2[:, 0:1]), reads=[f"pN{s2}", T("rr2")], writes=[f"hst4_{hb_}"])
                                  if last:
                                      P.op("sp", k.dma("sp", D["hf"][cb_:cb_ + 4].rearrange("c p e -> p c e"), hst4[hb_][:]), reads=[f"hst4_{hb_}"], writes=["d_hf"], dsem=f"hst4_{hb_}")
                              else:
                                  P.op("dve", k.stt("dve", hs[:], pN[s2][:, 0:256], rr2[:, 0:1], hfc4[b][:, ix, :], ALU.mult, ALU.add),
                                       reads=[f"pN{s2}", T("rr2"), f"hfc4_{b}"], writes=[T("hs")])
                                  P.op("act", k.act(junk2[:], hs[:], AF.Square, accum=ss2[:, 0:1]), reads=[T("hs")], writes=[T("junk2"), T("ss2")])
                                  P.op("act", k.act(sd2[:], ss2[:], AF.Sqrt, bias=EPS, scale=1.0 / 256), reads=[T("ss2")], writes=[T("sd2")])
                                  P.op("dve", k.rcp(rms[:], sd2[:]), reads=[T("sd2")], writes=[T("rms")])
                                  P.op("dve", k.stt("dve", ho[:], hs[:], rms[:, 0:1], ogc4[b][:, ix, :], ALU.mult, ALU.mult), reads=[T("hs"), T("rms"), f"ogc4_{b}"], writes=[T("ho")])
                                  for kk in range(2):
                                      P.op("pe", k.tr(pT[:, kk * 128:(kk + 1) * 128], ho[:, kk * 128:(kk + 1) * 128], ident_b[:]), reads=[T("ho"), "ident_b"], writes=["pT"])
                                  P.op("act", k.act(mst[s2][:], pT[:, 0:256], AF.Copy), reads=["pT"], writes=[f"mst{s2}"])
                                  stage_put(MS, "MS", 2, 0, mcnt, lambda kk, s2=s2: mst[s2][:, kk * 128:(kk + 1) * 128], c, f"mst{s2}")
                          if pend[0] is not None:
                              pend[0]()
                          pend[0] = epi
                  if pend[0] is not None:
                      pend[0]()
                      pend[0] = None

                  _stop(P, "b2")
                  EB = sb("EB", [128, 3200], F32)
                  qh = sb("qh", [128, 64, 128], BF16)
                  kh = sb("kh", [128, 64, 128], BF16)
                  vh = sb("vh", [128, 64, 128], BF16)
                  Es = [sb(f"Es{i}", [128, 640], F32) for i in range(3)]
                  Pt = [sb(f"Pt{i}", [128, 640], BF16) for i in range(3)]
                  rinv = [sb(f"rinv{i}", [128, 128], F32) for i in range(3)]
                  ob = [sb(f"ob{i}", [128, 128], BF16) for i in range(3)]
                  pSn = [(pS[0], "pS0"), (pS[1], "pS1"), (pN[0], "pN0")]
                  pOn = [(pK[0], "pK0"), (pK[1], "pK1"), (pN[1], "pN1")]

                  def kbv(pr):
                      kb = min(max(pr - 2, 0), 59)
                      v = 0 if pr == 0 else 1 if pr == 1 else 3 if pr == 62 else 4 if pr == 63 else 2
                      return kb, v

                  si = 0
                  npend = [None]
                  for hb in range(2):
                      ncnt = {}
                      P.op("sp", k.dma("sp", EB[:, :], D["nab"][hb]), writes=["EB"], dsem="EB")
                      P.op("act", k.act(EB[:, :], EB[:, :], AF.Exp), writes=["EB"])
                      P.op("sp", k.dma("sp", qh[:], D["qBT"][hb].rearrange("c p t -> p c t")), reads=[f"d_fm{4 + 2 * hb}"], writes=["qh"], dsem="qh")
                      P.op("sp", k.dma("sp", kh[:], D["kBT"][hb].rearrange("c p t -> p c t")), reads=[f"d_fm{5 + 2 * hb}"], writes=["kh"], dsem="kh")
                      P.op("sp", k.dma("sp", vh[:], D["vB"][:, :, hb * 128:(hb + 1) * 128].rearrange("c p d -> p c d")), writes=["vh"], dsem="vh")
                      for pr in range(64):
                          s2 = si % 3
                          si += 1
                          kb, v = kbv(pr)
                          (pSb, rS), (pOb, rO) = pSn[s2], pOn[s2]
                          for kt in range(5):
                              dst = pSb[:, kt * 128:(kt + 1) * 128] if kt < 4 else pOb[:, 256:384]
                              P.op("pe", k.mm(dst, kh[:, kb + kt, :], qh[:, pr, :]), reads=["kh", "qh"], writes=[rS if kt < 4 else rO])
                          P.op("act", k.act(Es[s2][:, 0:512], pSb[:, 0:512], AF.Exp, scale=SCALE_B), reads=[rS], writes=[f"Es{s2}"])
                          P.op("act", k.act(Es[s2][:, 512:640], pOb[:, 256:384], AF.Exp, scale=SCALE_B), reads=[rO], writes=[f"Es{s2}"])
                          P.op("dve", k.tt("dve", Pt[s2][:], Es[s2][:], EB[:, v * 640:(v + 1) * 640], ALU.mult), reads=[f"Es{s2}", "EB"], writes=[f"Pt{s2}"])

                          def back(s2=s2, kb=kb, pr=pr, hb=hb, pOb=pOb, rO=rO, ncnt=ncnt):
                              for kt in range(5):
                                  P.op("pe", k.mm(pOb[:, 0:128], vh[:, kb + kt, :], Pt[s2][:, kt * 128:(kt + 1) * 128], kt == 0, kt == 4),
                                       reads=["vh", f"Pt{s2}"], writes=[rO])
                              for kt in range(5):
                                  P.op("pe", k.mm(pOb[:, 128:256], ones_b[:], Pt[s2][:, kt * 128:(kt + 1) * 128], kt == 0, kt == 4),
                                       reads=["ones_b", f"Pt{s2}"], writes=[rO])
                              P.op("dve", k.rcp(rinv[s2][:], pOb[:, 128:256]), reads=[rO], writes=[f"rinv{s2}"])
                              P.op("dve", k.tt("dve", ob[s2][:], pOb[:, 0:128], rinv[s2][:], ALU.mult), reads=[rO, f"rinv{s2}"], writes=[f"ob{s2}"])
                              stage_put(NS, f"NS", 1, 256 + hb * 128, ncnt, lambda kk, s2=s2: ob[s2][:, :], pr, f"ob{s2}")
                          if npend[0] is not None:
                              npend[0]()
                          npend[0] = back
                      if npend[0] is not None:
                          npend[0]()
                          npend[0] = None
                  P.emit()

        if mode == "fused" and not Prog.stopped:
            ccs = nc.alloc_semaphore("cc_sem")
            with nc.Block() as blk:
                def _cc(g):
                    g.collective_compute("ReduceScatter", ALU.add, replica_groups=[[0, 1, 2, 3], [4, 5, 6, 7]],
                                         ins=[D["mixs"].rearrange("j r f t -> (j r f) t").opt()], outs=[D["mixr"].opt()]).then_inc(ccs, 1)
                    g.wait_ge(ccs, 1)
                blk.gpsimd(_cc)
                blk.sync(lambda e: e.wait_ge(ccs, 1))
                blk.tensor(lambda e: e.wait_ge(ccs, 1))
                blk.vector(lambda e: e.wait_ge(ccs, 1))
                blk.scalar(lambda e: e.wait_ge(ccs, 1))

        if do2:
            with contextlib.ExitStack() as st:
                sb = lambda name, shape, dty: st.enter_context(nc.sbuf_tensor(name, shape, dty))
                psum = lambda name, dty=F32, n=512: st.enter_context(nc.psum_tensor(name, [128, n], dty))
                P = Prog(nc, "c")
                if not do1:
                    consts(P)
                tpb = [psum("tp0c"), psum("tp1c")]
                pw = [psum(f"pw{i}") for i in range(4)]
                pmisc = psum("pmisc2")
                Wo = sb("Wo", [128, 16, 2048], BF16)
                P.op("pool", k.dma("pool", Wo[:], D["wout"].rearrange("(k p) n -> p k n", p=128)), writes=["Wo"], dsem="Wo")
                _stop(P, "cX")
                if do1:
                    ga1_bc, Gc, shc = ga1_bcP, Gc2P, shc2P
                else:
                    modrow = sb("modrow2", [1, 2048], F32)
                    mrun = mk_modrows(P, sb, pmisc, "2", 2048)
                    g2c = sb("g2c", [128, 16], F32)
                    P.op("sp", k.dma("sp", g2c[:], D["g2col"]), writes=["g2c"], dsem="g2c")
                    ga1_bc = sb("ga1_bc", [128, 2048], F32)
                    Gc = sb("Gc2", [128, 16], F32)
                    shc = sb("shc2", [128, 16], F32)

                    def bcast(dst, res):
                        for nb in range(4):
                            P.op("pe", k.mm(pw[nb][:, 0:512], ones_f[0:1, :], modrow[0:1, nb * 512:(nb + 1) * 512]), reads=["modrow", "ones_f"], writes=[f"pw{nb}"])
                            P.op("dve", k.cp("dve", dst[:, nb * 512:(nb + 1) * 512], pw[nb][:, 0:512]), reads=[f"pw{nb}"], writes=[res])

                    mrun(D["wada2"], D["bada2"], 0, 2048, modrow)
                    bcast(ga1_bc, "ga1_bc")
                    _stop(P, "cY")
                    mrun(D["wada2"], D["bada2"], 2048, 2048, modrow)
                    row2col(P, pmisc, modrow, 0, 272, "pmisc")
                    P.op("dve", k.cp("dve", shc[:], pmisc[:, 272:288]), reads=["pmisc"], writes=["shc"])
                    _stop(P, "cZ1")
                    mrun(D["wada2"], D["bada2"], 4096, 2048, modrow)
                    _stop(P, "cZ1b")
                    row2col(P, pmisc, modrow, 0, 256, "pmisc")
                    _stop(P, "cZ1c")
                    P.op("dve", k.ts("dve", Gc[:], pmisc[:, 256:272], 1.0, None, ALU.add), reads=["pmisc"], writes=["Gc0"])
                    _stop(P, "cZ1d")
                    P.op("dve", k.tt("dve", Gc[:], Gc[:], g2c[:], ALU.mult), reads=["Gc0", "g2c", "shc"], writes=["Gc"])
                    _stop(P, "cZ2")
                    mrun(D["wada2"], D["bada2"], 6144, 2048, modrow)
                    bcast(ga2_bc, "ga2_bc")
                _stop(P, "c0")
                mixT = [sb(f"mixT{i}", [128, 16, 128], BF16) for i in range(2)]
                xts = [sb(f"x2t{i}", [128, 2048], F32) for i in range(2)]
                x1s = [sb(f"x1s{i}", [128, 2048], F32) for i in range(2)]
                xs = sb("xs2", [128, 2048], F32)
                junk = sb("junk3", [128, 2048], BF16)
                ss, sd, rstd = sb("ss3", [128, 1], F32), sb("sd3", [128, 1], F32), sb("rstd3", [128, 1], F32)
                h2st = [sb(f"h2st{i}", [128, 16, 128], BF16) for i in range(2)]
                mixr3 = D["mixr"].rearrange("(k p) t -> p k t", p=128)

                def tile_info(t):
                    if t < 16:
                        return 128, slice(1 + t * 128, 1 + (t + 1) * 128), slice(t * 128, (t + 1) * 128)
                    return 2, slice(0, 2050, 2049), slice(2048, 2050)

                def loads2(t):
                    M, cs, rs = tile_info(t)
                    b = t % 2
                    if t < 16:
                        P.op("sp", k.dma("sp", mixT[b][:, :, 0:M], mixr3[:, :, cs]), writes=[f"mixT{b}"], dsem=f"mixT{b}")
                    else:
                        P.op("sp", k.dma("sp", mixT[b][:, :, 0:1], mixr3[:, :, 0:1], slow=True), writes=[f"mixT{b}"], dsem=f"mixT{b}")
                        P.op("sp", k.dma("sp", mixT[b][:, :, 1:2], mixr3[:, :, 2049:2050], slow=True), writes=[f"mixT{b}x"], dsem=f"mixT{b}")
                    P.op("sp", k.dma("sp", xts[b][0:M, :], D["xtok"][rs, :]), writes=[f"x2t{b}"], dsem=f"x2t{b}")

                loads2(0)
                pend2 = [None]
                for t in range(17):
                    if t == 1:
                        _stop(P, "c1")
                    if t == 16:
                        _stop(P, "c16")
                    if t + 1 < 17:
                        loads2(t + 1)
                    M, cs, rs = tile_info(t)
                    b = t % 2
                    for nb in range(4):
                        for fc in range(16):
                            P.op("pe", k.mm(pw[nb][0:M, 0:512], mixT[b][:, fc, 0:M], Wo[:, fc, nb * 512:(nb + 1) * 512], fc == 0, fc == 15),
                                 reads=[f"mixT{b}", f"mixT{b}x", "Wo"], writes=[f"pw{nb}"])
                        P.op("dve", k.tt("dve", x1s[b][0:M, nb * 512:(nb + 1) * 512], pw[nb][0:M, 0:512], ga1_bc[0:M, nb * 512:(nb + 1) * 512], ALU.mult),
                             reads=[f"pw{nb}", "ga1_bc"], writes=[f"x1s{b}"])
                    P.op("dve", k.tt("dve", x1s[b][0:M, :], x1s[b][0:M, :], xts[b][0:M, :], ALU.add), reads=[f"x2t{b}"], writes=[f"x1s{b}"])
                    if t < 16:
                        P.op("sp", k.dma("sp", D["x1"][rs, :], x1s[b][0:M, :]), reads=[f"x1s{b}"], writes=["d_x1"], dsem=f"x1s{b}")
                    def tail2(t=t, M=M, cs=cs, b=b):
                        norm_T(P, x1s[b], xs, M, ss, sd, rstd, junk, tpb, Gc, shc,
                               lambda kk, b=b, M=M: h2st[b][:, kk, 0:M], f"x1s{b}", (lambda kk, b=b: f"h2st{b}_{kk}"), t)
                        if t < 16:
                            P.op("sp", k.dma("sp", D["h2T"][:, :, cs], h2st[b][:, :, 0:M]), reads=[f"h2st{b}_{kk_}" for kk_ in range(16)], writes=["d_h2T"], dsem=f"h2st{b}")
                        else:
                            P.op("sp", k.dma("sp", D["h2T"][:, :, 0:1], h2st[b][:, :, 0:1], slow=True), reads=[f"h2st{b}_{kk_}" for kk_ in range(16)], writes=["d_h2T"], dsem=f"h2st{b}")
                            P.op("sp", k.dma("sp", D["h2T"][:, :, 2049:2050], h2st[b][:, :, 1:2], slow=True), reads=[f"h2st{b}_{kk_}" for kk_ in range(16)], writes=["d_h2Tx"], dsem=f"h2st{b}")
                    if pend2[0] is not None:
                        pend2[0]()
                    pend2[0] = tail2
                if pend2[0] is not None:
                    pend2[0]()
                P.emit()

            _stop(P, "c")
            with contextlib.ExitStack() as st:
                sb = lambda name, shape, dty: st.enter_context(nc.sbuf_tensor(name, shape, dty))
                psum = lambda name, dty=F32, n=512: st.enter_context(nc.psum_tensor(name, [128, n], dty))
                P = Prog(nc, "d")
                pU = [psum("pU0"), psum("pU1")]
                pG = [psum("pG0"), psum("pG1")]
                pX = psum("pX")
                pE = [psum(f"pE{i}") for i in range(3)]
                AT = sb("AT", [128, 44, 1024], BF16)
                arena = sb("arena", [128, 22528], BF16)
                h2blk = arena[:, 0:16 * 1026].rearrange("p (k t) -> p k t", k=16)
                wdh = [arena[:, i * 11264:(i + 1) * 11264].rearrange("p (f c) -> p f c", f=22) for i in range(2)]
                wug = [sb(f"wug{i}", [128, 16, 256], BF16) for i in range(2)]
                gsbs = [sb(f"gsb{i}", [128, 1026], F32) for i in range(2)]
                accs = [sb(f"acc{i}", [128, 1024], F32) for i in range(2)]
                cw = sb("cw", [128, 44, 3], F32)
                cb = sb("cb", [128, 44], F32)
                flg = sb("flg", [128, 2], F32)
                x1p = [sb(f"x1p{i}", [128, 512], F32) for i in range(3)]
                zst = [sb(f"zst{i}", [128, 512], F32) for i in range(3)]
                P.op("sp", k.dma("sp", cw[:], D["convw"]), writes=["cw"], dsem="cw")
                P.op("sp", k.dma("sp", cb[:], D["convb"]), writes=["cw"], dsem="cb")
                P.op("sp", k.dma("sp", flg[:], D["flags"]), writes=["cw"], dsem="flg")
                wi = 0
                for tbk in range(2):
                    P.op("sp", k.dma("sp", h2blk, D["h2T"][:, :, tbk * 1024:tbk * 1024 + 1026]), reads=["d_h2T", "d_h2Tx"], writes=["arena", "wdh0", "wdh1"], dsem="h2blk")
                    for ft in range(44):
                        if ft == 1 and tbk == 0:
                            _stop(P, "d0")
                        b = ft % 2
                        P.op("pool", k.dma("pool", wug[b][:], D["wup"][ft]), writes=[f"wug{b}"], dsem=f"wug{b}")
                        for sbk in range(2):
                            cols = slice(1 + sbk * 512, 1 + (sbk + 1) * 512)
                            for kk in range(16):
                                P.op("pe", k.mm(pG[sbk][:, 0:512], wug[b][:, kk, 128:256], h2blk[:, kk, cols], kk == 0, kk == 15),
                                     reads=[f"wug{b}", "arena"], writes=[f"pG{sbk}"])
                        for kk in range(16):
                            P.op("pe", k.mm(pX[:, 0:2], wug[b][:, kk, 128:256], h2blk[:, kk, 0:1026:1025], kk == 0, kk == 15),
                                 reads=[f"wug{b}", "arena"], writes=["pX"])
                        for sbk in range(2):
                            cols = slice(1 + sbk * 512, 1 + (sbk + 1) * 512)
                            for kk in range(16):
                                P.op("pe", k.mm(pU[sbk][:, 0:512], wug[b][:, kk, 0:128], h2blk[:, kk, cols], kk == 0, kk == 15),
                                     reads=[f"wug{b}", "arena"], writes=[f"pU{sbk}"])
                        gs, ac = gsbs[b], accs[b]
                        P.op("act", k.act(gs[:, 1:513], pG[0][:, 0:512], AF.Copy), reads=["pG0"], writes=[f"gsb{b}"])
                        P.op("act", k.act(gs[:, 513:1025], pG[1][:, 0:512], AF.Copy), reads=["pG1"], writes=[f"gsb{b}"])
                        if tbk == 0:
                            P.op("act", k.act(gs[:, 0:1], pX[:, 0:1], AF.Copy, scale=flg[:, 0:1]), reads=["pX", "cw"], writes=[f"gsb{b}"])
                            P.op("act", k.act(gs[:, 1025:1026], pX[:, 1:2], AF.Copy), reads=["pX"], writes=[f"gsb{b}"])
                        else:
                            P.op("act", k.act(gs[:, 0:1], pX[:, 0:1], AF.Copy), reads=["pX"], writes=[f"gsb{b}"])
                            P.op("act", k.act(gs[:, 1025:1026], pX[:, 1:2], AF.Copy, scale=flg[:, 1:2]), reads=["pX", "cw"], writes=[f"gsb{b}"])
                        P.op("dve", k.ts("dve", ac[:], gs[:, 1:1025], cw[:, ft, 1:2], cb[:, ft:ft + 1], ALU.mult, ALU.add), reads=[f"gsb{b}", "cw"], writes=[f"acc{b}"])
                        P.op("dve", k.stt("dve", ac[:], gs[:, 0:1024], cw[:, ft, 0:1], ac[:], ALU.mult, ALU.add), reads=[f"gsb{b}", "cw"], writes=[f"acc{b}"])
                        P.op("dve", k.stt("dve", ac[:], gs[:, 2:1026], cw[:, ft, 2:3], ac[:], ALU.mult, ALU.add), reads=[f"gsb{b}", "cw"], writes=[f"acc{b}"])
                        P.op("act", k.act(ac[:], ac[:], AF.Gelu), writes=[f"acc{b}"])
                        P.op("dve", k.tt("dve", AT[:, ft, 0:512], pU[0][:, 0:512], ac[:, 0:512], ALU.mult), reads=[f"acc{b}", "pU0"], writes=["AT"])
                        P.op("dve", k.tt("dve", AT[:, ft, 512:1024], pU[1][:, 0:512], ac[:, 512:1024], ALU.mult), reads=[f"acc{b}", "pU1"], writes=["AT"])
                    if tbk == 0:
                        _stop(P, "d1")
                    accs_ps = [(pU[0], "pU0"), (pU[1], "pU1"), (pG[0], "pG0"), (pG[1], "pG1"), (pX, "pX"), (pE[0], "pE0"), (pE[1], "pE1"), (pE[2], "pE2")]
                    for nb in range(4):
                        for hf in range(2):
                            wb = wi % 2
                            wi += 1
                            P.op("pool", k.dma("pool", wdh[wb], D["wdn"][nb, hf]), writes=["arena", f"wdh{wb}"], dsem=f"wdh{wb}")
                            for tt in range(8):
                                for f in range(22):
                                    ft = hf * 22 + f
                                    P.op("pe", k.mm(accs_ps[tt][0][:, 0:512], AT[:, ft, tt * 128:(tt + 1) * 128], wdh[wb][:, f, :], ft == 0, ft == 43),
                                         reads=["AT", f"wdh{wb}"], writes=[accs_ps[tt][1]])
                        for tt in range(8):
                            row0 = tbk * 1024 + tt * 128
                            xb_ = (nb * 8 + tt) % 3
                            zb = (nb * 8 + tt) % 3
                            cs_ = slice(nb * 512, (nb + 1) * 512)
                            P.op("sp", k.dma("sp", x1p[xb_][:], D["x1"][row0:row0 + 128, cs_]), writes=[f"x1p{xb_}"], dsem=f"x1p{xb_}")
                            P.op("dve", k.tt("dve", zst[zb][:], accs_ps[tt][0][:, 0:512], ga2_bc[:, cs_], ALU.mult), reads=[accs_ps[tt][1]], writes=[f"zst{zb}"])
                            P.op("dve", k.tt("dve", zst[zb][:], zst[zb][:], x1p[xb_][:], ALU.add), reads=[f"x1p{xb_}"], writes=[f"zst{zb}"])
                            P.op("sp", k.dma("sp", D["z"][row0:row0 + 128, cs_], zst[zb][:]), reads=[f"zst{zb}"], writes=["d_z"], dsem=f"zst{zb}")
                P.emit()

            _stop(P, "d")
            with contextlib.ExitStack() as st:
                sb = lambda name, shape, dty: st.enter_context(nc.sbuf_tensor(name, shape, dty))
                P = Prog(nc, "e")
                gf = sb("gf", [128, 2048], F32)
                P.op("sp", k.dma("sp", gf[:], D["gfin"]), writes=["gf"], dsem="gf")
                zt = [sb(f"zt{i}", [128, 2048], F32) for i in range(2)]
                ot = [sb(f"ot{i}", [128, 2048], F32) for i in range(2)]
                junk = sb("junk4", [128, 2048], BF16)
                ss, sd, rstd = sb("ss4", [128, 1], F32), sb("sd4", [128, 1], F32), sb("rstd4", [128, 1], F32)
                P.op("sp", k.dma("sp", zt[0][:], D["z"][0:128, :]), writes=["zt0"], dsem="zt0")
                for t in range(16):
                    b = t % 2
                    if t + 1 < 16:
                        P.op("sp", k.dma("sp", zt[1 - b][:], D["z"][(t + 1) * 128:(t + 2) * 128, :]), writes=[f"zt{1 - b}"], dsem=f"zt{1 - b}")
                    P.op("act", k.act(junk[:], zt[b][:], AF.Square, accum=ss[:, 0:1]), reads=[f"zt{b}"], writes=["junk", "ss"])
                    P.op("act", k.act(sd[:], ss[:], AF.Sqrt, bias=EPS, scale=1.0 / 2048), reads=["ss"], writes=["sd"])
                    P.op("dve", k.rcp(rstd[:], sd[:]), reads=["sd"], writes=["rstd"])
                    P.op("dve", k.stt("dve", ot[b][:], zt[b][:], rstd[:, 0:1], gf[:], ALU.mult, ALU.mult), reads=[f"zt{b}", "rstd", "gf"], writes=[f"ot{b}"])
                    P.op("sp", k.dma("sp", D["out"][t * 128:(t + 1) * 128, :], ot[b][:]), reads=[f"ot{b}"], dsem=f"ot{b}")
                P.emit()
    return nc


def _col(v):
    return np.ascontiguousarray(v.reshape(16, 128).T)


def _nab_tables(rpb_l, heads):
    NEG = np.float32(-30000.0)
    out = np.full((len(heads), 128, 5, 5, 128), NEG, np.float32)
    p = np.arange(128)
    q = np.arange(128)
    for v, pr in enumerate((0, 1, 10, 62, 63)):
        kb = min(max(pr - 2, 0), 59)
        r = 2 * pr + q // 64
        c = q % 64
        rs = np.clip(r - 4, 0, 120)
        cs = np.clip(c - 8, 0, 48)
        for kt in range(5):
            krow = 2 * (kb + kt) + p // 64
            kc = p % 64
            ok = ((krow[:, None] >= rs[None, :]) & (krow[:, None] < rs[None, :] + 8)
                  & (kc[:, None] >= cs[None, :]) & (kc[:, None] < cs[None, :] + 16))
            dr = np.clip(krow[:, None] - r[None, :] + 7, 0, 14)
            dc = np.clip(kc[:, None] - c[None, :] + 15, 0, 30)
            for hi, h in enumerate(heads):
                vals = rpb_l[h][dr, dc]
                out[hi, :, v, kt, :] = np.where(ok, vals, NEG)
    return out.reshape(len(heads), 128, 3200)


def _inputs_h1(inp, j):
    b, hq = divmod(j, 4)
    w_in = inp["w_in"][0]
    A = 1024
    qa = lambda h: slice(h * 256, (h + 1) * 256)
    cols = []
    cols += list(range(0 * A + hq * 256, 0 * A + (hq + 1) * 256))
    cols += list(range(1 * A + hq * 256, 1 * A + (hq + 1) * 256))
    gb = 4 * A + 16
    for hl in range(2):
        h = 2 * hq + hl
        cols += list(range(gb + h * 128, gb + (h + 1) * 128))
        cols += list(range(gb + 1024 + h * 128, gb + 1024 + (h + 1) * 128))
    cols += list(range(2 * A + hq * 256, 2 * A + (hq + 1) * 256))
    cols += list(range(3 * A + hq * 256, 3 * A + (hq + 1) * 256))
    cols += list(range(1 * A + hq * 256, 1 * A + (hq + 1) * 256))
    for hl in range(2):
        h = 2 * hq + hl
        cols += list(range(gb + 2048 + h * 128, gb + 2048 + (h + 1) * 128))
    gcols = [4 * A + g * 4 + hq for g in range(4)]
    cols += gcols
    tri = np.triu(np.ones((128, 128), np.float32))
    return {
        "ccol": _col(inp["c"][b]),
        "ident": np.eye(128, dtype=np.float32),
        "xb": np.ascontiguousarray(inp["x"][b]),
        "wada1": np.ascontiguousarray(inp["w_ada"][0][:, 0:4096]),
        "bada1": np.ascontiguousarray(inp["b_ada"][0][None, 0:4096]),
        "g1col": _col(inp["g_norm1"][0]),
        "win": np.ascontiguousarray(w_in[:, cols]),
        "bg": np.ascontiguousarray(np.broadcast_to(inp["b_gates"][0][[g * 4 + hq for g in range(4)]][None, :], (128, 4))),
        "gha": np.ascontiguousarray(np.broadcast_to(inp["g_head_a"][0][hq * 256:(hq + 1) * 256][None, :], (128, 256))),
        "nab": _nab_tables(inp["rpb"][0], [2 * hq, 2 * hq + 1]),
        "trif": tri,
        "trib": np.ascontiguousarray(tri.T),
        "rmask": np.ascontiguousarray(np.broadcast_to(np.eye(4, dtype=np.float32)[hq][None, :], (128, 4))),
    }


def _inputs_h2(inp, j, mixr=None):
    b, q = divmod(j, 4)
    t0 = q * 2048
    x = inp["x"][b]
    xtok = np.zeros((2050, 2048), np.float32)
    xtok[0:2048] = x[t0:t0 + 2048]
    if t0 > 0:
        xtok[2048] = x[t0 - 1]
    if t0 + 2048 < NTOK:
        xtok[2049] = x[t0 + 2048]
    flags = np.zeros((128, 2), np.float32)
    flags[:, 0] = 1.0 if t0 > 0 else 0.0
    flags[:, 1] = 1.0 if t0 + 2048 < NTOK else 0.0
    rows = []
    for r in range(4):
        rows += list(range(r * 256, (r + 1) * 256))
        rows += list(range(1024 + 2 * r * 128, 1024 + (2 * r + 2) * 128))
    w_up = inp["w_up"][0]
    wup = np.empty((44, 128, 16, 256), np.float32)
    wu = w_up[:, 0:5632].reshape(16, 128, 44, 128)
    wg = w_up[:, 5632:].reshape(16, 128, 44, 128)
    wup[:, :, :, 0:128] = wu.transpose(2, 1, 0, 3)
    wup[:, :, :, 128:256] = wg.transpose(2, 1, 0, 3)
    wd = inp["w_down"][0].reshape(2, 22, 128, 4, 512)
    wdn = np.ascontiguousarray(wd.transpose(3, 0, 2, 1, 4))
    d = {
        "ccol": _col(inp["c"][b]),
        "ident": np.eye(128, dtype=np.float32),
        "xtok": xtok,
        "flags": flags,
        "wada2": np.ascontiguousarray(inp["w_ada"][0][:, 4096:]),
        "bada2": np.ascontiguousarray(inp["b_ada"][0][None, 4096:]),
        "wout": np.ascontiguousarray(inp["w_out"][0][rows, :]),
        "g2col": _col(inp["g_norm2"][0]),
        "wup": wup,
        "convw": np.ascontiguousarray(inp["conv_w"][0].reshape(3, 44, 128).transpose(2, 1, 0)),
        "convb": np.ascontiguousarray(inp["conv_b"][0].reshape(44, 128).T),
        "wdn": wdn,
        "gfin": np.ascontiguousarray(np.broadcast_to(inp["g_final"][None, :], (128, 2048))),
    }
    if mixr is not None:
        d["mixr"] = mixr
    return d


MODE = "fused"
_NC_CACHE = {}


def _get_nc(mode):
    if mode not in _NC_CACHE:
        _NC_CACHE[mode] = build(mode)
    return _NC_CACHE[mode]


def run_h1(inp, cores=range(8)):
    nc = _get_nc("h1")
    cores = list(cores)
    maps = [_inputs_h1(inp, j) for j in cores]
    res = run_bass_kernel_spmd(nc, maps, core_ids=list(range(len(cores))))
    return [np.asarray(r["mixs"]) for r in res.results]


def exchange(mixs):
    out = []
    for j in range(8):
        b, q = divmod(j, 4)
        out.append(np.ascontiguousarray(np.concatenate([mixs[b * 4 + r][q, r] for r in range(4)], axis=0)))
    return out


def run_h2(inp, mixr):
    nc = _get_nc("h2")
    maps = [_inputs_h2(inp, j, mixr[j]) for j in range(8)]
    res = run_bass_kernel_spmd(nc, maps, core_ids=list(range(8)))
    return [np.asarray(r["out"]) for r in res.results]


def kernel(**inputs):
    inp = {k_: np.asarray(v) for k_, v in inputs.items()}
    if MODE == "fused":
        nc = _get_nc("fused")
        maps = []
        for j in range(8):
            d = _inputs_h1(inp, j)
            d.update(_inputs_h2(inp, j))
            maps.append(d)
        res = run_bass_kernel_spmd(nc, maps, core_ids=list(range(8)))
        outs = [np.asarray(r["out"]) for r in res.results]
    else:
        outs = run_h2(inp, exchange(run_h1(inp)))
    out = np.stack(outs, 0).reshape(2, NTOK, 2048)
    return out.astype(np.float32)
```
